# Optimizing a Trainium2 kernel written in Bass

```python
import jax, jax.numpy as jnp
from jax import lax
import numpy as np

D_MODEL = 1024
BATCH = 8
SEQ = 2048
DEPTH = 4
DEC_BATCH = 128
DEC_SEQ = 1
PAST_LEN = 16384
PAGE_SIZE = 128

H_A = 4
DK = D_MODEL // 8
DV = D_MODEL // 8
QK_DIM = H_A * DK
V_DIM = H_A * DV
QKV_DIM = 2 * QK_DIM + V_DIM
CONV_A = 4
DELTA_CHUNK = 64
H_B = 4
D_B = D_MODEL // 8
SGU_DIM = H_B * D_B
SGU_CHUNK = 128
OFF_Z = QKV_DIM
OFF_BETA = OFF_Z + V_DIM
OFF_ALPHA = OFF_BETA + H_A
OFF_U = OFF_ALPHA + H_A
OFF_VB = OFF_U + SGU_DIM
IN_DIM = OFF_VB + SGU_DIM
MIX_DIM = V_DIM + SGU_DIM
D_FF = ((8 * D_MODEL // 3 + 127) // 128) * 128
CONV_F = 3
N_MOD = 6
EPS = 1e-6

kernel_name = "hymba_gdn_chunkmlp_convffn_adaln_step"


def rms_norm(x, w):
    xf = x.astype(jnp.float32)
    y = xf * lax.rsqrt(jnp.mean(xf * xf, axis=-1, keepdims=True) + EPS)
    return y * w.astype(jnp.float32)


def l2_normalize(x):
    xf = x.astype(jnp.float32)
    return xf * lax.rsqrt(jnp.sum(xf * xf, axis=-1, keepdims=True) + EPS)


def causal_dwconv(x, prev, w):
    width = w.shape[0]
    T = x.shape[1]
    xp = jnp.concatenate([prev.astype(x.dtype), x], axis=1)
    y = xp[:, 0:T] * w[0]
    for j in range(1, width):
        y = y + xp[:, j:j + T] * w[j]
    return y, xp[:, T:]


def gated_delta_rule(q, k, v, g, beta, s0):
    nb, T = q.shape[0], q.shape[1]
    C = DELTA_CHUNK
    pad = (-T) % C
    n = (T + pad) // C
    q = l2_normalize(q) * (DK ** -0.5)
    k = l2_normalize(k)
    v = v.astype(jnp.float32)
    g = g.astype(jnp.float32)
    beta = beta.astype(jnp.float32)

    def to_chunks(a):
        a = jnp.pad(a, [(0, 0), (0, pad)] + [(0, 0)] * (a.ndim - 2))
        a = a.reshape((nb, n, C) + a.shape[2:])
        a = jnp.moveaxis(a, 3, 2)
        return jnp.moveaxis(a, 1, 0)

    qc, kc, vc, gc, bc = (to_chunks(a) for a in (q, k, v, g, beta))
    gcum = jnp.cumsum(gc, axis=-1)
    tril = jnp.tril(jnp.ones((C, C), dtype=bool))
    strict = jnp.tril(jnp.ones((C, C), dtype=bool), -1)
    decay = jnp.exp(jnp.where(tril, gcum[..., :, None] - gcum[..., None, :], -jnp.inf))
    kb = kc * bc[..., None]
    vb = vc * bc[..., None]
    lower = jnp.where(strict, jnp.einsum('nbhid,nbhjd->nbhij', kb, kc) * decay, 0.0)
    a_mat = lower + jnp.eye(C, dtype=jnp.float32)
    rhs = jnp.concatenate([vb, kb * jnp.exp(gcum)[..., None]], axis=-1)
    sol = lax.linalg.triangular_solve(a_mat, rhs, left_side=True, lower=True, unit_diagonal=True)
    u_c = sol[..., :DV]
    w_c = sol[..., DV:]
    qk = jnp.where(tril, jnp.einsum('nbhid,nbhjd->nbhij', qc, kc) * decay, 0.0)

    def step(S, xs):
        q_i, k_i, u_i, w_i, qk_i, g_i = xs
        v_new = u_i - jnp.einsum('bhck,bhkv->bhcv', w_i, S)
        o = (jnp.einsum('bhck,bhkv->bhcv', q_i * jnp.exp(g_i)[..., None], S)
             + jnp.einsum('bhij,bhjv->bhiv', qk_i, v_new))
        g_last = g_i[..., -1]
        k_dec = k_i * jnp.exp(g_last[..., None] - g_i)[..., None]
        S = S * jnp.exp(g_last)[..., None, None] + jnp.einsum('bhck,bhcv->bhkv', k_dec, v_new)
        return S, o

    S, o = lax.scan(step, s0.astype(jnp.float32), (qc, kc, u_c, w_c, qk, gcum))
    o = jnp.transpose(o, (1, 0, 3, 2, 4)).reshape(nb, n * C, H_A, DV)[:, :T]
    return o, S


def chunk_spatial_gating(u, v, w_s, b_s):
    nb, T = v.shape[0], v.shape[1]
    C = SGU_CHUNK
    pad = (-T) % C
    n = (T + pad) // C
    vp = jnp.pad(v, ((0, 0), (0, pad), (0, 0), (0, 0))).reshape(nb, n, C, H_B, D_B)
    w = jnp.where(jnp.tril(jnp.ones((C, C), dtype=bool)), w_s, 0.0)
    z = jnp.einsum('hij,bnjhd->bnihd', w, vp) + b_s.T[None, None, :, :, None]
    z = z.reshape(nb, n * C, H_B, D_B)[:, :T]
    return u * z


def run_trunk(x, c, s_delta, s_qkv, s_ffn, w_mod, b_mod, norm1, w_in, conv_qkv, a_log,
              dt_bias, gdn_norm, sgu_norm, w_sgu, b_sgu, w_out, norm2, w_up, conv_ffn_w,
              conv_ffn_b, w_down, final_norm):
    nb, T = x.shape[0], x.shape[1]
    new_S, new_qkv, new_ffn, v_rows = [], [], [], []
    c_act = jax.nn.silu(c)
    for l in range(DEPTH):
        mod = (c_act @ w_mod[l] + b_mod[l])[:, None, :]
        sh1, sc1, g1, sh2, sc2, g2 = jnp.split(mod, N_MOD, axis=-1)
        h = (rms_norm(x, norm1[l]) * (1.0 + sc1) + sh1).astype(x.dtype)
        proj = h @ w_in[l]
        qkv = proj[..., :OFF_Z]
        z = proj[..., OFF_Z:OFF_BETA].reshape(nb, T, H_A, DV)
        b_raw = proj[..., OFF_BETA:OFF_ALPHA]
        a_raw = proj[..., OFF_ALPHA:OFF_U]
        u_raw = proj[..., OFF_U:OFF_VB]
        vb_raw = proj[..., OFF_VB:]
        qkv_c, qkv_buf = causal_dwconv(qkv, s_qkv[l], conv_qkv[l])
        qkv_c = jax.nn.silu(qkv_c)
        q = qkv_c[..., :QK_DIM].reshape(nb, T, H_A, DK)
        k = qkv_c[..., QK_DIM:2 * QK_DIM].reshape(nb, T, H_A, DK)
        v = qkv_c[..., 2 * QK_DIM:].reshape(nb, T, H_A, DV)
        beta = jax.nn.sigmoid(b_raw.astype(jnp.float32))
        g = -jnp.exp(a_log[l].astype(jnp.float32)) * jax.nn.softplus(
            a_raw.astype(jnp.float32) + dt_bias[l].astype(jnp.float32))
        o_a, S = gated_delta_rule(q, k, v, g, beta, s_delta[l])
        o_a = rms_norm(o_a, gdn_norm[l]) * jax.nn.silu(z.astype(jnp.float32))
        u = jax.nn.gelu(u_raw.astype(jnp.float32)).reshape(nb, T, H_B, D_B)
        vv = rms_norm(jax.nn.gelu(vb_raw.astype(jnp.float32)).reshape(nb, T, H_B, D_B), sgu_norm[l])
        o_b = chunk_spatial_gating(u, vv, w_sgu[l].astype(jnp.float32), b_sgu[l].astype(jnp.float32))
        mix = jnp.concatenate([o_a.reshape(nb, T, V_DIM), o_b.reshape(nb, T, SGU_DIM)],
                              axis=-1).astype(x.dtype)
        x = x + g1 * (mix @ w_out[l])
        h2 = (rms_norm(x, norm2[l]) * (1.0 + sc2) + sh2).astype(x.dtype)
        up = h2 @ w_up[l]
        up_c, ffn_buf = causal_dwconv(up, s_ffn[l], conv_ffn_w[l])
        up_c = up_c + conv_ffn_b[l]
        x = x + g2 * ((jax.nn.silu(up_c[..., :D_FF]) * up_c[..., D_FF:]) @ w_down[l])
        new_S.append(S.astype(s_delta.dtype))
        new_qkv.append(qkv_buf)
        new_ffn.append(ffn_buf)
        v_rows.append(vv.astype(x.dtype))
    y = rms_norm(x, final_norm).astype(x.dtype)
    return y, jnp.stack(new_S), jnp.stack(new_qkv), jnp.stack(new_ffn), v_rows


def setup_inputs(seed: int = 0) -> dict:
    key = jax.random.key(seed)
    ks = jax.random.split(key, 32)
    nrm = lambda k, s, sc: jax.random.normal(k, s, jnp.float32) * sc
    dt = jnp.exp(jax.random.uniform(ks[12], (DEPTH, H_A), jnp.float32,
                                    np.log(1e-3).astype(np.float32), np.log(1e-1).astype(np.float32)))
    return {
        "x_prompt": nrm(ks[0], (BATCH, SEQ, D_MODEL), 1.0),
        "x_sample": nrm(ks[1], (DEC_BATCH, DEC_SEQ, D_MODEL), 1.0),
        "state_delta": nrm(ks[2], (DEPTH, DEC_BATCH, H_A, DK, DV), 0.5),
        "state_qkv_conv": nrm(ks[3], (DEPTH, DEC_BATCH, CONV_A - 1, QKV_DIM), 1.0),
        "state_ffn_conv": nrm(ks[4], (DEPTH, DEC_BATCH, CONV_F - 1, 2 * D_FF), 1.0),
        "c_prompt": nrm(ks[5], (BATCH, D_MODEL), 1.0),
        "c_sample": nrm(ks[6], (DEC_BATCH, D_MODEL), 1.0),
        "w_mod": nrm(ks[7], (DEPTH, D_MODEL, N_MOD * D_MODEL), 0.02),
        "b_mod": nrm(ks[8], (DEPTH, N_MOD * D_MODEL), 0.02),
        "norm1": 1.0 + nrm(ks[9], (DEPTH, D_MODEL), 0.02),
        "w_in": nrm(ks[10], (DEPTH, D_MODEL, IN_DIM), D_MODEL ** -0.5),
        "conv_qkv": nrm(ks[11], (DEPTH, CONV_A, QKV_DIM), CONV_A ** -0.5),
        "a_log": jnp.log(jax.random.uniform(ks[13], (DEPTH, H_A), jnp.float32, 1.0, 16.0)),
        "dt_bias": dt + jnp.log(-jnp.expm1(-dt)),
        "gdn_norm": 1.0 + nrm(ks[14], (DEPTH, DV), 0.02),
        "sgu_norm": 1.0 + nrm(ks[15], (DEPTH, H_B, D_B), 0.02),
        "w_sgu": nrm(ks[16], (DEPTH, H_B, SGU_CHUNK, SGU_CHUNK), SGU_CHUNK ** -0.5),
        "b_sgu": 1.0 + nrm(ks[17], (DEPTH, H_B, SGU_CHUNK), 0.02),
        "w_out": nrm(ks[18], (DEPTH, MIX_DIM, D_MODEL), MIX_DIM ** -0.5),
        "norm2": 1.0 + nrm(ks[19], (DEPTH, D_MODEL), 0.02),
        "w_up": nrm(ks[20], (DEPTH, D_MODEL, 2 * D_FF), D_MODEL ** -0.5),
        "conv_ffn_w": nrm(ks[21], (DEPTH, CONV_F, 2 * D_FF), CONV_F ** -0.5),
        "conv_ffn_b": nrm(ks[22], (DEPTH, 2 * D_FF), 0.02),
        "w_down": nrm(ks[23], (DEPTH, D_FF, D_MODEL), D_FF ** -0.5),
        "final_norm": 1.0 + nrm(ks[24], (D_MODEL,), 0.02),
    }


def reference(x_prompt, x_sample, state_delta, state_qkv_conv, state_ffn_conv, c_prompt, c_sample,
              w_mod, b_mod, norm1, w_in, conv_qkv, a_log, dt_bias, gdn_norm, sgu_norm, w_sgu, b_sgu,
              w_out, norm2, w_up, conv_ffn_w, conv_ffn_b, w_down, final_norm):
    weights = (w_mod, b_mod, norm1, w_in, conv_qkv, a_log, dt_bias, gdn_norm, sgu_norm, w_sgu,
               b_sgu, w_out, norm2, w_up, conv_ffn_w, conv_ffn_b, w_down, final_norm)
    nbp = x_prompt.shape[0]
    s0_delta = jnp.zeros((DEPTH, nbp, H_A, DK, DV), state_delta.dtype)
    s0_qkv = jnp.zeros((DEPTH, nbp, CONV_A - 1, QKV_DIM), x_prompt.dtype)
    s0_ffn = jnp.zeros((DEPTH, nbp, CONV_F - 1, 2 * D_FF), x_prompt.dtype)
    y_prompt, delta_p, qkv_p, ffn_p, _ = run_trunk(x_prompt, c_prompt, s0_delta, s0_qkv, s0_ffn, *weights)
    y_sample, delta_s, qkv_s, ffn_s, v_rows_s = run_trunk(
        x_sample, c_sample, state_delta, state_qkv_conv, state_ffn_conv, *weights)
    sgu_v_s = jnp.stack(v_rows_s)
    return (y_prompt, y_sample, delta_p, delta_s, qkv_p, qkv_s, ffn_p, ffn_s, sgu_v_s)
```

```python
import math
import sys
import numpy as np
from contextlib import ExitStack
import concourse.bass as bass
import concourse.mybir as mybir
from concourse.bass_utils import run_bass_kernel_spmd

F32 = mybir.dt.float32
BF16 = mybir.dt.bfloat16
AF = mybir.ActivationFunctionType
ALU = mybir.AluOpType
AX = mybir.AxisListType

D = 1024
T = 2048
NT = 16
L = 4
NS = 16
H = 4
QKV = 1536
IN = 3080
DFF = 2816
FF2 = 5632
NG = 11
EPS = 1e-6
NCORES = 8

C_ID, C_ONE, C_TI, C_TS, C_ML, C_MQ, C_TR = 0, 128, 256, 384, 512, 640, 768
C_CM = 896
C_E0 = 898
C_I16 = 914
NCONST = 914 + 256

ENG = ("pe", "act", "dve", "pool", "sp")
SELF_GAP = 1 << 30


def make_consts():
    c = np.zeros((128, NCONST), np.float32)
    i = np.arange(128)
    same = (i[:, None] // 64) == (i[None, :] // 64)
    c[:, C_ID:C_ID + 128] = np.eye(128)
    c[:, C_ONE:C_ONE + 128] = 1.0
    c[:, C_TI:C_TI + 128] = (same & (i[:, None] <= i[None, :]))
    c[:, C_TS:C_TS + 128] = (same & (i[:, None] > i[None, :]))
    c[:, C_ML:C_ML + 128] = (same & (i[:, None] > i[None, :]))
    c[:, C_MQ:C_MQ + 128] = (same & (i[:, None] >= i[None, :]))
    c[:, C_TR:C_TR + 128] = (i[:, None] >= i[None, :])
    c[:, C_CM + 0] = (i < 64)
    c[:, C_CM + 1] = (i >= 64)
    c[0, C_E0:C_E0 + 16] = 1.0
    c[:, C_I16:C_I16 + 256] = np.eye(16, dtype=np.float32).reshape(1, 256)
    return c


def _is_ap(v):
    return hasattr(v, "tensor") and hasattr(v, "ap") and hasattr(v, "offset")


def _box(ap):
    t = ap.tensor
    s = 2 if ap.dtype == BF16 else 4
    a = ap.ap
    off = int(ap.offset)
    if type(t).__name__.startswith("DRam"):
        ext = sum(st * (c - 1) for st, c in a)
        return (t.name, 0, 1, off * s, (off + ext + 1) * s)
    if type(t).__name__.startswith("PSum"):
        return (t.name, 0, 128, 0, 2048)
    pstep, npart = a[0]
    if pstep == 0:
        p0, fo = 0, off
    else:
        p0 = off // pstep
        fo = off - p0 * pstep
    ext = sum(st * (c - 1) for st, c in a[1:])
    return (t.name, p0, p0 + npart, fo * s, (fo + ext + 1) * s)


def _where():
    f = sys._getframe(2)
    out = []
    while f is not None and len(out) < 4:
        out.append(f.f_lineno)
        f = f.f_back
    return out


class Prog:
    def __init__(self):
        self.ops = {e: [] for e in ENG}
        self.cnt = {e: 0 for e in ENG}
        self.known = {e: {} for e in ENG}
        self.recs = {}
        self.dpool = {"sp": ["dsp%d" % i for i in range(24)], "pool": ["dpl%d" % i for i in range(12)]}
        self.dnext = {"sp": 0, "pool": 0}
        self.dval = {}
        self.rec = None

    def replay(self, item):
        kind, eng, meth, kw = item
        if kind == 0:
            self.op(eng, meth, **kw)
        else:
            self.dma(eng, **kw)

    def _collect(self, kw):
        accs = []
        need = {}
        for k, v in kw.items():
            if _is_ap(v):
                accs.append((_box(v), k in ("out", "accum_out", "ap")))
        for bx, isw in accs:
            name, p0, p1, lo, hi = bx
            for r in self.recs.get(name, ()):
                if r[0] < p1 and p0 < r[1] and r[2] < hi and lo < r[3]:
                    if isw or r[4] == "w":
                        if need.get(r[5], 0) < r[6]:
                            need[r[5]] = r[6]
        return accs, need

    def _commit(self, accs, key, val):
        for bx, isw in accs:
            name, p0, p1, lo, hi = bx
            lst = self.recs.setdefault(name, [])
            if isw:
                lst[:] = [r for r in lst if not (p0 <= r[0] and r[1] <= p1 and lo <= r[2] and r[3] <= hi)]
                lst.append([p0, p1, lo, hi, "w", key, val])
            else:
                for r in lst:
                    if r[4] == "r" and r[5] == key and r[0] == p0 and r[1] == p1 and r[2] == lo and r[3] == hi:
                        r[6] = val
                        break
                else:
                    lst.append([p0, p1, lo, hi, "r", key, val])

    def _filter(self, eng, need):
        wl = []
        kn = self.known[eng]
        for k, v in need.items():
            if k == eng:
                if eng == "pe":
                    continue
                if v <= self.cnt[eng] - SELF_GAP and False:
                    continue
            if kn.get(k, 0) >= v:
                continue
            kn[k] = v
            wl.append((k, v))
        return wl

    def op(self, eng, meth, **kw):
        if self.rec is not None:
            self.rec.append((0, eng, meth, kw))
            return
        accs, need = self._collect(kw)
        wl = self._filter(eng, need)
        self.cnt[eng] += 1
        self.ops[eng].append((wl, meth, kw, None, _where()))
        self._commit(accs, eng, self.cnt[eng])

    def dma(self, eng, **kw):
        if self.rec is not None:
            self.rec.append((1, eng, None, kw))
            return
        accs, need = self._collect(kw)
        pool = self.dpool[eng]
        i = self.dnext[eng]
        self.dnext[eng] = (i + 1) % len(pool)
        sem = pool[i]
        prev = self.dval.get(sem, 0)
        if prev:
            need[sem] = max(need.get(sem, 0), prev)
        wl = self._filter(eng, need)
        v = prev + 16
        self.dval[sem] = v
        self.ops[eng].append((wl, "dma_start", kw, sem, _where()))
        self._commit(accs, sem, v)

    def sem_names(self):
        return [e for e in ENG if e != "sp"] + self.dpool["sp"] + self.dpool["pool"]


def v2(ap, a):
    return ap.rearrange("p (a b) -> p a b", a=a)


def v3(ap, a, b):
    return ap.rearrange("p (a b c) -> p a b c", a=a, b=b)


class Carve:
    def __init__(self, ph, n):
        self.ph = ph
        self.n = n
        self.off = 0

    def reset(self):
        self.off = 0

    def f32(self, n):
        assert self.off + n <= self.n, ("arena overflow", self.off, n, self.n)
        ap = self.ph[:, self.off:self.off + n]
        self.off += n
        return ap

    def bf16(self, n):
        nf = (n + 1) // 2
        assert self.off + nf <= self.n, ("arena overflow", self.off, nf, self.n)
        ap = self.ph[:, self.off:self.off + nf].bitcast(BF16)[:, 0:n]
        self.off += nf
        return ap


class StopBuild(Exception):
    pass


def build_program(stop_at=None, dump_names=()):
    nc = bass.Bass("TRN2", target_bir_lowering=False)
    P = Prog()

    P.marks = []

    def ckpt(name):
        P.marks.append((name, dict(P.cnt)))
        if stop_at is not None and name == stop_at:
            raise StopBuild(name)

    P.dump_info = []
    dcol = [0]

    def dump(name, ap):
        if name not in dump_names:
            return
        if len(ap.shape) > 2:
            ap = ap.rearrange("p a b -> p (a b)") if len(ap.shape) == 3 else ap.rearrange("p a b c -> p (a b c)")
        w = ap.shape[1]
        P.dma("pool" if ap.dtype == BF16 else "sp", out=dbg[0:ap.shape[0], dcol[0]:dcol[0] + w], in_=ap)
        P.dump_info.append((name, dcol[0], ap.shape[0], w))
        dcol[0] += w

    def din(name, shape):
        return nc.dram_tensor(name, list(shape), F32, kind="ExternalInput").ap()

    def dout(name, shape):
        return nc.dram_tensor(name, list(shape), F32, kind="ExternalOutput").ap()

    xp = din("xp", [T, D])
    xs = din("xs", [NS, D])
    sd = din("sd", [L, NS, H, 128, 128])
    sq = din("sq", [L, NS, 3, QKV])
    sf = din("sf", [L, NS, 2, FF2])
    call = din("call", [17, D])
    w_mod = din("w_mod", [L, D, 6 * D])
    b_mod = din("b_mod", [L, 6 * D])
    norm1 = din("norm1", [L, D])
    w_in = din("w_in", [L, D, IN])
    conv_qkv = din("conv_qkv", [L, 4, QKV])
    a_log = din("a_log", [L, H])
    dt_bias = din("dt_bias", [L, H])
    gdn_norm = din("gdn_norm", [L, 128])
    sgu_norm = din("sgu_norm", [L, H, 128])
    w_sgu = din("w_sgu", [L, H, 128, 128])
    b_sgu = din("b_sgu", [L, H, 128])
    w_out = din("w_out", [L, D, D])
    norm2 = din("norm2", [L, D])
    w_up = din("w_up", [L, D, FF2])
    conv_ffn_w = din("conv_ffn_w", [L, 3, FF2])
    conv_ffn_b = din("conv_ffn_b", [L, FF2])
    w_down = din("w_down", [L, DFF, D])
    final_norm = din("final_norm", [D])
    consts = din("consts", [128, NCONST])

    yp = dout("yp", [T, D])
    ys = dout("ys", [NS, D])
    dp = dout("dp", [L, H, 128, 128])
    ds = dout("ds", [L, NS, H, 128, 128])
    qp = dout("qp", [L, 3, QKV])
    qs = dout("qs", [L, NS, 3, QKV])
    fp = dout("fp", [L, 2, FF2])
    fs = dout("fs", [L, NS, 2, FF2])
    sv = dout("sv", [L, NS, 512])
    dbg = dout("dbg", [128, 8192]) if stop_at is not None else None

    es = ExitStack()
    with es:
        def sb(name, shape, dt=F32):
            return es.enter_context(nc.sbuf_tensor(name, list(shape), dt))

        cst = sb("cst", [128, NCONST])
        identb_t = sb("identb", [128, 128], BF16)
        X = sb("X", [128, NT, D])
        Xs = sb("Xs", [128, D])
        cT_t = sb("cT", [128, 8 * 17], BF16)
        nT1 = sb("nT1", [128, 32])
        nT2 = sb("nT2", [128, 32])
        gdnB = sb("gdnB", [128, 128])
        sgnB = sb("sgnB", [128, 512])
        alB = sb("alB", [128, 16])
        dtB = sb("dtB", [128, 16])
        nexpA = sb("nexpA", [128, 16])
        modT_t = sb("modT", [128, 48 * 17])
        bmT = sb("bmT", [128, 48])
        A1T_t = sb("A1T", [128, 8 * 17])
        A2T_t = sb("A2T", [128, 8 * 17])
        gS = sb("gS", [128, D])
        SM = sb("SM", [128, 256])
        cwT = sb("cwT", [128, 48])
        bsT = sb("bsT", [128, 4])
        cfwT_t = sb("cfwT", [128, 3 * 44])
        cfbT = sb("cfbT", [128, 44])
        ps = [es.enter_context(nc.psum_tensor("ps%d" % i, [128, 512], F32)) for i in range(8)]
        rem = int(nc.sbuf_bytes_remaining) if not callable(nc.sbuf_bytes_remaining) else int(nc.sbuf_bytes_remaining())
        print('sbuf remaining', rem)
        if rem > 229376:
            rem = rem // 128
        NPH = (rem - 1024) // 4
        PH = sb("PH", [128, NPH])
        CV = Carve(PH, NPH)

        identF = cst[:, C_ID:C_ID + 128]
        ones = cst[:, C_ONE:C_ONE + 128]
        TRIinc = cst[:, C_TI:C_TI + 128]
        TRIsu = cst[:, C_TS:C_TS + 128]
        maskL = cst[:, C_ML:C_ML + 128]
        maskQ = cst[:, C_MQ:C_MQ + 128]
        tril = cst[:, C_TR:C_TR + 128]
        cm = cst[:, C_CM:C_CM + 2]
        e0 = cst[:, C_E0:C_E0 + 16]
        i16rep = v2(cst[:, C_I16:C_I16 + 256], 16)
        identb = identb_t[:, :]
        cT = v2(cT_t[:, :], 8)
        modT = v2(modT_t[:, :], 48)
        A1T = v2(A1T_t[:, :], 8)
        A2T = v2(A2T_t[:, :], 8)
        cfwT = v2(cfwT_t[:, :], 3)

        bank_i = [0]
        cur_pool = [None]

        def nb():
            pl = cur_pool[0]
            if pl is not None:
                b = ps[pl["l"][pl["i"]]]
                pl["i"] = (pl["i"] + 1) % len(pl["l"])
                return b[:, :]
            b = ps[bank_i[0]]
            bank_i[0] = (bank_i[0] + 1) % 8
            return b[:, :]

        def record(fns, pool):
            assert P.rec is None
            P.rec = []
            cur_pool[0] = pool
            for f in fns:
                f()
            out = P.rec
            P.rec = None
            cur_pool[0] = None
            return out

        def merge_replay(A, B, lead=0.0):
            na, nb_ = len(A), len(B)
            ia = ib = 0
            while ia < na or ib < nb_:
                if ib >= nb_ or (ia < na and ia * nb_ <= (ib + lead * nb_) * na):
                    P.replay(A[ia])
                    ia += 1
                else:
                    P.replay(B[ib])
                    ib += 1

        def mm(out, lhsT, rhs, start=True, stop=True):
            P.op("pe", "matmul", out=out, lhsT=lhsT, rhs=rhs, start=start, stop=stop)

        def tr(out, in_, identity):
            P.op("pe", "transpose", out=out, in_=in_, identity=identity)

        def act(out, in_, func, **kw):
            P.op("act", "activation", out=out, in_=in_, func=func, **kw)

        def tt(out, in0, in1, op, eng="dve"):
            P.op(eng, "tensor_tensor", out=out, in0=in0, in1=in1, op=op)

        def ts(out, in0, s1, s2, op0, op1=None, eng="dve"):
            if op1 is None:
                P.op(eng, "tensor_scalar", out=out, in0=in0, scalar1=s1, scalar2=None, op0=op0)
            else:
                P.op(eng, "tensor_scalar", out=out, in0=in0, scalar1=s1, scalar2=s2, op0=op0, op1=op1)

        def stt(out, in0, scalar, in1, op0, op1, eng="dve"):
            P.op(eng, "scalar_tensor_tensor", out=out, in0=in0, scalar=scalar, in1=in1, op0=op0, op1=op1)

        def cp(out, in_, eng="dve"):
            if eng == "act":
                P.op("act", "activation", out=out, in_=in_, func=AF.Copy)
            else:
                P.op(eng, "tensor_copy", out=out, in_=in_)

        def mset(ap, val, eng="dve"):
            P.op(eng, "memset", ap=ap, constant=val)

        def rsqrt_mean(out, ss, scale, n):
            act(out, ss, AF.Ln, scale=scale, bias=EPS)
            act(out, out, AF.Exp, scale=-0.5)

        def bc(ap, shape):
            return ap.to_broadcast(list(shape))

        P.dma("sp", out=cst[:, :], in_=consts[:, :])
        for i in range(4):
            P.dma("sp", out=X[:, 4 * i:4 * i + 4, :],
                  in_=xp.rearrange("(t p) d -> p t d", p=128)[:, 4 * i:4 * i + 4, :])
        P.dma("sp", out=Xs[0:16, :], in_=xs[:, :])
        cp(identb, identF)
        P.dma("sp", out=alB[:, :], in_=a_log.rearrange("l h -> (l h)").partition_broadcast(128))
        P.dma("sp", out=dtB[:, :], in_=dt_bias.rearrange("l h -> (l h)").partition_broadcast(128))
        act(nexpA[:, :], alB[:, :], AF.Exp)
        ts(nexpA[:, :], nexpA[:, :], -1.0, None, ALU.mult)
        CV.reset()
        craw = CV.f32(1024)
        n1raw = CV.f32(128)
        n2raw = CV.f32(128)
        P.dma("sp", out=craw[0:17, :], in_=call[:, :])
        P.dma("sp", out=n1raw[0:32, :], in_=norm1.rearrange("l (c p) -> (l c) p", p=128))
        P.dma("sp", out=n2raw[0:32, :], in_=norm2.rearrange("l (c p) -> (l c) p", p=128))
        act(craw[0:17, :], craw[0:17, :], AF.Silu)
        b = nb()
        for k in range(8):
            mm(b[:, k * 17:(k + 1) * 17], craw[0:17, k * 128:(k + 1) * 128], identF[0:17, 0:17])
        cp(cT_t[:, :], b[:, 0:136])
        b = nb()
        mm(b[:, 0:32], n1raw[0:32, :], identF[0:32, 0:32])
        mm(b[:, 32:64], n2raw[0:32, :], identF[0:32, 0:32])
        cp(nT1[:, :], b[:, 0:32])
        cp(nT2[:, :], b[:, 32:64])
        dump("n1raw", n1raw[0:32, :])
        dump("nT1a", nT1[:, :])
        dump("nT2a", nT2[:, :])
        dump("craw", craw[0:17, :])

        def sm(a, b_, n=128):
            return SM[0:n, a:b_]

        def mod_phase(l):
            CV.reset()
            wm = [v2(CV.bf16(8 * 512), 8) for _ in range(2)]
            bmraw = CV.f32(128)
            mod_ops(l, wm, bmraw, 512)

        def mod_ops(l, wm, bmraw, ncols):
            sub = ncols // 128
            P.dma("sp", out=bmraw[0:48, :], in_=b_mod[l].rearrange("(c p) -> c p", p=128))
            b = nb()
            mm(b[:, 0:48], bmraw[0:48, :], identF[0:48, 0:48])
            cp(bmT[:, :], b[:, 0:48])
            wsrc = w_mod[l].rearrange("(k p) n -> p k n", p=128)
            for c in range(6 * D // ncols):
                w_ = wm[c % 2]
                P.dma("pool", out=w_, in_=wsrc[:, :, c * ncols:(c + 1) * ncols])
                b = nb()
                for j in range(sub):
                    for k in range(8):
                        mm(b[:, j * 17:(j + 1) * 17], w_[:, k, j * 128:(j + 1) * 128], cT[:, k, :],
                           start=(k == 0), stop=(k == 7))
                tt(modT[:, sub * c:sub * c + sub, :], v2(b[:, 0:17 * sub], sub),
                   bc(bmT[:, sub * c:sub * c + sub].unsqueeze(2), [128, sub, 17]), ALU.add)
            for (AT, c0, nT) in ((A1T, 8, nT1), (A2T, 32, nT2)):
                ts(AT, modT[:, c0:c0 + 8, :], 1.0, None, ALU.add)
                tt(AT, AT, bc(nT[:, l * 8:l * 8 + 8].unsqueeze(2), [128, 8, 17]), ALU.mult)

        def make_g(c0, repbuf, gB):
            rep = v2(repbuf, 8)
            cp(rep, bc(modT[:, c0:c0 + 8, 0:1], [128, 8, 128]))
            for half in range(2):
                b = nb()
                for kk in range(4):
                    k = half * 4 + kk
                    mm(b[:, kk * 128:(kk + 1) * 128], rep[:, k, :], identF)
                cp(gB[:, half * 512:(half + 1) * 512], b, eng="act")
                b = nb()
                for kk in range(4):
                    k = half * 4 + kk
                    mm(b[0:16, kk * 128:(kk + 1) * 128], modT[:, c0 + k, 1:17], identF)
                cp(gS[0:16, half * 512:(half + 1) * 512], b[0:16, :], eng="act")

        def norm_transpose(xin, n, xn, AT, shc0, dstT, tmpA, col):
            ss = sm(0, 1, n)
            rstd = sm(1, 2, n)
            mset(ss, 0.0)
            act(tmpA[0:n, :], xin, AF.Square, accum_out=ss)
            rsqrt_mean(rstd, ss, 1.0 / D, n)
            ts(xn[0:n, :], xin, rstd, None, ALU.mult)
            b = nb().bitcast(BF16)
            for k in range(8):
                tr(b[:, k * n:(k + 1) * n], xn[0:n, k * 128:(k + 1) * 128], identb[0:n, 0:n])
            bv = v2(b[:, 0:8 * n], 8)
            tv = v2(tmpA[:, 0:8 * n], 8)
            if n == 128:
                Aap = bc(AT[:, :, 0:1], [128, 8, 128])
                Sap = bc(modT[:, shc0:shc0 + 8, 0:1], [128, 8, 128])
            else:
                Aap = AT[:, :, 1:17]
                Sap = modT[:, shc0:shc0 + 8, 1:17]
            tt(tv, bv, Aap, ALU.mult)
            tt(dstT[:, :, col:col + n], tv, Sap, ALU.add)

        def mixer_phase(l):
            CV.reset()
            w_in_sb = v2(CV.bf16(8 * IN), 8)
            w_out_sb = v2(CV.bf16(8 * D), 8)
            xn = CV.bf16(1024)
            hT = v2(CV.bf16(1024), 8)
            qkvx = CV.f32(524)
            qcar = v2(CV.f32(36), 12)
            qkn = v2(CV.f32(1024), 8)
            vtok = v2(CV.f32(512), 4)
            ktok = v2(CV.f32(512), 4)
            sz = CV.f32(512)
            u_ = v2(CV.f32(512), 4)
            TmAll = CV.f32(2048)
            Tm = [v2(TmAll[:, 512 * i:512 * (i + 1)], 4) for i in range(4)]
            tmpA = TmAll[:, 0:1024]
            NMall = CV.f32(2048)
            NMr = [NMall[:, 0:1024], NMall[:, 1024:2048]]
            acc = v2(NMall[:, 0:1536], 12)
            PQ = CV.f32(1536)
            stq_buf = PQ
            PPr = [v2(PQ[:, 0:512], 4), v2(PQ[:, 512:1024], 4)]
            QKmT = v2(PQ[:, 1024:1536], 4)
            print("mixer arena used", CV.off, "of", CV.n)
            o_ = v2(CV.f32(512), 4)
            S_ = v2(CV.f32(512), 4)
            WmT = v2(CV.f32(512), 4)
            mix = xn
            mixT = hT

            wsrc = w_in[l].rearrange("(k p) n -> p k n", p=128)
            for k in range(8):
                P.dma("pool", out=w_in_sb[:, k, :], in_=wsrc[:, k, :])
            wsrc2 = w_out[l].rearrange("(k p) n -> p k n", p=128)
            for k2 in range(2):
                P.dma("pool", out=w_out_sb[:, 4 * k2:4 * k2 + 4, :], in_=wsrc2[:, 4 * k2:4 * k2 + 4, :])

            make_g(16, tmpA, None)
            craw = Tm[0]
            cr = craw.rearrange("p a b -> p (a b)")
            P.dma("sp", out=cr[0:48, 0:128], in_=conv_qkv[l].rearrange("j (c p) -> (j c) p", p=128))
            P.dma("sp", out=cr[0:4, 128:256], in_=b_sgu[l])
            b = nb()
            mm(b[:, 0:48], cr[0:48, 0:128], identF[0:48, 0:48])
            mm(b[:, 64:68], cr[0:4, 128:256], identF[0:4, 0:4])
            cp(cwT[:, :], b[:, 0:48])
            cp(bsT[:, :], b[:, 64:68])
            P.dma("sp", out=sgnB[:, :], in_=sgu_norm[l].rearrange("h d -> (h d)").partition_broadcast(128))
            P.dma("sp", out=gdnB[:, :], in_=gdn_norm[l].partition_broadcast(128))
            mset(qcar, 0.0)
            Wraw = Tm[1]
            P.dma("sp", out=Wraw, in_=w_sgu[l].rearrange("h i j -> i h j"))
            w00 = sm(100, 104, 16)
            b00 = sm(104, 108, 16)
            b = nb()
            mm(b[0:16, 0:4], e0, Wraw[:, :, 0])
            mm(b[0:16, 4:8], e0, bsT[:, :])
            cp(SM[0:16, 100:108], b[0:16, 0:8])
            tt(Wraw, Wraw, bc(tril.unsqueeze(1), [128, 4, 128]), ALU.mult)
            b = nb()
            for h in range(4):
                tr(b[:, h * 128:(h + 1) * 128], Wraw[:, h, :], identF)
            cp(WmT, v2(b, 4))
            mset(S_, 0.0)

            nA = nexpA[:, l * 4:l * 4 + 4]
            dB = dtB[:, l * 4:l * 4 + 4]

            def tile(t, sample):
                n = 16 if sample else 128
                xin = Xs[0:16, :] if sample else X[:, t, :]
                tg_ = ("stile_" if sample else "tile_")

                def tck(x):
                    if l == 0 and t == 0:
                        ckpt(tg_ + x)
                norm_transpose(xin, n, xn, A1T, 0, hT, tmpA, 0)
                dump("hT", hT.rearrange("p a b -> p (a b)"))
                tck("a")
                if sample:
                    QX = v2(qkvx[:, 0:256], 4)

                    def tap(j):
                        return QX[:, :, 16 * j:16 * j + 16]
                    cur = QX[:, :, 48:64]
                    P.dma("sp", out=qs[l][:, 0:2, :], in_=sq[l][:, 1:3, :])
                else:
                    QX = v2(qkvx[:, 0:524], 4)

                    def tap(j):
                        return QX[:, :, j:j + 128]
                    cur = QX[:, :, 3:131]
                for grp in range(3):
                    c4 = slice(4 * grp, 4 * grp + 4)
                    b = nb()
                    for cc in range(4):
                        c = grp * 4 + cc
                        for k in range(8):
                            mm(b[:, cc * n:(cc + 1) * n], w_in_sb[:, k, c * 128:(c + 1) * 128], hT[:, k, 0:n],
                               start=(k == 0), stop=(k == 7))
                    cp(cur, v2(b[:, 0:4 * n], 4), eng="act")
                    if sample:
                        stq = v2(stq_buf[0:16, :], 3)
                        P.dma("sp", out=stq, in_=sq[l][:, :, grp * 512:(grp + 1) * 512])
                        b2 = nb()
                        for cc in range(4):
                            for j in range(3):
                                idx = cc * 3 + j
                                mm(b2[:, idx * 16:(idx + 1) * 16], stq[:, j, cc * 128:(cc + 1) * 128],
                                   identF[0:16, 0:16])
                        cp(QX[:, :, 0:48], v2(b2[:, 0:192], 4), eng="act")
                        b3 = nb()
                        for k in range(8):
                            mm(b3[0:16, :], hT[:, k, 0:16], w_in_sb[:, k, grp * 512:(grp + 1) * 512],
                               start=(k == 0), stop=(k == 7))
                        qsn = Tm[2].rearrange("p a b -> p (a b)")
                        cp(qsn[0:16, :], b3[0:16, :])
                        P.dma("sp", out=qs[l][:, 2, grp * 512:(grp + 1) * 512], in_=qsn[0:16, :])
                    else:
                        cp(QX[:, :, 0:3], qcar[:, c4, :])
                    ag = acc[:, c4, 0:n]
                    tg = Tm[3][:, :, 0:n]

                    def cw(j):
                        return bc(cwT[:, j * 12 + 4 * grp:j * 12 + 4 * grp + 4].unsqueeze(2), [128, 4, n])
                    tt(ag, tap(3), cw(3), ALU.mult)
                    for j in (2, 1, 0):
                        tt(tg, tap(j), cw(j), ALU.mult)
                        tt(ag, ag, tg, ALU.add)
                    if not sample:
                        cp(qcar[:, c4, :], QX[:, :, 128:131])
                if (not sample) and t == NT - 1:
                    b = nb()
                    for j in range(3):
                        mm(b[0:12, j * 128:(j + 1) * 128], qcar[:, :, j], identF)
                    qtail = Tm[2].rearrange("p a b -> p (a b)")
                    cp(qtail[0:12, 0:384], b[0:12, 0:384])
                    P.dma("sp", out=qp[l].rearrange("j (c p) -> c j p", p=128), in_=v2(qtail[0:12, 0:384], 3))
                QC = acc[:, :, 0:n]
                act(QC, QC, AF.Silu)
                dump("QC", acc.rearrange("p a b -> p (a b)"))
                tck("b")
                sqb = v2(tmpA[:, 0:8 * n], 8)
                tt(sqb, QC[:, 0:8, :], QC[:, 0:8, :], ALU.mult)
                for half in range(2):
                    b = nb()
                    mm(b[:, 0:4 * n], ones, sqb[:, 4 * half:4 * half + 4, :])
                    rsv = sqb[:, 4 * half:4 * half + 4, :]
                    act(rsv, v2(b[:, 0:4 * n], 4), AF.Ln, bias=EPS)
                    act(rsv, rsv, AF.Exp, scale=-0.5, bias=(-0.5 * math.log(128.0) if half == 0 else 0.0))
                tt(qkn[:, :, 0:n], QC[:, 0:8, :], sqb, ALU.mult)
                dump("qkn", qkn.rearrange("p a b -> p (a b)"))
                tck("c")
                b = nb()
                for h in range(4):
                    tr(b[0:n, h * 128:(h + 1) * 128], QC[:, 8 + h, :], identF)
                cp(vtok[0:n], v2(b[0:n, :], 4), eng="act")
                b = nb()
                for h in range(4):
                    tr(b[0:n, h * 128:(h + 1) * 128], qkn[:, 4 + h, 0:n], identF)
                cp(ktok[0:n], v2(b[0:n, :], 4), eng="act")
                bz, bu, bv_, bb = nb(), nb(), nb(), nb()
                for (bk, c0, nn) in ((bz, 1536, 512), (bb, 2048, 8), (bu, 2056, 512), (bv_, 2568, 512)):
                    for k in range(8):
                        mm(bk[0:n, 0:nn], hT[:, k, 0:n], w_in_sb[:, k, c0:c0 + nn], start=(k == 0), stop=(k == 7))
                act(sz[0:n, :], bz[0:n, :], AF.Silu)
                uf = u_.rearrange("p a b -> p (a b)")
                act(uf[0:n, :], bu[0:n, :], AF.Gelu_apprx_tanh)
                vg = Tm[0]
                vgf = vg.rearrange("p a b -> p (a b)")
                act(vgf[0:n, :], bv_[0:n, :], AF.Gelu_apprx_tanh)
                ba = sm(8, 16, n)
                cp(ba, bb[0:n, 0:8])
                beta = sm(16, 20, n)
                act(beta, SM[0:n, 8:12], AF.Sigmoid)
                xa, ab_, ee, mx, g_ = sm(20, 24, n), sm(24, 28, n), sm(28, 32, n), sm(32, 36, n), sm(36, 40, n)
                tt(xa, SM[0:n, 12:16], dB[0:n, :], ALU.add)
                ts(ab_, xa, -1.0, None, ALU.mult)
                tt(ab_, ab_, xa, ALU.max)
                act(ee, ab_, AF.Exp, scale=-1.0)
                act(ee, ee, AF.Ln, bias=1.0)
                ts(mx, xa, 0.0, None, ALU.max)
                tt(mx, mx, ee, ALU.add)
                tt(g_, mx, nA[0:n, :], ALU.mult)
                dump("vtok", vtok.rearrange("p a b -> p (a b)"))
                dump("ktok", ktok.rearrange("p a b -> p (a b)"))
                dump("sz", sz)
                dump("u", u_.rearrange("p a b -> p (a b)"))
                dump("SMd", SM[:, :])
                tck("d")
                sqv = Tm[1]
                tt(sqv[0:n], vg[0:n], vg[0:n], ALU.mult)
                ssv = sm(80, 84, n)
                P.op("dve", "reduce_sum", out=ssv, in_=sqv[0:n], axis=AX.X)
                tck("e1")
                rv = sm(84, 88, n)
                rsqrt_mean(rv, ssv, 1.0 / 128, n)
                vv = Tm[1]
                tt(vv[0:n], vg[0:n], bc(rv.unsqueeze(2), [n, 4, 128]), ALU.mult)
                tt(vv[0:n], vv[0:n], v2(sgnB[0:n, :], 4), ALU.mult)
                tck("e2")
                mixv = v2(mix, 8)
                if sample:
                    P.dma("sp", out=sv[l], in_=vv[0:16].rearrange("p a b -> p (a b)"))
                    zt = Tm[2]
                    tt(zt[0:16], vv[0:16], bc(w00.unsqueeze(2), [16, 4, 128]), ALU.mult)
                    tt(zt[0:16], zt[0:16], bc(b00.unsqueeze(2), [16, 4, 128]), ALU.add)
                    tt(mixv[0:16, 4:8, :], zt[0:16], u_[0:16], ALU.mult)
                else:
                    b = nb()
                    for h in range(4):
                        mm(b[:, h * 128:(h + 1) * 128], WmT[:, h, :], vv[:, h, :])
                    tck("e3")
                    for h in range(4):
                        stt(mixv[:, 4 + h, :], b[:, h * 128:(h + 1) * 128], bsT[:, h:h + 1], u_[:, h, :],
                            ALU.add, ALU.mult)
                dump("vv", Tm[1].rearrange("p a b -> p (a b)"))
                tck("e")
                if sample:
                    delta_sample(l, g_, beta)
                else:
                    delta_prompt(g_, beta)
                dump("o", o_.rearrange("p a b -> p (a b)"))
                dump("S", S_.rearrange("p a b -> p (a b)"))
                tck("f")
                sqo = Tm[0]
                tt(sqo[0:n], o_[0:n], o_[0:n], ALU.mult)
                sso = sm(88, 92, n)
                P.op("dve", "reduce_sum", out=sso, in_=sqo[0:n], axis=AX.X)
                ro = sm(92, 96, n)
                rsqrt_mean(ro, sso, 1.0 / 128, n)
                on = Tm[0]
                tt(on[0:n], o_[0:n], bc(ro.unsqueeze(2), [n, 4, 128]), ALU.mult)
                tt(on[0:n], on[0:n], bc(gdnB[0:n, :].unsqueeze(1), [n, 4, 128]), ALU.mult)
                tt(mixv[0:n, 0:4, :], on[0:n], v2(sz[0:n, :], 4), ALU.mult)
                dump("mix", mix)
                tck("g")
                b = nb().bitcast(BF16)
                for k in range(8):
                    tr(b[:, k * n:(k + 1) * n], mix[0:n, k * 128:(k + 1) * 128], identb[0:n, 0:n])
                cp(mixT[:, :, 0:n], v2(b[:, 0:8 * n], 8), eng="act")
                gsrc = gS if sample else gB
                for half in range(2):
                    b = nb()
                    for k in range(8):
                        mm(b[0:n, :], mixT[:, k, 0:n], w_out_sb[:, k, half * 512:(half + 1) * 512],
                           start=(k == 0), stop=(k == 7))
                    hs = slice(half * 512, (half + 1) * 512)
                    tt(tmpA[0:n, hs], b[0:n, :], gsrc[0:n, hs], ALU.mult)
                    tt(xin[:, hs], xin[:, hs], tmpA[0:n, hs], ALU.add)

            def delta_prompt(g_, beta):
                gc = sm(40, 56)
                ex = sm(56, 72)
                gsel = sm(72, 80)
                bg = nb()
                mm(bg[:, 0:4], TRIinc, g_)
                mm(bg[:, 4:8], TRIsu, g_)
                tt(v2(gsel, 4), bc(g_.unsqueeze(2), [128, 4, 2]), bc(cm.unsqueeze(1), [128, 4, 2]), ALU.mult)
                mm(bg[:, 8:16], ones, gsel)
                cp(gc, bg[:, 0:16])
                act(ex, gc, AF.Exp)
                gcum = SM[:, 40:44]
                eg = SM[:, 56:60]
                ekd = SM[:, 60:64]
                egl = SM[:, 64:72]
                gBk = Tm[2]
                cp(gBk, bc(g_.unsqueeze(2), [128, 4, 128]))
                bG = nb()
                for h in range(4):
                    mm(bG[:, h * 128:(h + 1) * 128], gBk[:, h, :], TRIinc)
                xd = Tm[3]
                for h in range(4):
                    ts(xd[:, h, :], bG[:, h * 128:(h + 1) * 128], gcum[:, h:h + 1], 0.0, ALU.subtract, ALU.max)
                act(xd, xd, AF.Exp, scale=-1.0)
                decL = Tm[2]
                decQ = o_
                tt(decL, xd, bc(maskL.unsqueeze(1), [128, 4, 128]), ALU.mult)
                tt(decQ, xd, bc(maskQ.unsqueeze(1), [128, 4, 128]), ALU.mult)
                bK = nb()
                bQ = nb()
                for h in range(4):
                    mm(bK[:, h * 128:(h + 1) * 128], qkn[:, 4 + h, :], qkn[:, 4 + h, :])
                for h in range(4):
                    mm(bQ[:, h * 128:(h + 1) * 128], qkn[:, h, :], qkn[:, 4 + h, :])
                NM = [v3(r, 2, 4) for r in NMr]
                N1 = NM[0][:, 0]
                M1 = NM[0][:, 1]
                for h in range(4):
                    stt(N1[:, h, :], bK[:, h * 128:(h + 1) * 128], beta[:, h:h + 1], decL[:, h, :], ALU.mult, ALU.mult)
                QKm = Tm[3]
                tt(QKm, v2(bQ, 4), decQ, ALU.mult)
                b1 = nb()
                b2 = nb()
                for h in range(4):
                    tr(b1[:, h * 128:(h + 1) * 128], N1[:, h, :], identF)
                for h in range(4):
                    tr(b2[:, h * 128:(h + 1) * 128], QKm[:, h, :], identF)
                cp(M1, v2(b1, 4), eng="act")
                cp(QKmT, v2(b2, 4), eng="act")
                Pc, Pn = PPr[0], PPr[1]
                tt(Pc, bc(identF.unsqueeze(1), [128, 4, 128]), M1, ALU.subtract)
                cur = 0
                for s in range(5):
                    Nc, Mc = NM[cur][:, 0], NM[cur][:, 1]
                    Nn, Mn = NM[1 - cur][:, 0], NM[1 - cur][:, 1]
                    bN = nb()
                    for h in range(4):
                        mm(bN[:, h * 128:(h + 1) * 128], Mc[:, h, :], Nc[:, h, :])
                    if s < 4:
                        bM = nb()
                        for h in range(4):
                            mm(bM[:, h * 128:(h + 1) * 128], Nc[:, h, :], Mc[:, h, :])
                    cp(Nn, v2(bN, 4), eng="act")
                    if s < 4:
                        cp(Mn, v2(bM, 4))
                    bP = nb()
                    for h in range(4):
                        mm(bP[:, h * 128:(h + 1) * 128], Nn[:, h, :], Pc[:, h, :])
                    tt(Pn, v2(bP, 4), Pc, ALU.add)
                    Pc, Pn = Pn, Pc
                    cur = 1 - cur
                TT = Pc
                kdec = ktok
                tt(kdec, ktok, bc(ekd.unsqueeze(2), [128, 4, 128]), ALU.mult)
                r2, rhs2, vnew, otmp = Tm[0], Tm[1], Tm[2], Tm[3]
                for c in range(2):
                    rs_ = slice(64 * c, 64 * c + 64)
                    bKS = nb()
                    bQS = nb()
                    for h in range(4):
                        mm(bKS[:, h * 128:(h + 1) * 128], qkn[:, 4 + h, :], S_[:, h, :])
                    for h in range(4):
                        mm(bQS[:, h * 128:(h + 1) * 128], qkn[:, h, :], S_[:, h, :])
                    for h in range(4):
                        stt(r2[rs_, h, :], bKS[rs_, h * 128:(h + 1) * 128], eg[rs_, h:h + 1], vtok[rs_, h, :],
                            ALU.mult, ALU.subtract)
                    for h in range(4):
                        ts(rhs2[rs_, h, :], r2[rs_, h, :], beta[rs_, h:h + 1], -1.0, ALU.mult, ALU.mult)
                    bV = nb()
                    for h in range(4):
                        mm(bV[:, h * 128:(h + 1) * 128], TT[rs_, h, :], rhs2[rs_, h, :])
                    cp(vnew[rs_], v2(bV, 4)[rs_], eng="act")
                    bO = nb()
                    for h in range(4):
                        mm(bO[:, h * 128:(h + 1) * 128], QKmT[rs_, h, :], vnew[rs_, h, :])
                    cp(otmp[rs_], v2(bO, 4)[rs_], eng="act")
                    for h in range(4):
                        stt(o_[rs_, h, :], bQS[rs_, h * 128:(h + 1) * 128], eg[rs_, h:h + 1], otmp[rs_, h, :],
                            ALU.mult, ALU.add)
                    bS = nb()
                    for h in range(4):
                        mm(bS[:, h * 128:(h + 1) * 128], kdec[rs_, h, :], vnew[rs_, h, :])
                    for h in range(4):
                        stt(S_[:, h, :], S_[:, h, :], egl[:, 2 * h + c:2 * h + c + 1], bS[:, h * 128:(h + 1) * 128],
                            ALU.mult, ALU.add)

            def delta_sample(l, g_, beta):
                egs = sm(56, 60, 16)
                act(egs, g_, AF.Exp)
                b = nb()
                for h in range(4):
                    tr(b[0:16, h * 128:(h + 1) * 128], qkn[:, h, 0:16], identF)
                qtok = Tm[0]
                cp(qtok[0:16], v2(b[0:16, :], 4), eng="act")
                tt(qtok[0:16], qtok[0:16], ktok[0:16], ALU.mult)
                qk = sm(108, 112, 16)
                P.op("dve", "reduce_sum", out=qk, in_=qtok[0:16], axis=AX.X)
                gdiag = Tm[1].rearrange("p a b -> p (a b)")[0:16, 0:64]
                tt(v2(gdiag, 16), bc(g_.unsqueeze(1), [16, 16, 4]),
                   bc(identF[0:16, 0:16].unsqueeze(2), [16, 16, 4]), ALU.mult)
                b = nb()
                mm(b[:, 0:64], ones[0:16, :], gdiag)
                EGb = sm(128, 192)
                act(EGb, b[:, 0:64], AF.Exp)
                r2, vnew, otmp = Tm[2], Tm[3], PPr[0]
                kqm = Tm[1].rearrange("p a b -> p (a b)")
                for h in range(4):
                    SA = v2(NMall, 16)
                    P.dma("sp", out=SA, in_=sd[l][:, h].rearrange("b k v -> k b v"))
                    kTm = v2(kqm[:, 0:256], 16)
                    qTm = v2(kqm[:, 256:512], 16)
                    tt(kTm, bc(qkn[:, 4 + h, 0:16].unsqueeze(1), [128, 16, 16]), i16rep, ALU.mult)
                    tt(qTm, bc(qkn[:, h, 0:16].unsqueeze(1), [128, 16, 16]), i16rep, ALU.mult)
                    bKS = nb()
                    bQS = nb()
                    for bb_ in range(16):
                        mm(bKS[0:16, 0:128], kTm[:, bb_, :], SA[:, bb_, :], start=(bb_ == 0), stop=(bb_ == 15))
                    for bb_ in range(16):
                        mm(bQS[0:16, 0:128], qTm[:, bb_, :], SA[:, bb_, :], start=(bb_ == 0), stop=(bb_ == 15))
                    stt(r2[0:16, h, :], bKS[0:16, 0:128], egs[:, h:h + 1], vtok[0:16, h, :], ALU.mult, ALU.subtract)
                    ts(vnew[0:16, h, :], r2[0:16, h, :], beta[:, h:h + 1], -1.0, ALU.mult, ALU.mult)
                    ts(otmp[0:16, h, :], vnew[0:16, h, :], qk[:, h:h + 1], None, ALU.mult)
                    stt(o_[0:16, h, :], bQS[0:16, 0:128], egs[:, h:h + 1], otmp[0:16, h, :], ALU.mult, ALU.add)
                    vmr = Tm[0].rearrange("p a b -> p (a b)")
                    for q4 in range(4):
                        bS = nb()
                        for j in range(4):
                            bb_ = q4 * 4 + j
                            vm = vmr[0:16, j * 128:(j + 1) * 128]
                            ts(vm, vnew[0:16, h, :], identF[0:16, bb_:bb_ + 1], None, ALU.mult)
                            mm(bS[:, j * 128:(j + 1) * 128], ktok[0:16, h, :], vm)
                        for j in range(4):
                            bb_ = q4 * 4 + j
                            stt(SA[:, bb_, :], SA[:, bb_, :], SM[:, 128 + bb_ * 4 + h:128 + bb_ * 4 + h + 1],
                                bS[:, j * 128:(j + 1) * 128], ALU.mult, ALU.add)
                    P.dma("sp", out=ds[l][:, h].rearrange("b k v -> k b v"), in_=SA)

            ckpt("mix%d_params" % l)
            for t in range(NT):
                tile(t, False)
                ckpt("mix%d_t%d" % (l, t))
            P.dma("sp", out=dp[l].rearrange("h k v -> k h v"), in_=S_)
            ckpt("mix%d_dp" % l)
            tile(0, True)

        def mixer_phase2(l):
            CV.reset()
            WBLK = [(0, 512), (512, 512), (1024, 512), (1536, 512), (2048, 8), (2056, 512), (2568, 512)]
            w_in_blk = [v2(CV.bf16(8 * wn), 8) for (_, wn) in WBLK]

            def wcol(k, c0, n_):
                for bi, (b0, wn) in enumerate(WBLK):
                    if b0 <= c0 and c0 + n_ <= b0 + wn:
                        return w_in_blk[bi][:, k, c0 - b0:c0 - b0 + n_]
                raise AssertionError((c0, n_))
            w_out_sb = v2(CV.bf16(8 * D), 8)
            xn = CV.bf16(1024)
            hT = v2(CV.bf16(1024), 8)
            mixT = v2(xn, 8)
            qkvx = CV.f32(524)
            qcar = v2(CV.f32(36), 12)
            FA = CV.f32(2560)
            acc = v2(FA[:, 0:1536], 12)
            FT = FA[:, 1536:2560]
            qkn = v2(CV.f32(1024), 8)
            vtok = v2(CV.f32(512), 4)
            ktok = v2(CV.f32(512), 4)
            szb = [CV.f32(512) for _ in range(2)]
            u_ = v2(CV.f32(512), 4)
            mixb = [CV.bf16(1024) for _ in range(2)]
            TmAll = CV.f32(1536)
            Tm = [v2(TmAll[:, 512 * i:512 * (i + 1)], 4) for i in range(3)]
            tmpA = TmAll[:, 0:1024]
            NPQ = CV.f32(2048)
            NP = NPQ[:, 0:1536]
            NM = v3(NP[:, 0:1024], 2, 4)
            Pm = v2(NP[:, 1024:1536], 4)
            stq_buf = NP
            QKmT = v2(NPQ[:, 1536:2048], 4)
            o_ = v2(CV.f32(512), 4)
            S_ = v2(CV.f32(512), 4)
            WmT = v2(CV.f32(512), 4)
            print("mixer2 arena used", CV.off, "of", CV.n)

            wsrc = w_in[l].rearrange("(k p) n -> p k n", p=128)
            for bi, (b0, wn) in enumerate(WBLK):
                P.dma("pool", out=w_in_blk[bi], in_=wsrc[:, :, b0:b0 + wn])
            wsrc2 = w_out[l].rearrange("(k p) n -> p k n", p=128)
            for k2 in range(2):
                P.dma("pool", out=w_out_sb[:, 4 * k2:4 * k2 + 4, :], in_=wsrc2[:, 4 * k2:4 * k2 + 4, :])

            gBt = FA[:, 0:1024]
            make_g(16, FA[:, 1024:2048], gBt)
            cr = Tm[0].rearrange("p a b -> p (a b)")
            P.dma("sp", out=cr[0:48, 0:128], in_=conv_qkv[l].rearrange("j (c p) -> (j c) p", p=128))
            P.dma("sp", out=cr[0:4, 128:256], in_=b_sgu[l])
            b = nb()
            mm(b[:, 0:48], cr[0:48, 0:128], identF[0:48, 0:48])
            mm(b[:, 64:68], cr[0:4, 128:256], identF[0:4, 0:4])
            cp(cwT[:, :], b[:, 0:48])
            cp(bsT[:, :], b[:, 64:68])
            P.dma("sp", out=sgnB[:, :], in_=sgu_norm[l].rearrange("h d -> (h d)").partition_broadcast(128))
            P.dma("sp", out=gdnB[:, :], in_=gdn_norm[l].partition_broadcast(128))
            mset(qcar, 0.0)
            Wraw = Tm[1]
            P.dma("sp", out=Wraw, in_=w_sgu[l].rearrange("h i j -> i h j"))
            w00 = sm(100, 104, 16)
            b00 = sm(104, 108, 16)
            b = nb()
            mm(b[0:16, 0:4], e0, Wraw[:, :, 0])
            mm(b[0:16, 4:8], e0, bsT[:, :])
            cp(SM[0:16, 100:108], b[0:16, 0:8])
            tt(Wraw, Wraw, bc(tril.unsqueeze(1), [128, 4, 128]), ALU.mult)
            b = nb()
            for h in range(4):
                tr(b[:, h * 128:(h + 1) * 128], Wraw[:, h, :], identF)
            cp(WmT, v2(b, 4))

            nA = nexpA[:, l * 4:l * 4 + 4]
            dB = dtB[:, l * 4:l * 4 + 4]

            def slots(par, n):
                beta = SM[0:n, 16:20] if par == 0 else SM[0:n, 112:116]
                g_ = SM[0:n, 36:40] if par == 0 else SM[0:n, 116:120]
                return beta, g_

            def front(t, sample):
                n = 16 if sample else 128
                par = t % 2
                xin = Xs[0:16, :] if sample else X[:, t, :]
                sz = szb[par]
                mix = mixb[par]
                mixv = v2(mix, 8)
                beta, g_ = slots(par, n)
                if sample:
                    QX = v2(qkvx[:, 0:256], 4)

                    def tap(j):
                        return QX[:, :, 16 * j:16 * j + 16]
                    cur = QX[:, :, 48:64]
                else:
                    QX = v2(qkvx[:, 0:524], 4)

                    def tap(j):
                        return QX[:, :, j:j + 128]
                    cur = QX[:, :, 3:131]
                QC = acc[:, :, 0:n]

                def F1():
                    norm_transpose(xin, n, xn, A1T, 0, hT, FT, 0)
                    if sample:
                        P.dma("sp", out=qs[l][:, 0:2, :], in_=sq[l][:, 1:3, :])

                def Fq(grp):
                    c4 = slice(4 * grp, 4 * grp + 4)
                    b = nb()
                    for cc in range(4):
                        c = grp * 4 + cc
                        for k in range(8):
                            mm(b[:, cc * n:(cc + 1) * n], wcol(k, c * 128, 128), hT[:, k, 0:n],
                               start=(k == 0), stop=(k == 7))
                    cp(cur, v2(b[:, 0:4 * n], 4), eng="act")
                    if sample:
                        stq = v2(stq_buf[0:16, :], 3)
                        P.dma("sp", out=stq, in_=sq[l][:, :, grp * 512:(grp + 1) * 512])
                        b2 = nb()
                        for cc in range(4):
                            for j in range(3):
                                idx = cc * 3 + j
                                mm(b2[:, idx * 16:(idx + 1) * 16], stq[:, j, cc * 128:(cc + 1) * 128],
                                   identF[0:16, 0:16])
                        cp(QX[:, :, 0:48], v2(b2[:, 0:192], 4), eng="act")
                        b3 = nb()
                        for k in range(8):
                            mm(b3[0:16, :], hT[:, k, 0:16], wcol(k, grp * 512, 512),
                               start=(k == 0), stop=(k == 7))
                        qsn = Tm[2].rearrange("p a b -> p (a b)")
                        cp(qsn[0:16, :], b3[0:16, :])
                        P.dma("sp", out=qs[l][:, 2, grp * 512:(grp + 1) * 512], in_=qsn[0:16, :])
                    else:
                        cp(QX[:, :, 0:3], qcar[:, c4, :])
                    ag = acc[:, c4, 0:n]
                    tg = v2(FT[:, 0:512], 4)[:, :, 0:n]

                    def cw(j):
                        return bc(cwT[:, j * 12 + 4 * grp:j * 12 + 4 * grp + 4].unsqueeze(2), [128, 4, n])
                    tt(ag, tap(3), cw(3), ALU.mult)
                    for j in (2, 1, 0):
                        tt(tg, tap(j), cw(j), ALU.mult)
                        tt(ag, ag, tg, ALU.add)
                    if not sample:
                        cp(qcar[:, c4, :], QX[:, :, 128:131])
                    act(ag, ag, AF.Silu)
                    if grp == 2 and (not sample) and t == NT - 1:
                        b = nb()
                        for j in range(3):
                            mm(b[0:12, j * 128:(j + 1) * 128], qcar[:, :, j], identF)
                        qtail = FT
                        cp(qtail[0:12, 0:384], b[0:12, 0:384])
                        P.dma("sp", out=qp[l].rearrange("j (c p) -> c j p", p=128), in_=v2(qtail[0:12, 0:384], 3))

                def F7():
                    bz, bu, bv_, bb = nb(), nb(), nb(), nb()
                    for (bk, c0, nn) in ((bb, 2048, 8), (bz, 1536, 512), (bu, 2056, 512), (bv_, 2568, 512)):
                        for k in range(8):
                            mm(bk[0:n, 0:nn], hT[:, k, 0:n], wcol(k, c0, nn), start=(k == 0), stop=(k == 7))
                    ba = sm(8, 16, n)
                    cp(ba, bb[0:n, 0:8])
                    act(sz[0:n, :], bz[0:n, :], AF.Silu)
                    uf = u_.rearrange("p a b -> p (a b)")
                    act(uf[0:n, :], bu[0:n, :], AF.Gelu_apprx_tanh)
                    act(FT[0:n, 0:512], bv_[0:n, :], AF.Gelu_apprx_tanh)
                    act(beta, SM[0:n, 8:12], AF.Sigmoid)
                    xa, ab_, ee, mx = sm(20, 24, n), sm(24, 28, n), sm(28, 32, n), sm(32, 36, n)
                    tt(xa, SM[0:n, 12:16], dB[0:n, :], ALU.add)
                    ts(ab_, xa, -1.0, None, ALU.mult)
                    tt(ab_, ab_, xa, ALU.max)
                    act(ee, ab_, AF.Exp, scale=-1.0)
                    act(ee, ee, AF.Ln, bias=1.0)
                    ts(mx, xa, 0.0, None, ALU.max)
                    tt(mx, mx, ee, ALU.add)
                    tt(g_, mx, nA[0:n, :], ALU.mult)

                def F8():
                    vg = v2(FT[:, 0:512], 4)
                    vv = v2(FT[:, 512:1024], 4)
                    tt(vv[0:n], vg[0:n], vg[0:n], ALU.mult)
                    ssv = sm(80, 84, n)
                    P.op("dve", "reduce_sum", out=ssv, in_=vv[0:n], axis=AX.X)
                    rv = sm(84, 88, n)
                    rsqrt_mean(rv, ssv, 1.0 / 128, n)
                    tt(vv[0:n], vg[0:n], bc(rv.unsqueeze(2), [n, 4, 128]), ALU.mult)
                    tt(vv[0:n], vv[0:n], v2(sgnB[0:n, :], 4), ALU.mult)
                    if sample:
                        P.dma("sp", out=sv[l], in_=vv[0:16].rearrange("p a b -> p (a b)"))
                        zt = Tm[2]
                        tt(zt[0:16], vv[0:16], bc(w00.unsqueeze(2), [16, 4, 128]), ALU.mult)
                        tt(zt[0:16], zt[0:16], bc(b00.unsqueeze(2), [16, 4, 128]), ALU.add)
                        tt(mixv[0:16, 4:8, :], zt[0:16], u_[0:16], ALU.mult)
                    else:
                        b = nb()
                        for h in range(4):
                            mm(b[:, h * 128:(h + 1) * 128], WmT[:, h, :], vv[:, h, :])
                        for h in range(4):
                            stt(mixv[:, 4 + h, :], b[:, h * 128:(h + 1) * 128], bsT[:, h:h + 1], u_[:, h, :],
                                ALU.add, ALU.mult)

                def F5():
                    sqb = v2(FT[:, 0:8 * n], 8)
                    tt(sqb, QC[:, 0:8, :], QC[:, 0:8, :], ALU.mult)
                    for half in range(2):
                        b = nb()
                        mm(b[:, 0:4 * n], ones, sqb[:, 4 * half:4 * half + 4, :])
                        rsv = sqb[:, 4 * half:4 * half + 4, :]
                        act(rsv, v2(b[:, 0:4 * n], 4), AF.Ln, bias=EPS)
                        act(rsv, rsv, AF.Exp, scale=-0.5, bias=(-0.5 * math.log(128.0) if half == 0 else 0.0))
                    tt(qkn[:, :, 0:n], QC[:, 0:8, :], sqb, ALU.mult)

                def F6():
                    b = nb()
                    for h in range(4):
                        tr(b[0:n, h * 128:(h + 1) * 128], QC[:, 8 + h, :], identF)
                    cp(vtok[0:n], v2(b[0:n, :], 4), eng="act")
                    b = nb()
                    for h in range(4):
                        tr(b[0:n, h * 128:(h + 1) * 128], qkn[:, 4 + h, 0:n], identF)
                    cp(ktok[0:n], v2(b[0:n, :], 4), eng="act")

                early = [F1, lambda: Fq(0), lambda: Fq(1), lambda: Fq(2), F7, F8]
                late = [F5, F6]
                return early, late

            def back(t):
                par = t % 2
                sz = szb[par]
                mix = mixb[par]
                mixv = v2(mix, 8)
                beta, g_ = slots(par, 128)
                xin = X[:, t, :]
                gc = sm(40, 56)
                ex = sm(56, 72)
                gsel = sm(72, 80)
                gcum = SM[:, 40:44]
                eg = SM[:, 56:60]
                ekd = SM[:, 60:64]
                egl = SM[:, 64:72]
                N_ = NM[:, 0]
                M_ = NM[:, 1]

                def B1():
                    bg = nb()
                    mm(bg[:, 0:4], TRIinc, g_)
                    mm(bg[:, 4:8], TRIsu, g_)
                    tt(v2(gsel, 4), bc(g_.unsqueeze(2), [128, 4, 2]), bc(cm.unsqueeze(1), [128, 4, 2]), ALU.mult)
                    mm(bg[:, 8:16], ones, gsel)
                    cp(gc, bg[:, 0:16])
                    act(ex, gc, AF.Exp)
                    gBk = Tm[1]
                    cp(gBk, bc(g_.unsqueeze(2), [128, 4, 128]))
                    bG = nb()
                    for h in range(4):
                        mm(bG[:, h * 128:(h + 1) * 128], gBk[:, h, :], TRIinc)
                    xd = Tm[2]
                    for h in range(4):
                        ts(xd[:, h, :], bG[:, h * 128:(h + 1) * 128], gcum[:, h:h + 1], 0.0, ALU.subtract, ALU.max)
                    act(xd, xd, AF.Exp, scale=-1.0)
                    tt(Tm[1], xd, bc(maskL.unsqueeze(1), [128, 4, 128]), ALU.mult)
                    tt(o_, xd, bc(maskQ.unsqueeze(1), [128, 4, 128]), ALU.mult)

                def B2():
                    decL, decQ = Tm[1], o_
                    bK = nb()
                    bQ = nb()
                    for h in range(4):
                        mm(bK[:, h * 128:(h + 1) * 128], qkn[:, 4 + h, :], qkn[:, 4 + h, :])
                    for h in range(4):
                        mm(bQ[:, h * 128:(h + 1) * 128], qkn[:, h, :], qkn[:, 4 + h, :])
                    for h in range(4):
                        stt(N_[:, h, :], bK[:, h * 128:(h + 1) * 128], beta[:, h:h + 1], decL[:, h, :],
                            ALU.mult, ALU.mult)
                    QKm = Tm[2]
                    tt(QKm, v2(bQ, 4), decQ, ALU.mult)
                    b1 = nb()
                    b2 = nb()
                    for h in range(4):
                        tr(b1[:, h * 128:(h + 1) * 128], N_[:, h, :], identF)
                    for h in range(4):
                        tr(b2[:, h * 128:(h + 1) * 128], QKm[:, h, :], identF)
                    cp(M_, v2(b1, 4), eng="act")
                    cp(QKmT, v2(b2, 4), eng="act")
                    tt(Pm, bc(identF.unsqueeze(1), [128, 4, 128]), M_, ALU.subtract)

                def Bs(s):
                    bN = nb()
                    for h in range(4):
                        mm(bN[:, h * 128:(h + 1) * 128], M_[:, h, :], N_[:, h, :])
                    if s < 4:
                        bM = nb()
                        for h in range(4):
                            mm(bM[:, h * 128:(h + 1) * 128], N_[:, h, :], M_[:, h, :])
                    cp(N_, v2(bN, 4), eng="act")
                    if s < 4:
                        cp(M_, v2(bM, 4))
                    bP = nb()
                    for h in range(4):
                        mm(bP[:, h * 128:(h + 1) * 128], N_[:, h, :], Pm[:, h, :])
                    tt(Pm, v2(bP, 4), Pm, ALU.add)
                    if s == 4:
                        tt(ktok, ktok, bc(ekd.unsqueeze(2), [128, 4, 128]), ALU.mult)

                def Bc(c):
                    TT = Pm
                    kdec = ktok
                    r2, vnew, otmp = Tm[0], Tm[1], Tm[2]
                    rs_ = slice(64 * c, 64 * c + 64)
                    bKS = nb()
                    bQS = nb()
                    for h in range(4):
                        mm(bKS[:, h * 128:(h + 1) * 128], qkn[:, 4 + h, :], S_[:, h, :])
                    for h in range(4):
                        mm(bQS[:, h * 128:(h + 1) * 128], qkn[:, h, :], S_[:, h, :])
                    for h in range(4):
                        stt(r2[rs_, h, :], bKS[rs_, h * 128:(h + 1) * 128], eg[rs_, h:h + 1], vtok[rs_, h, :],
                            ALU.mult, ALU.subtract)
                    for h in range(4):
                        ts(r2[rs_, h, :], r2[rs_, h, :], beta[rs_, h:h + 1], -1.0, ALU.mult, ALU.mult)
                    bV = nb()
                    for h in range(4):
                        mm(bV[:, h * 128:(h + 1) * 128], TT[rs_, h, :], r2[rs_, h, :])
                    cp(vnew[rs_], v2(bV, 4)[rs_], eng="act")
                    bO = nb()
                    for h in range(4):
                        mm(bO[:, h * 128:(h + 1) * 128], QKmT[rs_, h, :], vnew[rs_, h, :])
                    cp(otmp[rs_], v2(bO, 4)[rs_], eng="act")
                    for h in range(4):
                        stt(o_[rs_, h, :], bQS[rs_, h * 128:(h + 1) * 128], eg[rs_, h:h + 1], otmp[rs_, h, :],
                            ALU.mult, ALU.add)
                    bS = nb()
                    for h in range(4):
                        mm(bS[:, h * 128:(h + 1) * 128], kdec[rs_, h, :], vnew[rs_, h, :])
                    for h in range(4):
                        stt(S_[:, h, :], S_[:, h, :], egl[:, 2 * h + c:2 * h + c + 1], bS[:, h * 128:(h + 1) * 128],
                            ALU.mult, ALU.add)

                def B10():
                    gated_norm(128, mixv, sz)

                def B11():
                    out_proj(128, mix, xin, None)

                a = [B1, B2] + [(lambda s=s: Bs(s)) for s in range(5)] + [lambda: Bc(0), lambda: Bc(1)]
                return a, [B10, B11]

            def gated_norm(n, mixv, sz):
                sqo = Tm[0]
                tt(sqo[0:n], o_[0:n], o_[0:n], ALU.mult)
                sso = sm(88, 92, n)
                P.op("dve", "reduce_sum", out=sso, in_=sqo[0:n], axis=AX.X)
                ro = sm(92, 96, n)
                rsqrt_mean(ro, sso, 1.0 / 128, n)
                on = Tm[0]
                tt(on[0:n], o_[0:n], bc(ro.unsqueeze(2), [n, 4, 128]), ALU.mult)
                tt(on[0:n], on[0:n], bc(gdnB[0:n, :].unsqueeze(1), [n, 4, 128]), ALU.mult)
                tt(mixv[0:n, 0:4, :], on[0:n], v2(sz[0:n, :], 4), ALU.mult)

            def out_proj(n, mix, xin, gsrc):
                b = nb().bitcast(BF16)
                for k in range(8):
                    tr(b[:, k * n:(k + 1) * n], mix[0:n, k * 128:(k + 1) * 128], identb[0:n, 0:n])
                cp(mixT[:, :, 0:n], v2(b[:, 0:8 * n], 8), eng="act")
                for half in range(2):
                    b = nb()
                    for k in range(8):
                        mm(b[0:n, :], mixT[:, k, 0:n], w_out_sb[:, k, half * 512:(half + 1) * 512],
                           start=(k == 0), stop=(k == 7))
                    hs = slice(half * 512, (half + 1) * 512)
                    if gsrc is None:
                        tt(xin[:, hs], xin[:, hs], b[0:n, :], ALU.add)
                    else:
                        tt(tmpA[0:n, hs], b[0:n, :], gsrc[0:n, hs], ALU.mult)
                        tt(xin[:, hs], xin[:, hs], tmpA[0:n, hs], ALU.add)

            def delta_sample(beta, g_):
                egs = sm(56, 60, 16)
                act(egs, g_, AF.Exp)
                b = nb()
                for h in range(4):
                    tr(b[0:16, h * 128:(h + 1) * 128], qkn[:, h, 0:16], identF)
                qtok = Tm[0]
                cp(qtok[0:16], v2(b[0:16, :], 4), eng="act")
                tt(qtok[0:16], qtok[0:16], ktok[0:16], ALU.mult)
                qk = sm(108, 112, 16)
                P.op("dve", "reduce_sum", out=qk, in_=qtok[0:16], axis=AX.X)
                gdiag = Tm[1].rearrange("p a b -> p (a b)")[0:16, 0:64]
                tt(v2(gdiag, 16), bc(g_.unsqueeze(1), [16, 16, 4]),
                   bc(identF[0:16, 0:16].unsqueeze(2), [16, 16, 4]), ALU.mult)
                b = nb()
                mm(b[:, 0:64], ones[0:16, :], gdiag)
                EGb = sm(128, 192)
                act(EGb, b[:, 0:64], AF.Exp)
                r2, vnew, otmp = Tm[2], S_, u_
                kqm = Tm[1].rearrange("p a b -> p (a b)")
                for h in range(4):
                    SA = v2(FA[:, 0:2048], 16) if h % 2 == 0 else v2(NPQ, 16)
                    P.dma("sp", out=SA, in_=sd[l][:, h].rearrange("b k v -> k b v"))
                    kTm = v2(kqm[:, 0:256], 16)
                    qTm = v2(kqm[:, 256:512], 16)
                    tt(kTm, bc(qkn[:, 4 + h, 0:16].unsqueeze(1), [128, 16, 16]), i16rep, ALU.mult)
                    tt(qTm, bc(qkn[:, h, 0:16].unsqueeze(1), [128, 16, 16]), i16rep, ALU.mult)
                    bKS = nb()
                    bQS = nb()
                    for bb_ in range(16):
                        mm(bKS[0:16, 0:128], kTm[:, bb_, :], SA[:, bb_, :], start=(bb_ == 0), stop=(bb_ == 15))
                    for bb_ in range(16):
                        mm(bQS[0:16, 0:128], qTm[:, bb_, :], SA[:, bb_, :], start=(bb_ == 0), stop=(bb_ == 15))
                    stt(r2[0:16, h, :], bKS[0:16, 0:128], egs[:, h:h + 1], vtok[0:16, h, :], ALU.mult, ALU.subtract)
                    ts(vnew[0:16, h, :], r2[0:16, h, :], beta[:, h:h + 1], -1.0, ALU.mult, ALU.mult)
                    ts(otmp[0:16, h, :], vnew[0:16, h, :], qk[:, h:h + 1], None, ALU.mult)
                    stt(o_[0:16, h, :], bQS[0:16, 0:128], egs[:, h:h + 1], otmp[0:16, h, :], ALU.mult, ALU.add)
                    vmr = Tm[0].rearrange("p a b -> p (a b)")
                    for q4 in range(4):
                        bS = nb()
                        for j in range(4):
                            bb_ = q4 * 4 + j
                            vm = vmr[0:16, j * 128:(j + 1) * 128]
                            ts(vm, vnew[0:16, h, :], identF[0:16, bb_:bb_ + 1], None, ALU.mult)
                            mm(bS[:, j * 128:(j + 1) * 128], ktok[0:16, h, :], vm)
                        for j in range(4):
                            bb_ = q4 * 4 + j
                            stt(SA[:, bb_, :], SA[:, bb_, :], SM[:, 128 + bb_ * 4 + h:128 + bb_ * 4 + h + 1],
                                bS[:, j * 128:(j + 1) * 128], ALU.mult, ALU.add)
                    P.dma("sp", out=ds[l][:, h].rearrange("b k v -> k b v"), in_=SA)

            ckpt("mix%d_params" % l)
            e_, l_ = front(0, True)
            for f in e_ + l_:
                f()
            beta_s, g_s = slots(0, 16)
            delta_sample(beta_s, g_s)
            gated_norm(16, v2(mixb[0], 8), szb[0])
            out_proj(16, mixb[0], Xs[0:16, :], gS)
            ckpt("mix%d_s" % l)
            mset(S_, 0.0)
            make_g(16, FA[:, 1024:2048], gBt)
            tt(w_out_sb, w_out_sb, bc(gBt.unsqueeze(1), [128, 8, 1024]), ALU.mult)
            e_, l_ = front(0, False)
            for f in e_ + l_:
                f()
            poolF = {"l": [0, 1, 2, 3], "i": 0}
            poolB = {"l": [4, 5, 6, 7], "i": 0}
            for t in range(NT):
                ba_, bb_l = back(t)
                if t + 1 < NT:
                    fe, fl = front(t + 1, False)
                else:
                    fe, fl = [], []
                merge_replay(record(fe, poolF), record(ba_, poolB), lead=-0.15)
                merge_replay(record(fl, poolF), record(bb_l, poolB))
                ckpt("mix%d_t%d" % (l, t))
            P.dma("sp", out=dp[l].rearrange("h k v -> k h v"), in_=S_)
            ckpt("mix%d_dp" % l)

        def ffn_phase(l):
            CV.reset()
            NTK = T + NS
            h2T = v2(CV.bf16(8 * NTK), 8)
            wu = [v2(CV.bf16(8 * 512), 8) for _ in range(2)]
            wd = [v2(CV.bf16(2 * 1024), 2) for _ in range(2)]
            xn = CV.bf16(1024)
            tmpA = CV.f32(1024)
            upx = [[[CV.f32(516) for _ in range(2)] for _ in range(2)] for _ in range(2)]
            yb = [[CV.f32(512) for _ in range(2)] for _ in range(2)]
            actT = [[CV.bf16(512) for _ in range(2)] for _ in range(2)]
            tmpD = [CV.f32(1024) for _ in range(2)]
            fcar = v2(CV.f32(88), 2)
            fraw = CV.f32(128)
            sfg = CV.f32(1024)
            fsn = CV.f32(512)
            ptmp = CV.f32(512)
            upxs = [[CV.f32(48) for _ in range(2)] for _ in range(2)]

            gB = CV.f32(1024)
            wm2 = [v2(CV.bf16(8 * 256), 8) for _ in range(2)]
            make_g(40, tmpA, gB)
            b = nb()
            for j in range(3):
                P.dma("sp", out=fraw[0:44, :], in_=conv_ffn_w[l][j].rearrange("(c p) -> c p", p=128))
                mm(b[:, j * 44:(j + 1) * 44], fraw[0:44, :], identF[0:44, 0:44])
            cp(cfwT_t[:, :], b[:, 0:132])
            P.dma("sp", out=fraw[0:44, :], in_=conv_ffn_b[l].rearrange("(c p) -> c p", p=128))
            b = nb()
            mm(b[:, 0:44], fraw[0:44, :], identF[0:44, 0:44])
            cp(cfbT[:, :], b[:, 0:44])
            P.dma("sp", out=fs[l][:, 0, :], in_=sf[l][:, 1, :])

            norm_transpose(Xs[0:16, :], 16, xn, A2T, 24, h2T, tmpA, T)
            for t in range(4):
                norm_transpose(X[:, t, :], 128, xn, A2T, 24, h2T, tmpA, t * 128)
            ckpt("ffn%d_h2T" % l)

            wus = w_up[l].rearrange("(k p) n -> p k n", p=128)
            wds = w_down[l].rearrange("(c p) n -> p c n", p=128)

            def load_w(g):
                s = g % 2
                P.dma("pool", out=wu[s][:, :, 0:256], in_=wus[:, :, g * 256:(g + 1) * 256])
                P.dma("pool", out=wu[s][:, :, 256:512], in_=wus[:, :, DFF + g * 256:DFF + (g + 1) * 256])
                P.dma("pool", out=wd[s], in_=wds[:, 2 * g:2 * g + 2, :])

            load_w(0)
            load_w(1)
            poolU = {"l": [0, 1, 2, 3], "i": 0}
            poolD = {"l": [4, 5, 6, 7], "i": 0}

            def nbp(pl):
                b_ = ps[pl["l"][pl["i"]]]
                pl["i"] = (pl["i"] + 1) % len(pl["l"])
                return b_[:, :]

            units = [(g, blk) for g in range(NG) for blk in (4, 0, 1, 2, 3)]

            def up(ui):
                g, blk = units[ui]
                s = g % 2
                sample = (blk == 4)
                N = 16 if sample else 512
                col = T if sample else blk * 512
                sfv = v3(sfg[0:16, :], 2, 2)
                if sample:
                    ckpt("ffn%d_g%d" % (l, g))
                    P.dma("sp", out=sfv[:, :, 0, :], in_=sf[l][:, :, g * 256:(g + 1) * 256])
                    P.dma("sp", out=sfv[:, :, 1, :], in_=sf[l][:, :, DFF + g * 256:DFF + (g + 1) * 256])
                for pr in range(2):
                    cidx = (2 * g + pr, 22 + 2 * g + pr)
                    bks = (nbp(poolU), nbp(poolU))
                    for gv in range(2):
                        for k in range(8):
                            mm(bks[gv][:, 0:N], wu[s][:, k, gv * 256 + pr * 128:gv * 256 + (pr + 1) * 128],
                               h2T[:, k, col:col + N], start=(k == 0), stop=(k == 7))
                    for gv in range(2):
                        c = cidx[gv]
                        y = yb[gv][pr][:, 0:N]
                        if sample:
                            ux = upxs[gv][pr]
                            b2 = nbp(poolU)
                            for j in range(2):
                                mm(b2[:, j * 16:(j + 1) * 16], sfv[:, j, gv, pr * 128:(pr + 1) * 128],
                                   identF[0:16, 0:16])
                            cp(ux[:, 0:32], b2[:, 0:32], eng="act")
                            cp(ux[:, 32:48], bks[gv][:, 0:16], eng="act")
                            taps = [ux[:, 0:16], ux[:, 16:32], ux[:, 32:48]]
                            ts(y, taps[2], cfwT[:, 2, c:c + 1], cfbT[:, c:c + 1], ALU.mult, ALU.add)
                        else:
                            ux = upx[gv][pr][blk % 2]
                            cp(ux[:, 2:2 + N], bks[gv][:, 0:N], eng="act")
                            act(y, bks[gv][:, 0:N], AF.Identity, scale=cfwT[:, 2, c:c + 1], bias=cfbT[:, c:c + 1])
                            if blk == 0:
                                mset(ux[:, 0:2], 0.0)
                            if blk < 3:
                                cp(upx[gv][pr][(blk + 1) % 2][:, 0:2], ux[:, N:N + 2])
                            else:
                                cp(fcar[:, :, c], ux[:, N:N + 2])
                            taps = [ux[:, 0:N], ux[:, 1:1 + N], ux[:, 2:2 + N]]
                        stt(y, taps[1], cfwT[:, 1, c:c + 1], y, ALU.mult, ALU.add)
                        stt(y, taps[0], cfwT[:, 0, c:c + 1], y, ALU.mult, ALU.add)
                    yg = yb[0][pr][:, 0:N]
                    act(yg, yg, AF.Silu)
                    tt(actT[pr][ui % 2][:, 0:N], yg, yb[1][pr][:, 0:N], ALU.mult)
                if sample:
                    b3 = nbp(poolU)
                    for k in range(8):
                        mm(b3[0:16, :], h2T[:, k, T:T + 16], wu[s][:, k, :], start=(k == 0), stop=(k == 7))
                    cp(fsn[0:16, :], b3[0:16, :])
                    P.dma("sp", out=fs[l][:, 1, g * 256:(g + 1) * 256], in_=fsn[0:16, 0:256])
                    P.dma("sp", out=fs[l][:, 1, DFF + g * 256:DFF + (g + 1) * 256], in_=fsn[0:16, 256:512])

            def down(ui):
                g, blk = units[ui]
                s = g % 2
                sample = (blk == 4)
                ntile = 1 if sample else 4
                n = 16 if sample else 128
                for tt_ in range(ntile):
                    xin = Xs[0:16, :] if sample else X[:, blk * 4 + tt_, :]
                    td = tmpD[tt_ % 2]
                    for half in range(2):
                        bD = nbp(poolD)
                        for pr in range(2):
                            mm(bD[0:n, :], actT[pr][ui % 2][:, tt_ * n:(tt_ + 1) * n],
                               wd[s][:, pr, half * 512:(half + 1) * 512], start=(pr == 0), stop=(pr == 1))
                        hs = slice(half * 512, (half + 1) * 512)
                        if sample:
                            tt(td[0:n, hs], bD[0:n, :], gS[0:n, hs], ALU.mult)
                            tt(xin[:, hs], xin[:, hs], td[0:n, hs], ALU.add)
                        else:
                            tt(xin[:, hs], xin[:, hs], bD[0:n, :], ALU.add)
                if sample:
                    tt(wd[s], wd[s], bc(gB.unsqueeze(1), [128, 2, 1024]), ALU.mult)
                if blk == 3 and g + 2 < NG:
                    load_w(g + 2)

            poolA = {"l": [7], "i": 0}
            poolD["l"] = [4, 5, 6]
            recA = record([(lambda t=t: norm_transpose(X[:, t, :], 128, xn, A2T, 24, h2T, tmpA, t * 128))
                           for t in range(4, NT)], poolA)
            recB = record([lambda: up(0), lambda: up(1), lambda: down(0), lambda: up(2), lambda: down(1)], None)
            merge_replay(recA, recB)
            poolD["l"] = [4, 5, 6, 7]
            poolD["i"] = 0
            def run_units(a_, b_):
                for ui in range(a_, b_):
                    if ui + 1 < len(units):
                        up(ui + 1)
                    down(ui)

            run_units(2, 5)
            if l + 1 < L:
                poolU["l"] = [0, 1, 2]
                poolU["i"] = 0
                recM = record([lambda: mod_ops(l + 1, wm2, fraw, 256)], {"l": [3], "i": 0})
                recB = record([lambda: run_units(5, 25)], None)
                merge_replay(recM, recB)
                poolU["l"] = [0, 1, 2, 3]
                poolU["i"] = 0
                run_units(25, len(units))
            else:
                run_units(5, len(units))
            b = nb()
            mm(b[0:88, 0:128], fcar.rearrange("p a b -> p (a b)"), identF)
            cp(tmpA[0:88, 0:128], b[0:88, 0:128])
            P.dma("sp", out=fp[l].rearrange("j (c p) -> (j c) p", p=128), in_=tmpA[0:88, 0:128])

        stopped = False
        try:
            ckpt("setup")
            mod_phase(0)
            ckpt("mod0")
            for l in range(L):
                mixer_phase2(l)
                ckpt("mix%d" % l)
                ffn_phase(l)
                ckpt("ffn%d" % l)
        except StopBuild:
            stopped = True
            PT = {"modT": modT_t[:, :], "X0": X[:, 0, :], "X1": X[:, 1, :], "X15": X[:, 15, :], "Xs": Xs[0:16, :], "SM": SM[:, :],
                  "gS": gS[0:16, :], "A1T": A1T_t[:, :], "A2T": A2T_t[:, :], "cwT": cwT[:, :], "bsT": bsT[:, :],
                  "nT1": nT1[:, :], "bmT": bmT[:, :], "cfwT": cfwT_t[:, :], "cfbT": cfbT[:, :], "cT": cT_t[:, :]}
            for nm in dump_names:
                if nm in PT:
                    dump(nm, PT[nm])

        CV.reset()
        fnB = CV.f32(1024)
        junk = CV.f32(1024)
        yo = [CV.f32(1024) for _ in range(2)]
        P.dma("sp", out=fnB, in_=final_norm.partition_broadcast(128))
        for t in range(NT + 1):
            sample = (t == NT)
            n = 16 if sample else 128
            xin = Xs[0:16, :] if sample else X[:, t, :]
            ss = SM[0:n, (2 * (t % 2)):(2 * (t % 2)) + 1]
            rstd = SM[0:n, (2 * (t % 2)) + 1:(2 * (t % 2)) + 2]
            mset(ss, 0.0)
            act(junk[0:n, :], xin, AF.Square, accum_out=ss)
            rsqrt_mean(rstd, ss, 1.0 / D, n)
            y_ = yo[t % 2]
            stt(y_[0:n, :], xin, rstd, fnB[0:n, :], ALU.mult, ALU.mult)
            if sample:
                P.dma("sp", out=ys[:, :], in_=y_[0:16, :])
            else:
                P.dma("sp", out=yp[t * 128:(t + 1) * 128, :], in_=y_[:, :])

        names = P.sem_names()
        semh = {nm: es.enter_context(nc.semaphore(nm)) for nm in names}
        block = es.enter_context(nc.Block())

        def emit(engname, e):
            for (wl, meth, kw, dsem, where) in P.ops[engname]:
                for (k, v) in wl:
                    e.wait_ge(semh[k], v)
                try:
                    ins = getattr(e, meth)(**kw)
                except Exception:
                    print("EMIT FAILED", engname, meth, where, {k_: (v_.shape if _is_ap(v_) else v_) for k_, v_ in kw.items()})
                    raise
                if dsem is None:
                    ins.then_inc(semh[engname], 1)
                else:
                    ins.then_inc(semh[dsem], 16)

        @block.tensor
        def _(e):
            emit("pe", e)

        @block.scalar
        def _(e):
            emit("act", e)

        @block.vector
        def _(e):
            emit("dve", e)

        @block.gpsimd
        def _(e):
            emit("pool", e)

        @block.sync
        def _(e):
            emit("sp", e)
            for nm, v in P.dval.items():
                e.wait_ge(semh[nm], v)
            for en in ("pe", "act", "dve"):
                if P.cnt[en]:
                    e.wait_ge(semh[en], P.cnt[en])
    return nc, P


_CACHE = {}


def kernel(**inputs):
    f = lambda a: np.ascontiguousarray(np.asarray(a, dtype=np.float32))
    inp = {k: f(v) for k, v in inputs.items()}
    if "nc" not in _CACHE:
        _CACHE["nc"] = build_program()[0]
    nc = _CACHE["nc"]
    consts = make_consts()
    shared = {k: inp[k] for k in ("w_mod", "b_mod", "norm1", "w_in", "conv_qkv", "a_log", "dt_bias", "gdn_norm",
                                  "sgu_norm", "w_sgu", "b_sgu", "w_out", "norm2", "w_up", "conv_ffn_w",
                                  "conv_ffn_b", "w_down", "final_norm")}
    in_maps = []
    for i in range(NCORES):
        sl = slice(NS * i, NS * (i + 1))
        m = dict(shared)
        m["consts"] = consts
        m["xp"] = f(inp["x_prompt"][i])
        m["xs"] = f(inp["x_sample"][sl, 0, :])
        m["sd"] = f(inp["state_delta"][:, sl])
        m["sq"] = f(inp["state_qkv_conv"][:, sl])
        m["sf"] = f(inp["state_ffn_conv"][:, sl])
        m["call"] = f(np.concatenate([inp["c_prompt"][i:i + 1], inp["c_sample"][sl]], axis=0))
        in_maps.append(m)
    ncr = _CACHE.get("ncores_dbg", NCORES)
    res = run_bass_kernel_spmd(nc, in_maps[:ncr], core_ids=list(range(ncr)))
    R = list(res.results)
    while len(R) < NCORES:
        R.append({k: np.zeros_like(v) for k, v in R[0].items()})
    _CACHE["dbg"] = np.stack([R[i]["dbg"] for i in range(NCORES)]) if "dbg" in R[0] else None
    y_prompt = np.stack([R[i]["yp"] for i in range(NCORES)], axis=0)
    y_sample = np.concatenate([R[i]["ys"] for i in range(NCORES)], axis=0)[:, None, :]
    delta_p = np.stack([R[i]["dp"] for i in range(NCORES)], axis=1)
    delta_s = np.concatenate([R[i]["ds"] for i in range(NCORES)], axis=1)
    qkv_p = np.stack([R[i]["qp"] for i in range(NCORES)], axis=1)
    qkv_s = np.concatenate([R[i]["qs"] for i in range(NCORES)], axis=1)
    ffn_p = np.stack([R[i]["fp"] for i in range(NCORES)], axis=1)
    ffn_s = np.concatenate([R[i]["fs"] for i in range(NCORES)], axis=1)
    sgu_v = np.concatenate([R[i]["sv"] for i in range(NCORES)], axis=1).reshape(L, NS * NCORES, 1, H, 128)
    outs = (y_prompt, y_sample, delta_p, delta_s, qkv_p, qkv_s, ffn_p, ffn_s, sgu_v)
    return tuple(np.ascontiguousarray(o, dtype=np.float32) for o in outs)
```

```python
import math
import sys
import numpy as np
from contextlib import ExitStack
import concourse.bass as bass
import concourse.mybir as mybir
from concourse.bass_utils import run_bass_kernel_spmd

F32 = mybir.dt.float32
BF16 = mybir.dt.bfloat16
AF = mybir.ActivationFunctionType
ALU = mybir.AluOpType
AX = mybir.AxisListType

D = 1024
T = 2048
NT = 16
L = 4
NS = 16
H = 4
QKV = 1536
IN = 3080
DFF = 2816
FF2 = 5632
NG = 11
EPS = 1e-6
NCORES = 8

C_ID, C_ONE, C_TI, C_TS, C_ML, C_MQ, C_TR = 0, 128, 256, 384, 512, 640, 768
C_CM = 896
C_E0 = 898
C_I16 = 914
NCONST = 914 + 256

ENG = ("pe", "act", "dve", "pool", "sp")
SELF_GAP = 1 << 30


def make_consts():
    c = np.zeros((128, NCONST), np.float32)
    i = np.arange(128)
    same = (i[:, None] // 64) == (i[None, :] // 64)
    c[:, C_ID:C_ID + 128] = np.eye(128)
    c[:, C_ONE:C_ONE + 128] = 1.0
    c[:, C_TI:C_TI + 128] = (same & (i[:, None] <= i[None, :]))
    c[:, C_TS:C_TS + 128] = (same & (i[:, None] > i[None, :]))
    c[:, C_ML:C_ML + 128] = (same & (i[:, None] > i[None, :]))
    c[:, C_MQ:C_MQ + 128] = (same & (i[:, None] >= i[None, :]))
    c[:, C_TR:C_TR + 128] = (i[:, None] >= i[None, :])
    c[:, C_CM + 0] = (i < 64)
    c[:, C_CM + 1] = (i >= 64)
    c[0, C_E0:C_E0 + 16] = 1.0
    c[:, C_I16:C_I16 + 256] = np.eye(16, dtype=np.float32).reshape(1, 256)
    return c


def _is_ap(v):
    return hasattr(v, "tensor") and hasattr(v, "ap") and hasattr(v, "offset")


def _box(ap):
    t = ap.tensor
    s = 2 if ap.dtype == BF16 else 4
    a = ap.ap
    off = int(ap.offset)
    if type(t).__name__.startswith("DRam"):
        ext = sum(st * (c - 1) for st, c in a)
        return (t.name, 0, 1, off * s, (off + ext + 1) * s)
    if type(t).__name__.startswith("PSum"):
        return (t.name, 0, 128, 0, 2048)
    pstep, npart = a[0]
    if pstep == 0:
        p0, fo = 0, off
    else:
        p0 = off // pstep
        fo = off - p0 * pstep
    ext = sum(st * (c - 1) for st, c in a[1:])
    return (t.name, p0, p0 + npart, fo * s, (fo + ext + 1) * s)


def _where():
    f = sys._getframe(2)
    out = []
    while f is not None and len(out) < 4:
        out.append(f.f_lineno)
        f = f.f_back
    return out


class Prog:
    def __init__(self):
        self.ops = {e: [] for e in ENG}
        self.cnt = {e: 0 for e in ENG}
        self.known = {e: {} for e in ENG}
        self.recs = {}
        self.dpool = {"sp": ["dsp%d" % i for i in range(24)], "pool": ["dpl%d" % i for i in range(12)]}
        self.dnext = {"sp": 0, "pool": 0}
        self.dval = {}
        self.rec = None

    def replay(self, item):
        kind, eng, meth, kw = item
        if kind == 0:
            self.op(eng, meth, **kw)
        else:
            self.dma(eng, **kw)

    def _collect(self, kw):
        accs = []
        need = {}
        for k, v in kw.items():
            if _is_ap(v):
                accs.append((_box(v), k in ("out", "accum_out", "ap")))
        for bx, isw in accs:
            name, p0, p1, lo, hi = bx
            for r in self.recs.get(name, ()):
                if r[0] < p1 and p0 < r[1] and r[2] < hi and lo < r[3]:
                    if isw or r[4] == "w":
                        if need.get(r[5], 0) < r[6]:
                            need[r[5]] = r[6]
        return accs, need

    def _commit(self, accs, key, val):
        for bx, isw in accs:
            name, p0, p1, lo, hi = bx
            lst = self.recs.setdefault(name, [])
            if isw:
                lst[:] = [r for r in lst if not (p0 <= r[0] and r[1] <= p1 and lo <= r[2] and r[3] <= hi)]
                lst.append([p0, p1, lo, hi, "w", key, val])
            else:
                for r in lst:
                    if r[4] == "r" and r[5] == key and r[0] == p0 and r[1] == p1 and r[2] == lo and r[3] == hi:
                        r[6] = val
                        break
                else:
                    lst.append([p0, p1, lo, hi, "r", key, val])

    def _filter(self, eng, need):
        wl = []
        kn = self.known[eng]
        for k, v in need.items():
            if k == eng:
                if eng == "pe":
                    continue
                if v <= self.cnt[eng] - SELF_GAP and False:
                    continue
            if kn.get(k, 0) >= v:
                continue
            kn[k] = v
            wl.append((k, v))
        return wl

    def op(self, eng, meth, **kw):
        if self.rec is not None:
            self.rec.append((0, eng, meth, kw))
            return
        accs, need = self._collect(kw)
        wl = self._filter(eng, need)
        self.cnt[eng] += 1
        self.ops[eng].append((wl, meth, kw, None, _where()))
        self._commit(accs, eng, self.cnt[eng])

    def dma(self, eng, **kw):
        if self.rec is not None:
            self.rec.append((1, eng, None, kw))
            return
        accs, need = self._collect(kw)
        pool = self.dpool[eng]
        i = self.dnext[eng]
        self.dnext[eng] = (i + 1) % len(pool)
        sem = pool[i]
        prev = self.dval.get(sem, 0)
        if prev:
            need[sem] = max(need.get(sem, 0), prev)
        wl = self._filter(eng, need)
        v = prev + 16
        self.dval[sem] = v
        self.ops[eng].append((wl, "dma_start", kw, sem, _where()))
        self._commit(accs, sem, v)

    def sem_names(self):
        return [e for e in ENG if e != "sp"] + self.dpool["sp"] + self.dpool["pool"]


def v2(ap, a):
    return ap.rearrange("p (a b) -> p a b", a=a)


def v3(ap, a, b):
    return ap.rearrange("p (a b c) -> p a b c", a=a, b=b)


class Carve:
    def __init__(self, ph, n):
        self.ph = ph
        self.n = n
        self.off = 0

    def reset(self):
        self.off = 0

    def f32(self, n):
        assert self.off + n <= self.n, ("arena overflow", self.off, n, self.n)
        ap = self.ph[:, self.off:self.off + n]
        self.off += n
        return ap

    def bf16(self, n):
        nf = (n + 1) // 2
        assert self.off + nf <= self.n, ("arena overflow", self.off, nf, self.n)
        ap = self.ph[:, self.off:self.off + nf].bitcast(BF16)[:, 0:n]
        self.off += nf
        return ap


class StopBuild(Exception):
    pass


def build_program(stop_at=None, dump_names=()):
    nc = bass.Bass("TRN2", target_bir_lowering=False)
    P = Prog()

    P.marks = []

    def ckpt(name):
        P.marks.append((name, dict(P.cnt)))
        if stop_at is not None and name == stop_at:
            raise StopBuild(name)

    P.dump_info = []
    dcol = [0]

    def dump(name, ap):
        if name not in dump_names:
            return
        if len(ap.shape) > 2:
            ap = ap.rearrange("p a b -> p (a b)") if len(ap.shape) == 3 else ap.rearrange("p a b c -> p (a b c)")
        w = ap.shape[1]
        P.dma("pool" if ap.dtype == BF16 else "sp", out=dbg[0:ap.shape[0], dcol[0]:dcol[0] + w], in_=ap)
        P.dump_info.append((name, dcol[0], ap.shape[0], w))
        dcol[0] += w

    def din(name, shape):
        return nc.dram_tensor(name, list(shape), F32, kind="ExternalInput").ap()

    def dout(name, shape):
        return nc.dram_tensor(name, list(shape), F32, kind="ExternalOutput").ap()

    xp = din("xp", [T, D])
    xs = din("xs", [NS, D])
    sd = din("sd", [L, NS, H, 128, 128])
    sq = din("sq", [L, NS, 3, QKV])
    sf = din("sf", [L, NS, 2, FF2])
    call = din("call", [17, D])
    w_mod = din("w_mod", [L, D, 6 * D])
    b_mod = din("b_mod", [L, 6 * D])
    norm1 = din("norm1", [L, D])
    w_in = din("w_in", [L, D, IN])
    conv_qkv = din("conv_qkv", [L, 4, QKV])
    a_log = din("a_log", [L, H])
    dt_bias = din("dt_bias", [L, H])
    gdn_norm = din("gdn_norm", [L, 128])
    sgu_norm = din("sgu_norm", [L, H, 128])
    w_sgu = din("w_sgu", [L, H, 128, 128])
    b_sgu = din("b_sgu", [L, H, 128])
    w_out = din("w_out", [L, D, D])
    norm2 = din("norm2", [L, D])
    w_up = din("w_up", [L, D, FF2])
    conv_ffn_w = din("conv_ffn_w", [L, 3, FF2])
    conv_ffn_b = din("conv_ffn_b", [L, FF2])
    w_down = din("w_down", [L, DFF, D])
    final_norm = din("final_norm", [D])
    consts = din("consts", [128, NCONST])

    yp = dout("yp", [T, D])
    ys = dout("ys", [NS, D])
    dp = dout("dp", [L, H, 128, 128])
    ds = dout("ds", [L, NS, H, 128, 128])
    qp = dout("qp", [L, 3, QKV])
    qs = dout("qs", [L, NS, 3, QKV])
    fp = dout("fp", [L, 2, FF2])
    fs = dout("fs", [L, NS, 2, FF2])
    sv = dout("sv", [L, NS, 512])
    dbg = dout("dbg", [128, 8192]) if stop_at is not None else None

    es = ExitStack()
    with es:
        def sb(name, shape, dt=F32):
            return es.enter_context(nc.sbuf_tensor(name, list(shape), dt))

        cst = sb("cst", [128, NCONST])
        identb_t = sb("identb", [128, 128], BF16)
        X = sb("X", [128, NT, D])
        Xs = sb("Xs", [128, D])
        cT_t = sb("cT", [128, 8 * 17], BF16)
        nT1 = sb("nT1", [128, 32])
        nT2 = sb("nT2", [128, 32])
        gdnB = sb("gdnB", [128, 128])
        sgnB = sb("sgnB", [128, 512])
        alB = sb("alB", [128, 16])
        dtB = sb("dtB", [128, 16])
        nexpA = sb("nexpA", [128, 16])
        modT_t = sb("modT", [128, 48 * 17])
        bmT = sb("bmT", [128, 48])
        A1T_t = sb("A1T", [128, 8 * 17])
        A2T_t = sb("A2T", [128, 8 * 17])
        gS = sb("gS", [128, D])
        SM = sb("SM", [128, 256])
        cwT = sb("cwT", [128, 48])
        bsT = sb("bsT", [128, 4])
        cfwT_t = sb("cfwT", [128, 3 * 44])
        cfbT = sb("cfbT", [128, 44])
        ps = [es.enter_context(nc.psum_tensor("ps%d" % i, [128, 512], F32)) for i in range(8)]
        rem = int(nc.sbuf_bytes_remaining) if not callable(nc.sbuf_bytes_remaining) else int(nc.sbuf_bytes_remaining())
        print('sbuf remaining', rem)
        if rem > 229376:
            rem = rem // 128
        NPH = (rem - 1024) // 4
        PH = sb("PH", [128, NPH])
        CV = Carve(PH, NPH)

        identF = cst[:, C_ID:C_ID + 128]
        ones = cst[:, C_ONE:C_ONE + 128]
        TRIinc = cst[:, C_TI:C_TI + 128]
        TRIsu = cst[:, C_TS:C_TS + 128]
        maskL = cst[:, C_ML:C_ML + 128]
        maskQ = cst[:, C_MQ:C_MQ + 128]
        tril = cst[:, C_TR:C_TR + 128]
        cm = cst[:, C_CM:C_CM + 2]
        e0 = cst[:, C_E0:C_E0 + 16]
        i16rep = v2(cst[:, C_I16:C_I16 + 256], 16)
        identb = identb_t[:, :]
        cT = v2(cT_t[:, :], 8)
        modT = v2(modT_t[:, :], 48)
        A1T = v2(A1T_t[:, :], 8)
        A2T = v2(A2T_t[:, :], 8)
        cfwT = v2(cfwT_t[:, :], 3)

        bank_i = [0]
        cur_pool = [None]

        def nb():
            pl = cur_pool[0]
            if pl is not None:
                b = ps[pl["l"][pl["i"]]]
                pl["i"] = (pl["i"] + 1) % len(pl["l"])
                return b[:, :]
            b = ps[bank_i[0]]
            bank_i[0] = (bank_i[0] + 1) % 8
            return b[:, :]

        def record(fns, pool):
            assert P.rec is None
            P.rec = []
            cur_pool[0] = pool
            for f in fns:
                f()
            out = P.rec
            P.rec = None
            cur_pool[0] = None
            return out

        def merge_replay(A, B, lead=0.0):
            na, nb_ = len(A), len(B)
            ia = ib = 0
            while ia < na or ib < nb_:
                if ib >= nb_ or (ia < na and ia * nb_ <= (ib + lead * nb_) * na):
                    P.replay(A[ia])
                    ia += 1
                else:
                    P.replay(B[ib])
                    ib += 1

        def mm(out, lhsT, rhs, start=True, stop=True):
            P.op("pe", "matmul", out=out, lhsT=lhsT, rhs=rhs, start=start, stop=stop)

        def tr(out, in_, identity):
            P.op("pe", "transpose", out=out, in_=in_, identity=identity)

        def act(out, in_, func, **kw):
            P.op("act", "activation", out=out, in_=in_, func=func, **kw)

        def tt(out, in0, in1, op, eng="dve"):
            P.op(eng, "tensor_tensor", out=out, in0=in0, in1=in1, op=op)

        def ts(out, in0, s1, s2, op0, op1=None, eng="dve"):
            if op1 is None:
                P.op(eng, "tensor_scalar", out=out, in0=in0, scalar1=s1, scalar2=None, op0=op0)
            else:
                P.op(eng, "tensor_scalar", out=out, in0=in0, scalar1=s1, scalar2=s2, op0=op0, op1=op1)

        def stt(out, in0, scalar, in1, op0, op1, eng="dve"):
            P.op(eng, "scalar_tensor_tensor", out=out, in0=in0, scalar=scalar, in1=in1, op0=op0, op1=op1)

        def cp(out, in_, eng="dve"):
            if eng == "act":
                P.op("act", "activation", out=out, in_=in_, func=AF.Copy)
            else:
                P.op(eng, "tensor_copy", out=out, in_=in_)

        def mset(ap, val, eng="dve"):
            P.op(eng, "memset", ap=ap, constant=val)

        def rsqrt_mean(out, ss, scale, n):
            act(out, ss, AF.Ln, scale=scale, bias=EPS)
            act(out, out, AF.Exp, scale=-0.5)

        def bc(ap, shape):
            return ap.to_broadcast(list(shape))

        P.dma("sp", out=cst[:, :], in_=consts[:, :])
        for i in range(4):
            P.dma("sp", out=X[:, 4 * i:4 * i + 4, :],
                  in_=xp.rearrange("(t p) d -> p t d", p=128)[:, 4 * i:4 * i + 4, :])
        P.dma("sp", out=Xs[0:16, :], in_=xs[:, :])
        cp(identb, identF)
        P.dma("sp", out=alB[:, :], in_=a_log.rearrange("l h -> (l h)").partition_broadcast(128))
        P.dma("sp", out=dtB[:, :], in_=dt_bias.rearrange("l h -> (l h)").partition_broadcast(128))
        act(nexpA[:, :], alB[:, :], AF.Exp)
        ts(nexpA[:, :], nexpA[:, :], -1.0, None, ALU.mult)
        CV.reset()
        craw = CV.f32(1024)
        n1raw = CV.f32(128)
        n2raw = CV.f32(128)
        P.dma("sp", out=craw[0:17, :], in_=call[:, :])
        P.dma("sp", out=n1raw[0:32, :], in_=norm1.rearrange("l (c p) -> (l c) p", p=128))
        P.dma("sp", out=n2raw[0:32, :], in_=norm2.rearrange("l (c p) -> (l c) p", p=128))
        act(craw[0:17, :], craw[0:17, :], AF.Silu)
        b = nb()
        for k in range(8):
            mm(b[:, k * 17:(k + 1) * 17], craw[0:17, k * 128:(k + 1) * 128], identF[0:17, 0:17])
        cp(cT_t[:, :], b[:, 0:136])
        b = nb()
        mm(b[:, 0:32], n1raw[0:32, :], identF[0:32, 0:32])
        mm(b[:, 32:64], n2raw[0:32, :], identF[0:32, 0:32])
        cp(nT1[:, :], b[:, 0:32])
        cp(nT2[:, :], b[:, 32:64])
        dump("n1raw", n1raw[0:32, :])
        dump("nT1a", nT1[:, :])
        dump("nT2a", nT2[:, :])
        dump("craw", craw[0:17, :])

        def sm(a, b_, n=128):
            return SM[0:n, a:b_]

        def mod_phase(l):
            CV.reset()
            wm = [v2(CV.bf16(8 * 512), 8) for _ in range(2)]
            bmraw = CV.f32(128)
            mod_ops(l, wm, bmraw, 512)

        def mod_ops(l, wm, bmraw, ncols):
            sub = ncols // 128
            P.dma("sp", out=bmraw[0:48, :], in_=b_mod[l].rearrange("(c p) -> c p", p=128))
            b = nb()
            mm(b[:, 0:48], bmraw[0:48, :], identF[0:48, 0:48])
            cp(bmT[:, :], b[:, 0:48])
            wsrc = w_mod[l].rearrange("(k p) n -> p k n", p=128)
            for c in range(6 * D // ncols):
                w_ = wm[c % 2]
                P.dma("pool", out=w_, in_=wsrc[:, :, c * ncols:(c + 1) * ncols])
                b = nb()
                for j in range(sub):
                    for k in range(8):
                        mm(b[:, j * 17:(j + 1) * 17], w_[:, k, j * 128:(j + 1) * 128], cT[:, k, :],
                           start=(k == 0), stop=(k == 7))
                tt(modT[:, sub * c:sub * c + sub, :], v2(b[:, 0:17 * sub], sub),
                   bc(bmT[:, sub * c:sub * c + sub].unsqueeze(2), [128, sub, 17]), ALU.add)
            for (AT, c0, nT) in ((A1T, 8, nT1), (A2T, 32, nT2)):
                ts(AT, modT[:, c0:c0 + 8, :], 1.0, None, ALU.add)
                tt(AT, AT, bc(nT[:, l * 8:l * 8 + 8].unsqueeze(2), [128, 8, 17]), ALU.mult)

        def make_g(c0, repbuf, gB):
            rep = v2(repbuf, 8)
            cp(rep, bc(modT[:, c0:c0 + 8, 0:1], [128, 8, 128]))
            for half in range(2):
                b = nb()
                for kk in range(4):
                    k = half * 4 + kk
                    mm(b[:, kk * 128:(kk + 1) * 128], rep[:, k, :], identF)
                cp(gB[:, half * 512:(half + 1) * 512], b, eng="act")
                b = nb()
                for kk in range(4):
                    k = half * 4 + kk
                    mm(b[0:16, kk * 128:(kk + 1) * 128], modT[:, c0 + k, 1:17], identF)
                cp(gS[0:16, half * 512:(half + 1) * 512], b[0:16, :], eng="act")

        def norm_transpose(xin, n, xn, AT, shc0, dstT, tmpA, col):
            ss = sm(0, 1, n)
            rstd = sm(1, 2, n)
            mset(ss, 0.0)
            act(tmpA[0:n, :], xin, AF.Square, accum_out=ss)
            rsqrt_mean(rstd, ss, 1.0 / D, n)
            ts(xn[0:n, :], xin, rstd, None, ALU.mult)
            b = nb().bitcast(BF16)
            for k in range(8):
                tr(b[:, k * n:(k + 1) * n], xn[0:n, k * 128:(k + 1) * 128], identb[0:n, 0:n])
            bv = v2(b[:, 0:8 * n], 8)
            tv = v2(tmpA[:, 0:8 * n], 8)
            if n == 128:
                Aap = bc(AT[:, :, 0:1], [128, 8, 128])
                Sap = bc(modT[:, shc0:shc0 + 8, 0:1], [128, 8, 128])
            else:
                Aap = AT[:, :, 1:17]
                Sap = modT[:, shc0:shc0 + 8, 1:17]
            tt(tv, bv, Aap, ALU.mult)
            tt(dstT[:, :, col:col + n], tv, Sap, ALU.add)

        def mixer_phase(l):
            CV.reset()
            w_in_sb = v2(CV.bf16(8 * IN), 8)
            w_out_sb = v2(CV.bf16(8 * D), 8)
            xn = CV.bf16(1024)
            hT = v2(CV.bf16(1024), 8)
            qkvx = CV.f32(524)
            qcar = v2(CV.f32(36), 12)
            qkn = v2(CV.f32(1024), 8)
            vtok = v2(CV.f32(512), 4)
            ktok = v2(CV.f32(512), 4)
            sz = CV.f32(512)
            u_ = v2(CV.f32(512), 4)
            TmAll = CV.f32(2048)
            Tm = [v2(TmAll[:, 512 * i:512 * (i + 1)], 4) for i in range(4)]
            tmpA = TmAll[:, 0:1024]
            NMall = CV.f32(2048)
            NMr = [NMall[:, 0:1024], NMall[:, 1024:2048]]
            acc = v2(NMall[:, 0:1536], 12)
            PQ = CV.f32(1536)
            stq_buf = PQ
            PPr = [v2(PQ[:, 0:512], 4), v2(PQ[:, 512:1024], 4)]
            QKmT = v2(PQ[:, 1024:1536], 4)
            print("mixer arena used", CV.off, "of", CV.n)
            o_ = v2(CV.f32(512), 4)
            S_ = v2(CV.f32(512), 4)
            WmT = v2(CV.f32(512), 4)
            mix = xn
            mixT = hT

            wsrc = w_in[l].rearrange("(k p) n -> p k n", p=128)
            for k in range(8):
                P.dma("pool", out=w_in_sb[:, k, :], in_=wsrc[:, k, :])
            wsrc2 = w_out[l].rearrange("(k p) n -> p k n", p=128)
            for k2 in range(2):
                P.dma("pool", out=w_out_sb[:, 4 * k2:4 * k2 + 4, :], in_=wsrc2[:, 4 * k2:4 * k2 + 4, :])

            make_g(16, tmpA, None)
            craw = Tm[0]
            cr = craw.rearrange("p a b -> p (a b)")
            P.dma("sp", out=cr[0:48, 0:128], in_=conv_qkv[l].rearrange("j (c p) -> (j c) p", p=128))
            P.dma("sp", out=cr[0:4, 128:256], in_=b_sgu[l])
            b = nb()
            mm(b[:, 0:48], cr[0:48, 0:128], identF[0:48, 0:48])
            mm(b[:, 64:68], cr[0:4, 128:256], identF[0:4, 0:4])
            cp(cwT[:, :], b[:, 0:48])
            cp(bsT[:, :], b[:, 64:68])
            P.dma("sp", out=sgnB[:, :], in_=sgu_norm[l].rearrange("h d -> (h d)").partition_broadcast(128))
            P.dma("sp", out=gdnB[:, :], in_=gdn_norm[l].partition_broadcast(128))
            mset(qcar, 0.0)
            Wraw = Tm[1]
            P.dma("sp", out=Wraw, in_=w_sgu[l].rearrange("h i j -> i h j"))
            w00 = sm(100, 104, 16)
            b00 = sm(104, 108, 16)
            b = nb()
            mm(b[0:16, 0:4], e0, Wraw[:, :, 0])
            mm(b[0:16, 4:8], e0, bsT[:, :])
            cp(SM[0:16, 100:108], b[0:16, 0:8])
            tt(Wraw, Wraw, bc(tril.unsqueeze(1), [128, 4, 128]), ALU.mult)
            b = nb()
            for h in range(4):
                tr(b[:, h * 128:(h + 1) * 128], Wraw[:, h, :], identF)
            cp(WmT, v2(b, 4))
            mset(S_, 0.0)

            nA = nexpA[:, l * 4:l * 4 + 4]
            dB = dtB[:, l * 4:l * 4 + 4]

            def tile(t, sample):
                n = 16 if sample else 128
                xin = Xs[0:16, :] if sample else X[:, t, :]
                tg_ = ("stile_" if sample else "tile_")

                def tck(x):
                    if l == 0 and t == 0:
                        ckpt(tg_ + x)
                norm_transpose(xin, n, xn, A1T, 0, hT, tmpA, 0)
                dump("hT", hT.rearrange("p a b -> p (a b)"))
                tck("a")
                if sample:
                    QX = v2(qkvx[:, 0:256], 4)

                    def tap(j):
                        return QX[:, :, 16 * j:16 * j + 16]
                    cur = QX[:, :, 48:64]
                    P.dma("sp", out=qs[l][:, 0:2, :], in_=sq[l][:, 1:3, :])
                else:
                    QX = v2(qkvx[:, 0:524], 4)

                    def tap(j):
                        return QX[:, :, j:j + 128]
                    cur = QX[:, :, 3:131]
                for grp in range(3):
                    c4 = slice(4 * grp, 4 * grp + 4)
                    b = nb()
                    for cc in range(4):
                        c = grp * 4 + cc
                        for k in range(8):
                            mm(b[:, cc * n:(cc + 1) * n], w_in_sb[:, k, c * 128:(c + 1) * 128], hT[:, k, 0:n],
                               start=(k == 0), stop=(k == 7))
                    cp(cur, v2(b[:, 0:4 * n], 4), eng="act")
                    if sample:
                        stq = v2(stq_buf[0:16, :], 3)
                        P.dma("sp", out=stq, in_=sq[l][:, :, grp * 512:(grp + 1) * 512])
                        b2 = nb()
                        for cc in range(4):
                            for j in range(3):
                                idx = cc * 3 + j
                                mm(b2[:, idx * 16:(idx + 1) * 16], stq[:, j, cc * 128:(cc + 1) * 128],
                                   identF[0:16, 0:16])
                        cp(QX[:, :, 0:48], v2(b2[:, 0:192], 4), eng="act")
                        b3 = nb()
                        for k in range(8):
                            mm(b3[0:16, :], hT[:, k, 0:16], w_in_sb[:, k, grp * 512:(grp + 1) * 512],
                               start=(k == 0), stop=(k == 7))
                        qsn = Tm[2].rearrange("p a b -> p (a b)")
                        cp(qsn[0:16, :], b3[0:16, :])
                        P.dma("sp", out=qs[l][:, 2, grp * 512:(grp + 1) * 512], in_=qsn[0:16, :])
                    else:
                        cp(QX[:, :, 0:3], qcar[:, c4, :])
                    ag = acc[:, c4, 0:n]
                    tg = Tm[3][:, :, 0:n]

                    def cw(j):
                        return bc(cwT[:, j * 12 + 4 * grp:j * 12 + 4 * grp + 4].unsqueeze(2), [128, 4, n])
                    tt(ag, tap(3), cw(3), ALU.mult)
                    for j in (2, 1, 0):
                        tt(tg, tap(j), cw(j), ALU.mult)
                        tt(ag, ag, tg, ALU.add)
                    if not sample:
                        cp(qcar[:, c4, :], QX[:, :, 128:131])
                if (not sample) and t == NT - 1:
                    b = nb()
                    for j in range(3):
                        mm(b[0:12, j * 128:(j + 1) * 128], qcar[:, :, j], identF)
                    qtail = Tm[2].rearrange("p a b -> p (a b)")
                    cp(qtail[0:12, 0:384], b[0:12, 0:384])
                    P.dma("sp", out=qp[l].rearrange("j (c p) -> c j p", p=128), in_=v2(qtail[0:12, 0:384], 3))
                QC = acc[:, :, 0:n]
                act(QC, QC, AF.Silu)
                dump("QC", acc.rearrange("p a b -> p (a b)"))
                tck("b")
                sqb = v2(tmpA[:, 0:8 * n], 8)
                tt(sqb, QC[:, 0:8, :], QC[:, 0:8, :], ALU.mult)
                for half in range(2):
                    b = nb()
                    mm(b[:, 0:4 * n], ones, sqb[:, 4 * half:4 * half + 4, :])
                    rsv = sqb[:, 4 * half:4 * half + 4, :]
                    act(rsv, v2(b[:, 0:4 * n], 4), AF.Ln, bias=EPS)
                    act(rsv, rsv, AF.Exp, scale=-0.5, bias=(-0.5 * math.log(128.0) if half == 0 else 0.0))
                tt(qkn[:, :, 0:n], QC[:, 0:8, :], sqb, ALU.mult)
                dump("qkn", qkn.rearrange("p a b -> p (a b)"))
                tck("c")
                b = nb()
                for h in range(4):
                    tr(b[0:n, h * 128:(h + 1) * 128], QC[:, 8 + h, :], identF)
                cp(vtok[0:n], v2(b[0:n, :], 4), eng="act")
                b = nb()
                for h in range(4):
                    tr(b[0:n, h * 128:(h + 1) * 128], qkn[:, 4 + h, 0:n], identF)
                cp(ktok[0:n], v2(b[0:n, :], 4), eng="act")
                bz, bu, bv_, bb = nb(), nb(), nb(), nb()
                for (bk, c0, nn) in ((bz, 1536, 512), (bb, 2048, 8), (bu, 2056, 512), (bv_, 2568, 512)):
                    for k in range(8):
                        mm(bk[0:n, 0:nn], hT[:, k, 0:n], w_in_sb[:, k, c0:c0 + nn], start=(k == 0), stop=(k == 7))
                act(sz[0:n, :], bz[0:n, :], AF.Silu)
                uf = u_.rearrange("p a b -> p (a b)")
                act(uf[0:n, :], bu[0:n, :], AF.Gelu_apprx_tanh)
                vg = Tm[0]
                vgf = vg.rearrange("p a b -> p (a b)")
                act(vgf[0:n, :], bv_[0:n, :], AF.Gelu_apprx_tanh)
                ba = sm(8, 16, n)
                cp(ba, bb[0:n, 0:8])
                beta = sm(16, 20, n)
                act(beta, SM[0:n, 8:12], AF.Sigmoid)
                xa, ab_, ee, mx, g_ = sm(20, 24, n), sm(24, 28, n), sm(28, 32, n), sm(32, 36, n), sm(36, 40, n)
                tt(xa, SM[0:n, 12:16], dB[0:n, :], ALU.add)
                ts(ab_, xa, -1.0, None, ALU.mult)
                tt(ab_, ab_, xa, ALU.max)
                act(ee, ab_, AF.Exp, scale=-1.0)
                act(ee, ee, AF.Ln, bias=1.0)
                ts(mx, xa, 0.0, None, ALU.max)
                tt(mx, mx, ee, ALU.add)
                tt(g_, mx, nA[0:n, :], ALU.mult)
                dump("vtok", vtok.rearrange("p a b -> p (a b)"))
                dump("ktok", ktok.rearrange("p a b -> p (a b)"))
                dump("sz", sz)
                dump("u", u_.rearrange("p a b -> p (a b)"))
                dump("SMd", SM[:, :])
                tck("d")
                sqv = Tm[1]
                tt(sqv[0:n], vg[0:n], vg[0:n], ALU.mult)
                ssv = sm(80, 84, n)
                P.op("dve", "reduce_sum", out=ssv, in_=sqv[0:n], axis=AX.X)
                tck("e1")
                rv = sm(84, 88, n)
                rsqrt_mean(rv, ssv, 1.0 / 128, n)
                vv = Tm[1]
                tt(vv[0:n], vg[0:n], bc(rv.unsqueeze(2), [n, 4, 128]), ALU.mult)
                tt(vv[0:n], vv[0:n], v2(sgnB[0:n, :], 4), ALU.mult)
                tck("e2")
                mixv = v2(mix, 8)
                if sample:
                    P.dma("sp", out=sv[l], in_=vv[0:16].rearrange("p a b -> p (a b)"))
                    zt = Tm[2]
                    tt(zt[0:16], vv[0:16], bc(w00.unsqueeze(2), [16, 4, 128]), ALU.mult)
                    tt(zt[0:16], zt[0:16], bc(b00.unsqueeze(2), [16, 4, 128]), ALU.add)
                    tt(mixv[0:16, 4:8, :], zt[0:16], u_[0:16], ALU.mult)
                else:
                    b = nb()
                    for h in range(4):
                        mm(b[:, h * 128:(h + 1) * 128], WmT[:, h, :], vv[:, h, :])
                    tck("e3")
                    for h in range(4):
                        stt(mixv[:, 4 + h, :], b[:, h * 128:(h + 1) * 128], bsT[:, h:h + 1], u_[:, h, :],
                            ALU.add, ALU.mult)
                dump("vv", Tm[1].rearrange("p a b -> p (a b)"))
                tck("e")
                if sample:
                    delta_sample(l, g_, beta)
                else:
                    delta_prompt(g_, beta)
                dump("o", o_.rearrange("p a b -> p (a b)"))
                dump("S", S_.rearrange("p a b -> p (a b)"))
                tck("f")
                sqo = Tm[0]
                tt(sqo[0:n], o_[0:n], o_[0:n], ALU.mult)
                sso = sm(88, 92, n)
                P.op("dve", "reduce_sum", out=sso, in_=sqo[0:n], axis=AX.X)
                ro = sm(92, 96, n)
                rsqrt_mean(ro, sso, 1.0 / 128, n)
                on = Tm[0]
                tt(on[0:n], o_[0:n], bc(ro.unsqueeze(2), [n, 4, 128]), ALU.mult)
                tt(on[0:n], on[0:n], bc(gdnB[0:n, :].unsqueeze(1), [n, 4, 128]), ALU.mult)
                tt(mixv[0:n, 0:4, :], on[0:n], v2(sz[0:n, :], 4), ALU.mult)
                dump("mix", mix)
                tck("g")
                b = nb().bitcast(BF16)
                for k in range(8):
                    tr(b[:, k * n:(k + 1) * n], mix[0:n, k * 128:(k + 1) * 128], identb[0:n, 0:n])
                cp(mixT[:, :, 0:n], v2(b[:, 0:8 * n], 8), eng="act")
                gsrc = gS if sample else gB
                for half in range(2):
                    b = nb()
                    for k in range(8):
                        mm(b[0:n, :], mixT[:, k, 0:n], w_out_sb[:, k, half * 512:(half + 1) * 512],
                           start=(k == 0), stop=(k == 7))
                    hs = slice(half * 512, (half + 1) * 512)
                    tt(tmpA[0:n, hs], b[0:n, :], gsrc[0:n, hs], ALU.mult)
                    tt(xin[:, hs], xin[:, hs], tmpA[0:n, hs], ALU.add)

            def delta_prompt(g_, beta):
                gc = sm(40, 56)
                ex = sm(56, 72)
                gsel = sm(72, 80)
                bg = nb()
                mm(bg[:, 0:4], TRIinc, g_)
                mm(bg[:, 4:8], TRIsu, g_)
                tt(v2(gsel, 4), bc(g_.unsqueeze(2), [128, 4, 2]), bc(cm.unsqueeze(1), [128, 4, 2]), ALU.mult)
                mm(bg[:, 8:16], ones, gsel)
                cp(gc, bg[:, 0:16])
                act(ex, gc, AF.Exp)
                gcum = SM[:, 40:44]
                eg = SM[:, 56:60]
                ekd = SM[:, 60:64]
                egl = SM[:, 64:72]
                gBk = Tm[2]
                cp(gBk, bc(g_.unsqueeze(2), [128, 4, 128]))
                bG = nb()
                for h in range(4):
                    mm(bG[:, h * 128:(h + 1) * 128], gBk[:, h, :], TRIinc)
                xd = Tm[3]
                for h in range(4):
                    ts(xd[:, h, :], bG[:, h * 128:(h + 1) * 128], gcum[:, h:h + 1], 0.0, ALU.subtract, ALU.max)
                act(xd, xd, AF.Exp, scale=-1.0)
                decL = Tm[2]
                decQ = o_
                tt(decL, xd, bc(maskL.unsqueeze(1), [128, 4, 128]), ALU.mult)
                tt(decQ, xd, bc(maskQ.unsqueeze(1), [128, 4, 128]), ALU.mult)
                bK = nb()
                bQ = nb()
                for h in range(4):
                    mm(bK[:, h * 128:(h + 1) * 128], qkn[:, 4 + h, :], qkn[:, 4 + h, :])
                for h in range(4):
                    mm(bQ[:, h * 128:(h + 1) * 128], qkn[:, h, :], qkn[:, 4 + h, :])
                NM = [v3(r, 2, 4) for r in NMr]
                N1 = NM[0][:, 0]
                M1 = NM[0][:, 1]
                for h in range(4):
                    stt(N1[:, h, :], bK[:, h * 128:(h + 1) * 128], beta[:, h:h + 1], decL[:, h, :], ALU.mult, ALU.mult)
                QKm = Tm[3]
                tt(QKm, v2(bQ, 4), decQ, ALU.mult)
                b1 = nb()
                b2 = nb()
                for h in range(4):
                    tr(b1[:, h * 128:(h + 1) * 128], N1[:, h, :], identF)
                for h in range(4):
                    tr(b2[:, h * 128:(h + 1) * 128], QKm[:, h, :], identF)
                cp(M1, v2(b1, 4), eng="act")
                cp(QKmT, v2(b2, 4), eng="act")
                Pc, Pn = PPr[0], PPr[1]
                tt(Pc, bc(identF.unsqueeze(1), [128, 4, 128]), M1, ALU.subtract)
                cur = 0
                for s in range(5):
                    Nc, Mc = NM[cur][:, 0], NM[cur][:, 1]
                    Nn, Mn = NM[1 - cur][:, 0], NM[1 - cur][:, 1]
                    bN = nb()
                    for h in range(4):
                        mm(bN[:, h * 128:(h + 1) * 128], Mc[:, h, :], Nc[:, h, :])
                    if s < 4:
                        bM = nb()
                        for h in range(4):
                            mm(bM[:, h * 128:(h + 1) * 128], Nc[:, h, :], Mc[:, h, :])
                    cp(Nn, v2(bN, 4), eng="act")
                    if s < 4:
                        cp(Mn, v2(bM, 4))
                    bP = nb()
                    for h in range(4):
                        mm(bP[:, h * 128:(h + 1) * 128], Nn[:, h, :], Pc[:, h, :])
                    tt(Pn, v2(bP, 4), Pc, ALU.add)
                    Pc, Pn = Pn, Pc
                    cur = 1 - cur
                TT = Pc
                kdec = ktok
                tt(kdec, ktok, bc(ekd.unsqueeze(2), [128, 4, 128]), ALU.mult)
                r2, rhs2, vnew, otmp = Tm[0], Tm[1], Tm[2], Tm[3]
                for c in range(2):
                    rs_ = slice(64 * c, 64 * c + 64)
                    bKS = nb()
                    bQS = nb()
                    for h in range(4):
                        mm(bKS[:, h * 128:(h + 1) * 128], qkn[:, 4 + h, :], S_[:, h, :])
                    for h in range(4):
                        mm(bQS[:, h * 128:(h + 1) * 128], qkn[:, h, :], S_[:, h, :])
                    for h in range(4):
                        stt(r2[rs_, h, :], bKS[rs_, h * 128:(h + 1) * 128], eg[rs_, h:h + 1], vtok[rs_, h, :],
                            ALU.mult, ALU.subtract)
                    for h in range(4):
                        ts(rhs2[rs_, h, :], r2[rs_, h, :], beta[rs_, h:h + 1], -1.0, ALU.mult, ALU.mult)
                    bV = nb()
                    for h in range(4):
                        mm(bV[:, h * 128:(h + 1) * 128], TT[rs_, h, :], rhs2[rs_, h, :])
                    cp(vnew[rs_], v2(bV, 4)[rs_], eng="act")
                    bO = nb()
                    for h in range(4):
                        mm(bO[:, h * 128:(h + 1) * 128], QKmT[rs_, h, :], vnew[rs_, h, :])
                    cp(otmp[rs_], v2(bO, 4)[rs_], eng="act")
                    for h in range(4):
                        stt(o_[rs_, h, :], bQS[rs_, h * 128:(h + 1) * 128], eg[rs_, h:h + 1], otmp[rs_, h, :],
                            ALU.mult, ALU.add)
                    bS = nb()
                    for h in range(4):
                        mm(bS[:, h * 128:(h + 1) * 128], kdec[rs_, h, :], vnew[rs_, h, :])
                    for h in range(4):
                        stt(S_[:, h, :], S_[:, h, :], egl[:, 2 * h + c:2 * h + c + 1], bS[:, h * 128:(h + 1) * 128],
                            ALU.mult, ALU.add)

            def delta_sample(l, g_, beta):
                egs = sm(56, 60, 16)
                act(egs, g_, AF.Exp)
                b = nb()
                for h in range(4):
                    tr(b[0:16, h * 128:(h + 1) * 128], qkn[:, h, 0:16], identF)
                qtok = Tm[0]
                cp(qtok[0:16], v2(b[0:16, :], 4), eng="act")
                tt(qtok[0:16], qtok[0:16], ktok[0:16], ALU.mult)
                qk = sm(108, 112, 16)
                P.op("dve", "reduce_sum", out=qk, in_=qtok[0:16], axis=AX.X)
                gdiag = Tm[1].rearrange("p a b -> p (a b)")[0:16, 0:64]
                tt(v2(gdiag, 16), bc(g_.unsqueeze(1), [16, 16, 4]),
                   bc(identF[0:16, 0:16].unsqueeze(2), [16, 16, 4]), ALU.mult)
                b = nb()
                mm(b[:, 0:64], ones[0:16, :], gdiag)
                EGb = sm(128, 192)
                act(EGb, b[:, 0:64], AF.Exp)
                r2, vnew, otmp = Tm[2], Tm[3], PPr[0]
                kqm = Tm[1].rearrange("p a b -> p (a b)")
                for h in range(4):
                    SA = v2(NMall, 16)
                    P.dma("sp", out=SA, in_=sd[l][:, h].rearrange("b k v -> k b v"))
                    kTm = v2(kqm[:, 0:256], 16)
                    qTm = v2(kqm[:, 256:512], 16)
                    tt(kTm, bc(qkn[:, 4 + h, 0:16].unsqueeze(1), [128, 16, 16]), i16rep, ALU.mult)
                    tt(qTm, bc(qkn[:, h, 0:16].unsqueeze(1), [128, 16, 16]), i16rep, ALU.mult)
                    bKS = nb()
                    bQS = nb()
                    for bb_ in range(16):
                        mm(bKS[0:16, 0:128], kTm[:, bb_, :], SA[:, bb_, :], start=(bb_ == 0), stop=(bb_ == 15))
                    for bb_ in range(16):
                        mm(bQS[0:16, 0:128], qTm[:, bb_, :], SA[:, bb_, :], start=(bb_ == 0), stop=(bb_ == 15))
                    stt(r2[0:16, h, :], bKS[0:16, 0:128], egs[:, h:h + 1], vtok[0:16, h, :], ALU.mult, ALU.subtract)
                    ts(vnew[0:16, h, :], r2[0:16, h, :], beta[:, h:h + 1], -1.0, ALU.mult, ALU.mult)
                    ts(otmp[0:16, h, :], vnew[0:16, h, :], qk[:, h:h + 1], None, ALU.mult)
                    stt(o_[0:16, h, :], bQS[0:16, 0:128], egs[:, h:h + 1], otmp[0:16, h, :], ALU.mult, ALU.add)
                    vmr = Tm[0].rearrange("p a b -> p (a b)")
                    for q4 in range(4):
                        bS = nb()
                        for j in range(4):
                            bb_ = q4 * 4 + j
                            vm = vmr[0:16, j * 128:(j + 1) * 128]
                            ts(vm, vnew[0:16, h, :], identF[0:16, bb_:bb_ + 1], None, ALU.mult)
                            mm(bS[:, j * 128:(j + 1) * 128], ktok[0:16, h, :], vm)
                        for j in range(4):
                            bb_ = q4 * 4 + j
                            stt(SA[:, bb_, :], SA[:, bb_, :], SM[:, 128 + bb_ * 4 + h:128 + bb_ * 4 + h + 1],
                                bS[:, j * 128:(j + 1) * 128], ALU.mult, ALU.add)
                    P.dma("sp", out=ds[l][:, h].rearrange("b k v -> k b v"), in_=SA)

            ckpt("mix%d_params" % l)
            for t in range(NT):
                tile(t, False)
                ckpt("mix%d_t%d" % (l, t))
            P.dma("sp", out=dp[l].rearrange("h k v -> k h v"), in_=S_)
            ckpt("mix%d_dp" % l)
            tile(0, True)

        def mixer_phase2(l):
            CV.reset()
            WBLK = [(0, 512), (512, 512), (1024, 512), (1536, 512), (2048, 8), (2056, 512), (2568, 512)]
            w_in_blk = [v2(CV.bf16(8 * wn), 8) for (_, wn) in WBLK]

            def wcol(k, c0, n_):
                for bi, (b0, wn) in enumerate(WBLK):
                    if b0 <= c0 and c0 + n_ <= b0 + wn:
                        return w_in_blk[bi][:, k, c0 - b0:c0 - b0 + n_]
                raise AssertionError((c0, n_))
            w_out_sb = v2(CV.bf16(8 * D), 8)
            xn = CV.bf16(1024)
            hT = v2(CV.bf16(1024), 8)
            mixT = v2(xn, 8)
            qkvx = CV.f32(524)
            qcar = v2(CV.f32(36), 12)
            FA = CV.f32(2560)
            acc = v2(FA[:, 0:1536], 12)
            FT = FA[:, 1536:2560]
            qkn = v2(CV.f32(1024), 8)
            vtok = v2(CV.f32(512), 4)
            ktok = v2(CV.f32(512), 4)
            szb = [CV.f32(512) for _ in range(2)]
            u_ = v2(CV.f32(512), 4)
            mixb = [CV.bf16(1024) for _ in range(2)]
            TmAll = CV.f32(1536)
            Tm = [v2(TmAll[:, 512 * i:512 * (i + 1)], 4) for i in range(3)]
            tmpA = TmAll[:, 0:1024]
            NPQ = CV.f32(2048)
            NP = NPQ[:, 0:1536]
            NM = v3(NP[:, 0:1024], 2, 4)
            Pm = v2(NP[:, 1024:1536], 4)
            stq_buf = NP
            QKmT = v2(NPQ[:, 1536:2048], 4)
            o_ = v2(CV.f32(512), 4)
            S_ = v2(CV.f32(512), 4)
            WmT = v2(CV.f32(512), 4)
            print("mixer2 arena used", CV.off, "of", CV.n)

            wsrc = w_in[l].rearrange("(k p) n -> p k n", p=128)
            for bi, (b0, wn) in enumerate(WBLK):
                P.dma("pool", out=w_in_blk[bi], in_=wsrc[:, :, b0:b0 + wn])
            wsrc2 = w_out[l].rearrange("(k p) n -> p k n", p=128)
            for k2 in range(2):
                P.dma("pool", out=w_out_sb[:, 4 * k2:4 * k2 + 4, :], in_=wsrc2[:, 4 * k2:4 * k2 + 4, :])

            gBt = FA[:, 0:1024]
            make_g(16, FA[:, 1024:2048], gBt)
            cr = Tm[0].rearrange("p a b -> p (a b)")
            P.dma("sp", out=cr[0:48, 0:128], in_=conv_qkv[l].rearrange("j (c p) -> (j c) p", p=128))
            P.dma("sp", out=cr[0:4, 128:256], in_=b_sgu[l])
            b = nb()
            mm(b[:, 0:48], cr[0:48, 0:128], identF[0:48, 0:48])
            mm(b[:, 64:68], cr[0:4, 128:256], identF[0:4, 0:4])
            cp(cwT[:, :], b[:, 0:48])
            cp(bsT[:, :], b[:, 64:68])
            P.dma("sp", out=sgnB[:, :], in_=sgu_norm[l].rearrange("h d -> (h d)").partition_broadcast(128))
            P.dma("sp", out=gdnB[:, :], in_=gdn_norm[l].partition_broadcast(128))
            mset(qcar, 0.0)
            Wraw = Tm[1]
            P.dma("sp", out=Wraw, in_=w_sgu[l].rearrange("h i j -> i h j"))
            w00 = sm(100, 104, 16)
            b00 = sm(104, 108, 16)
            b = nb()
            mm(b[0:16, 0:4], e0, Wraw[:, :, 0])
            mm(b[0:16, 4:8], e0, bsT[:, :])
            cp(SM[0:16, 100:108], b[0:16, 0:8])
            tt(Wraw, Wraw, bc(tril.unsqueeze(1), [128, 4, 128]), ALU.mult)
            b = nb()
            for h in range(4):
                tr(b[:, h * 128:(h + 1) * 128], Wraw[:, h, :], identF)
            cp(WmT, v2(b, 4))

            nA = nexpA[:, l * 4:l * 4 + 4]
            dB = dtB[:, l * 4:l * 4 + 4]

            def slots(par, n):
                beta = SM[0:n, 16:20] if par == 0 else SM[0:n, 112:116]
                g_ = SM[0:n, 36:40] if par == 0 else SM[0:n, 116:120]
                return beta, g_

            def front(t, sample):
                n = 16 if sample else 128
                par = t % 2
                xin = Xs[0:16, :] if sample else X[:, t, :]
                sz = szb[par]
                mix = mixb[par]
                mixv = v2(mix, 8)
                beta, g_ = slots(par, n)
                if sample:
                    QX = v2(qkvx[:, 0:256], 4)

                    def tap(j):
                        return QX[:, :, 16 * j:16 * j + 16]
                    cur = QX[:, :, 48:64]
                else:
                    QX = v2(qkvx[:, 0:524], 4)

                    def tap(j):
                        return QX[:, :, j:j + 128]
                    cur = QX[:, :, 3:131]
                QC = acc[:, :, 0:n]

                def F1():
                    norm_transpose(xin, n, xn, A1T, 0, hT, FT, 0)
                    if sample:
                        P.dma("sp", out=qs[l][:, 0:2, :], in_=sq[l][:, 1:3, :])

                def Fq(grp):
                    c4 = slice(4 * grp, 4 * grp + 4)
                    b = nb()
                    for cc in range(4):
                        c = grp * 4 + cc
                        for k in range(8):
                            mm(b[:, cc * n:(cc + 1) * n], wcol(k, c * 128, 128), hT[:, k, 0:n],
                               start=(k == 0), stop=(k == 7))
                    cp(cur, v2(b[:, 0:4 * n], 4), eng="act")
                    if sample:
                        stq = v2(stq_buf[0:16, :], 3)
                        P.dma("sp", out=stq, in_=sq[l][:, :, grp * 512:(grp + 1) * 512])
                        b2 = nb()
                        for cc in range(4):
                            for j in range(3):
                                idx = cc * 3 + j
                                mm(b2[:, idx * 16:(idx + 1) * 16], stq[:, j, cc * 128:(cc + 1) * 128],
                                   identF[0:16, 0:16])
                        cp(QX[:, :, 0:48], v2(b2[:, 0:192], 4), eng="act")
                        b3 = nb()
                        for k in range(8):
                            mm(b3[0:16, :], hT[:, k, 0:16], wcol(k, grp * 512, 512),
                               start=(k == 0), stop=(k == 7))
                        qsn = Tm[2].rearrange("p a b -> p (a b)")
                        cp(qsn[0:16, :], b3[0:16, :])
                        P.dma("sp", out=qs[l][:, 2, grp * 512:(grp + 1) * 512], in_=qsn[0:16, :])
                    else:
                        cp(QX[:, :, 0:3], qcar[:, c4, :])
                    ag = acc[:, c4, 0:n]
                    tg = v2(FT[:, 0:512], 4)[:, :, 0:n]

                    def cw(j):
                        return bc(cwT[:, j * 12 + 4 * grp:j * 12 + 4 * grp + 4].unsqueeze(2), [128, 4, n])
                    tt(ag, tap(3), cw(3), ALU.mult)
                    for j in (2, 1, 0):
                        tt(tg, tap(j), cw(j), ALU.mult)
                        tt(ag, ag, tg, ALU.add)
                    if not sample:
                        cp(qcar[:, c4, :], QX[:, :, 128:131])
                    act(ag, ag, AF.Silu)
                    if grp == 2 and (not sample) and t == NT - 1:
                        b = nb()
                        for j in range(3):
                            mm(b[0:12, j * 128:(j + 1) * 128], qcar[:, :, j], identF)
                        qtail = FT
                        cp(qtail[0:12, 0:384], b[0:12, 0:384])
                        P.dma("sp", out=qp[l].rearrange("j (c p) -> c j p", p=128), in_=v2(qtail[0:12, 0:384], 3))

                def F7():
                    bz, bu, bv_, bb = nb(), nb(), nb(), nb()
                    for (bk, c0, nn) in ((bb, 2048, 8), (bz, 1536, 512), (bu, 2056, 512), (bv_, 2568, 512)):
                        for k in range(8):
                            mm(bk[0:n, 0:nn], hT[:, k, 0:n], wcol(k, c0, nn), start=(k == 0), stop=(k == 7))
                    ba = sm(8, 16, n)
                    cp(ba, bb[0:n, 0:8])
                    act(sz[0:n, :], bz[0:n, :], AF.Silu)
                    uf = u_.rearrange("p a b -> p (a b)")
                    act(uf[0:n, :], bu[0:n, :], AF.Gelu_apprx_tanh)
                    act(FT[0:n, 0:512], bv_[0:n, :], AF.Gelu_apprx_tanh)
                    act(beta, SM[0:n, 8:12], AF.Sigmoid)
                    xa, ab_, ee, mx = sm(20, 24, n), sm(24, 28, n), sm(28, 32, n), sm(32, 36, n)
                    tt(xa, SM[0:n, 12:16], dB[0:n, :], ALU.add)
                    ts(ab_, xa, -1.0, None, ALU.mult)
                    tt(ab_, ab_, xa, ALU.max)
                    act(ee, ab_, AF.Exp, scale=-1.0)
                    act(ee, ee, AF.Ln, bias=1.0)
                    ts(mx, xa, 0.0, None, ALU.max)
                    tt(mx, mx, ee, ALU.add)
                    tt(g_, mx, nA[0:n, :], ALU.mult)

                def F8():
                    vg = v2(FT[:, 0:512], 4)
                    vv = v2(FT[:, 512:1024], 4)
                    tt(vv[0:n], vg[0:n], vg[0:n], ALU.mult)
                    ssv = sm(80, 84, n)
                    P.op("dve", "reduce_sum", out=ssv, in_=vv[0:n], axis=AX.X)
                    rv = sm(84, 88, n)
                    rsqrt_mean(rv, ssv, 1.0 / 128, n)
                    tt(vv[0:n], vg[0:n], bc(rv.unsqueeze(2), [n, 4, 128]), ALU.mult)
                    tt(vv[0:n], vv[0:n], v2(sgnB[0:n, :], 4), ALU.mult)
                    if sample:
                        P.dma("sp", out=sv[l], in_=vv[0:16].rearrange("p a b -> p (a b)"))
                        zt = Tm[2]
                        tt(zt[0:16], vv[0:16], bc(w00.unsqueeze(2), [16, 4, 128]), ALU.mult)
                        tt(zt[0:16], zt[0:16], bc(b00.unsqueeze(2), [16, 4, 128]), ALU.add)
                        tt(mixv[0:16, 4:8, :], zt[0:16], u_[0:16], ALU.mult)
                    else:
                        b = nb()
                        for h in range(4):
                            mm(b[:, h * 128:(h + 1) * 128], WmT[:, h, :], vv[:, h, :])
                        for h in range(4):
                            stt(mixv[:, 4 + h, :], b[:, h * 128:(h + 1) * 128], bsT[:, h:h + 1], u_[:, h, :],
                                ALU.add, ALU.mult)

                def F5():
                    sqb = v2(FT[:, 0:8 * n], 8)
                    tt(sqb, QC[:, 0:8, :], QC[:, 0:8, :], ALU.mult)
                    for half in range(2):
                        b = nb()
                        mm(b[:, 0:4 * n], ones, sqb[:, 4 * half:4 * half + 4, :])
                        rsv = sqb[:, 4 * half:4 * half + 4, :]
                        act(rsv, v2(b[:, 0:4 * n], 4), AF.Ln, bias=EPS)
                        act(rsv, rsv, AF.Exp, scale=-0.5, bias=(-0.5 * math.log(128.0) if half == 0 else 0.0))
                    tt(qkn[:, :, 0:n], QC[:, 0:8, :], sqb, ALU.mult)

                def F6():
                    b = nb()
                    for h in range(4):
                        tr(b[0:n, h * 128:(h + 1) * 128], QC[:, 8 + h, :], identF)
                    cp(vtok[0:n], v2(b[0:n, :], 4), eng="act")
                    b = nb()
                    for h in range(4):
                        tr(b[0:n, h * 128:(h + 1) * 128], qkn[:, 4 + h, 0:n], identF)
                    cp(ktok[0:n], v2(b[0:n, :], 4), eng="act")

                early = [F1, lambda: Fq(0), lambda: Fq(1), lambda: Fq(2), F7, F8]
                late = [F5, F6]
                return early, late

            def back(t):
                par = t % 2
                sz = szb[par]
                mix = mixb[par]
                mixv = v2(mix, 8)
                beta, g_ = slots(par, 128)
                xin = X[:, t, :]
                gc = sm(40, 56)
                ex = sm(56, 72)
                gsel = sm(72, 80)
                gcum = SM[:, 40:44]
                eg = SM[:, 56:60]
                ekd = SM[:, 60:64]
                egl = SM[:, 64:72]
                N_ = NM[:, 0]
                M_ = NM[:, 1]

                def B1():
                    bg = nb()
                    mm(bg[:, 0:4], TRIinc, g_)
                    mm(bg[:, 4:8], TRIsu, g_)
                    tt(v2(gsel, 4), bc(g_.unsqueeze(2), [128, 4, 2]), bc(cm.unsqueeze(1), [128, 4, 2]), ALU.mult)
                    mm(bg[:, 8:16], ones, gsel)
                    cp(gc, bg[:, 0:16])
                    act(ex, gc, AF.Exp)
                    gBk = Tm[1]
                    cp(gBk, bc(g_.unsqueeze(2), [128, 4, 128]))
                    bG = nb()
                    for h in range(4):
                        mm(bG[:, h * 128:(h + 1) * 128], gBk[:, h, :], TRIinc)
                    xd = Tm[2]
                    for h in range(4):
                        ts(xd[:, h, :], bG[:, h * 128:(h + 1) * 128], gcum[:, h:h + 1], 0.0, ALU.subtract, ALU.max)
                    act(xd, xd, AF.Exp, scale=-1.0)
                    tt(Tm[1], xd, bc(maskL.unsqueeze(1), [128, 4, 128]), ALU.mult)
                    tt(o_, xd, bc(maskQ.unsqueeze(1), [128, 4, 128]), ALU.mult)

                def B2():
                    decL, decQ = Tm[1], o_
                    bK = nb()
                    bQ = nb()
                    for h in range(4):
                        mm(bK[:, h * 128:(h + 1) * 128], qkn[:, 4 + h, :], qkn[:, 4 + h, :])
                    for h in range(4):
                        mm(bQ[:, h * 128:(h + 1) * 128], qkn[:, h, :], qkn[:, 4 + h, :])
                    for h in range(4):
                        stt(N_[:, h, :], bK[:, h * 128:(h + 1) * 128], beta[:, h:h + 1], decL[:, h, :],
                            ALU.mult, ALU.mult)
                    QKm = Tm[2]
                    tt(QKm, v2(bQ, 4), decQ, ALU.mult)
                    b1 = nb()
                    b2 = nb()
                    for h in range(4):
                        tr(b1[:, h * 128:(h + 1) * 128], N_[:, h, :], identF)
                    for h in range(4):
                        tr(b2[:, h * 128:(h + 1) * 128], QKm[:, h, :], identF)
                    cp(M_, v2(b1, 4), eng="act")
                    cp(QKmT, v2(b2, 4), eng="act")
                    tt(Pm, bc(identF.unsqueeze(1), [128, 4, 128]), M_, ALU.subtract)

                def Bs(s):
                    bN = nb()
                    for h in range(4):
                        mm(bN[:, h * 128:(h + 1) * 128], M_[:, h, :], N_[:, h, :])
                    if s < 4:
                        bM = nb()
                        for h in range(4):
                            mm(bM[:, h * 128:(h + 1) * 128], N_[:, h, :], M_[:, h, :])
                    cp(N_, v2(bN, 4), eng="act")
                    if s < 4:
                        cp(M_, v2(bM, 4))
                    bP = nb()
                    for h in range(4):
                        mm(bP[:, h * 128:(h + 1) * 128], N_[:, h, :], Pm[:, h, :])
                    tt(Pm, v2(bP, 4), Pm, ALU.add)
                    if s == 4:
                        tt(ktok, ktok, bc(ekd.unsqueeze(2), [128, 4, 128]), ALU.mult)

                def Bc(c):
                    TT = Pm
                    kdec = ktok
                    r2, vnew, otmp = Tm[0], Tm[1], Tm[2]
                    rs_ = slice(64 * c, 64 * c + 64)
                    bKS = nb()
                    bQS = nb()
                    for h in range(4):
                        mm(bKS[:, h * 128:(h + 1) * 128], qkn[:, 4 + h, :], S_[:, h, :])
                    for h in range(4):
                        mm(bQS[:, h * 128:(h + 1) * 128], qkn[:, h, :], S_[:, h, :])
                    for h in range(4):
                        stt(r2[rs_, h, :], bKS[rs_, h * 128:(h + 1) * 128], eg[rs_, h:h + 1], vtok[rs_, h, :],
                            ALU.mult, ALU.subtract)
                    for h in range(4):
                        ts(r2[rs_, h, :], r2[rs_, h, :], beta[rs_, h:h + 1], -1.0, ALU.mult, ALU.mult)
                    bV = nb()
                    for h in range(4):
                        mm(bV[:, h * 128:(h + 1) * 128], TT[rs_, h, :], r2[rs_, h, :])
                    cp(vnew[rs_], v2(bV, 4)[rs_], eng="act")
                    bO = nb()
                    for h in range(4):
                        mm(bO[:, h * 128:(h + 1) * 128], QKmT[rs_, h, :], vnew[rs_, h, :])
                    cp(otmp[rs_], v2(bO, 4)[rs_], eng="act")
                    for h in range(4):
                        stt(o_[rs_, h, :], bQS[rs_, h * 128:(h + 1) * 128], eg[rs_, h:h + 1], otmp[rs_, h, :],
                            ALU.mult, ALU.add)
                    bS = nb()
                    for h in range(4):
                        mm(bS[:, h * 128:(h + 1) * 128], kdec[rs_, h, :], vnew[rs_, h, :])
                    for h in range(4):
                        stt(S_[:, h, :], S_[:, h, :], egl[:, 2 * h + c:2 * h + c + 1], bS[:, h * 128:(h + 1) * 128],
                            ALU.mult, ALU.add)

                def B10():
                    gated_norm(128, mixv, sz)

                def B11():
                    out_proj(128, mix, xin, None)

                a = [B1, B2] + [(lambda s=s: Bs(s)) for s in range(5)] + [lambda: Bc(0), lambda: Bc(1)]
                return a, [B10, B11]

            def gated_norm(n, mixv, sz):
                sqo = Tm[0]
                tt(sqo[0:n], o_[0:n], o_[0:n], ALU.mult)
                sso = sm(88, 92, n)
                P.op("dve", "reduce_sum", out=sso, in_=sqo[0:n], axis=AX.X)
                ro = sm(92, 96, n)
                rsqrt_mean(ro, sso, 1.0 / 128, n)
                on = Tm[0]
                tt(on[0:n], o_[0:n], bc(ro.unsqueeze(2), [n, 4, 128]), ALU.mult)
                tt(on[0:n], on[0:n], bc(gdnB[0:n, :].unsqueeze(1), [n, 4, 128]), ALU.mult)
                tt(mixv[0:n, 0:4, :], on[0:n], v2(sz[0:n, :], 4), ALU.mult)

            def out_proj(n, mix, xin, gsrc):
                b = nb().bitcast(BF16)
                for k in range(8):
                    tr(b[:, k * n:(k + 1) * n], mix[0:n, k * 128:(k + 1) * 128], identb[0:n, 0:n])
                cp(mixT[:, :, 0:n], v2(b[:, 0:8 * n], 8), eng="act")
                for half in range(2):
                    b = nb()
                    for k in range(8):
                        mm(b[0:n, :], mixT[:, k, 0:n], w_out_sb[:, k, half * 512:(half + 1) * 512],
                           start=(k == 0), stop=(k == 7))
                    hs = slice(half * 512, (half + 1) * 512)
                    if gsrc is None:
                        tt(xin[:, hs], xin[:, hs], b[0:n, :], ALU.add)
                    else:
                        tt(tmpA[0:n, hs], b[0:n, :], gsrc[0:n, hs], ALU.mult)
                        tt(xin[:, hs], xin[:, hs], tmpA[0:n, hs], ALU.add)

            def delta_sample(beta, g_):
                egs = sm(56, 60, 16)
                act(egs, g_, AF.Exp)
                b = nb()
                for h in range(4):
                    tr(b[0:16, h * 128:(h + 1) * 128], qkn[:, h, 0:16], identF)
                qtok = Tm[0]
                cp(qtok[0:16], v2(b[0:16, :], 4), eng="act")
                tt(qtok[0:16], qtok[0:16], ktok[0:16], ALU.mult)
                qk = sm(108, 112, 16)
                P.op("dve", "reduce_sum", out=qk, in_=qtok[0:16], axis=AX.X)
                gdiag = Tm[1].rearrange("p a b -> p (a b)")[0:16, 0:64]
                tt(v2(gdiag, 16), bc(g_.unsqueeze(1), [16, 16, 4]),
                   bc(identF[0:16, 0:16].unsqueeze(2), [16, 16, 4]), ALU.mult)
                b = nb()
                mm(b[:, 0:64], ones[0:16, :], gdiag)
                EGb = sm(128, 192)
                act(EGb, b[:, 0:64], AF.Exp)
                r2, vnew, otmp = Tm[2], S_, u_
                kqm = Tm[1].rearrange("p a b -> p (a b)")
                for h in range(4):
                    SA = v2(FA[:, 0:2048], 16) if h % 2 == 0 else v2(NPQ, 16)
                    P.dma("sp", out=SA, in_=sd[l][:, h].rearrange("b k v -> k b v"))
                    kTm = v2(kqm[:, 0:256], 16)
                    qTm = v2(kqm[:, 256:512], 16)
                    tt(kTm, bc(qkn[:, 4 + h, 0:16].unsqueeze(1), [128, 16, 16]), i16rep, ALU.mult)
                    tt(qTm, bc(qkn[:, h, 0:16].unsqueeze(1), [128, 16, 16]), i16rep, ALU.mult)
                    bKS = nb()
                    bQS = nb()
                    for bb_ in range(16):
                        mm(bKS[0:16, 0:128], kTm[:, bb_, :], SA[:, bb_, :], start=(bb_ == 0), stop=(bb_ == 15))
                    for bb_ in range(16):
                        mm(bQS[0:16, 0:128], qTm[:, bb_, :], SA[:, bb_, :], start=(bb_ == 0), stop=(bb_ == 15))
                    stt(r2[0:16, h, :], bKS[0:16, 0:128], egs[:, h:h + 1], vtok[0:16, h, :], ALU.mult, ALU.subtract)
                    ts(vnew[0:16, h, :], r2[0:16, h, :], beta[:, h:h + 1], -1.0, ALU.mult, ALU.mult)
                    ts(otmp[0:16, h, :], vnew[0:16, h, :], qk[:, h:h + 1], None, ALU.mult)
                    stt(o_[0:16, h, :], bQS[0:16, 0:128], egs[:, h:h + 1], otmp[0:16, h, :], ALU.mult, ALU.add)
                    vmr = Tm[0].rearrange("p a b -> p (a b)")
                    for q4 in range(4):
                        bS = nb()
                        for j in range(4):
                            bb_ = q4 * 4 + j
                            vm = vmr[0:16, j * 128:(j + 1) * 128]
                            ts(vm, vnew[0:16, h, :], identF[0:16, bb_:bb_ + 1], None, ALU.mult)
                            mm(bS[:, j * 128:(j + 1) * 128], ktok[0:16, h, :], vm)
                        for j in range(4):
                            bb_ = q4 * 4 + j
                            stt(SA[:, bb_, :], SA[:, bb_, :], SM[:, 128 + bb_ * 4 + h:128 + bb_ * 4 + h + 1],
                                bS[:, j * 128:(j + 1) * 128], ALU.mult, ALU.add)
                    P.dma("sp", out=ds[l][:, h].rearrange("b k v -> k b v"), in_=SA)

            ckpt("mix%d_params" % l)
            e_, l_ = front(0, True)
            for f in e_ + l_:
                f()
            beta_s, g_s = slots(0, 16)
            delta_sample(beta_s, g_s)
            gated_norm(16, v2(mixb[0], 8), szb[0])
            out_proj(16, mixb[0], Xs[0:16, :], gS)
            ckpt("mix%d_s" % l)
            mset(S_, 0.0)
            make_g(16, FA[:, 1024:2048], gBt)
            tt(w_out_sb, w_out_sb, bc(gBt.unsqueeze(1), [128, 8, 1024]), ALU.mult)
            e_, l_ = front(0, False)
            for f in e_ + l_:
                f()
            poolF = {"l": [0, 1, 2, 3], "i": 0}
            poolB = {"l": [4, 5, 6, 7], "i": 0}
            for t in range(NT):
                ba_, bb_l = back(t)
                if t + 1 < NT:
                    fe, fl = front(t + 1, False)
                else:
                    fe, fl = [], []
                merge_replay(record(fe, poolF), record(ba_, poolB), lead=0.05)
                merge_replay(record(fl, poolF), record(bb_l, poolB))
                ckpt("mix%d_t%d" % (l, t))
            P.dma("sp", out=dp[l].rearrange("h k v -> k h v"), in_=S_)
            ckpt("mix%d_dp" % l)

        def ffn_phase(l):
            CV.reset()
            NTK = T + NS
            h2T = v2(CV.bf16(8 * NTK), 8)
            wu = [v2(CV.bf16(8 * 512), 8) for _ in range(2)]
            wd = [v2(CV.bf16(2 * 1024), 2) for _ in range(2)]
            xn = CV.bf16(1024)
            tmpA = CV.f32(1024)
            upx = [[[CV.f32(516) for _ in range(2)] for _ in range(2)] for _ in range(2)]
            yb = [[CV.f32(512) for _ in range(2)] for _ in range(2)]
            actT = [[CV.bf16(512) for _ in range(2)] for _ in range(2)]
            tmpD = [CV.f32(1024) for _ in range(2)]
            fcar = v2(CV.f32(88), 2)
            fraw = CV.f32(128)
            sfg = CV.f32(1024)
            fsn = CV.f32(512)
            ptmp = CV.f32(512)
            upxs = [[CV.f32(48) for _ in range(2)] for _ in range(2)]

            gB = CV.f32(1024)
            wm2 = [v2(CV.bf16(8 * 256), 8) for _ in range(2)]
            make_g(40, tmpA, gB)
            b = nb()
            for j in range(3):
                P.dma("sp", out=fraw[0:44, :], in_=conv_ffn_w[l][j].rearrange("(c p) -> c p", p=128))
                mm(b[:, j * 44:(j + 1) * 44], fraw[0:44, :], identF[0:44, 0:44])
            cp(cfwT_t[:, :], b[:, 0:132])
            P.dma("sp", out=fraw[0:44, :], in_=conv_ffn_b[l].rearrange("(c p) -> c p", p=128))
            b = nb()
            mm(b[:, 0:44], fraw[0:44, :], identF[0:44, 0:44])
            cp(cfbT[:, :], b[:, 0:44])
            P.dma("sp", out=fs[l][:, 0, :], in_=sf[l][:, 1, :])

            norm_transpose(Xs[0:16, :], 16, xn, A2T, 24, h2T, tmpA, T)
            for t in range(4):
                norm_transpose(X[:, t, :], 128, xn, A2T, 24, h2T, tmpA, t * 128)
            ckpt("ffn%d_h2T" % l)

            wus = w_up[l].rearrange("(k p) n -> p k n", p=128)
            wds = w_down[l].rearrange("(c p) n -> p c n", p=128)

            def load_w(g):
                s = g % 2
                P.dma("pool", out=wu[s][:, :, 0:256], in_=wus[:, :, g * 256:(g + 1) * 256])
                P.dma("pool", out=wu[s][:, :, 256:512], in_=wus[:, :, DFF + g * 256:DFF + (g + 1) * 256])
                P.dma("pool", out=wd[s], in_=wds[:, 2 * g:2 * g + 2, :])

            load_w(0)
            load_w(1)
            poolU = {"l": [0, 1, 2, 3], "i": 0}
            poolD = {"l": [4, 5, 6, 7], "i": 0}

            def nbp(pl):
                b_ = ps[pl["l"][pl["i"]]]
                pl["i"] = (pl["i"] + 1) % len(pl["l"])
                return b_[:, :]

            units = [(g, blk) for g in range(NG) for blk in (4, 0, 1, 2, 3)]

            def up(ui):
                g, blk = units[ui]
                s = g % 2
                sample = (blk == 4)
                N = 16 if sample else 512
                col = T if sample else blk * 512
                sfv = v3(sfg[0:16, :], 2, 2)
                if sample:
                    ckpt("ffn%d_g%d" % (l, g))
                    P.dma("sp", out=sfv[:, :, 0, :], in_=sf[l][:, :, g * 256:(g + 1) * 256])
                    P.dma("sp", out=sfv[:, :, 1, :], in_=sf[l][:, :, DFF + g * 256:DFF + (g + 1) * 256])
                for pr in range(2):
                    cidx = (2 * g + pr, 22 + 2 * g + pr)
                    bks = (nbp(poolU), nbp(poolU))
                    for gv in range(2):
                        for k in range(8):
                            mm(bks[gv][:, 0:N], wu[s][:, k, gv * 256 + pr * 128:gv * 256 + (pr + 1) * 128],
                               h2T[:, k, col:col + N], start=(k == 0), stop=(k == 7))
                    for gv in range(2):
                        c = cidx[gv]
                        y = yb[gv][pr][:, 0:N]
                        if sample:
                            ux = upxs[gv][pr]
                            b2 = nbp(poolU)
                            for j in range(2):
                                mm(b2[:, j * 16:(j + 1) * 16], sfv[:, j, gv, pr * 128:(pr + 1) * 128],
                                   identF[0:16, 0:16])
                            cp(ux[:, 0:32], b2[:, 0:32], eng="act")
                            cp(ux[:, 32:48], bks[gv][:, 0:16], eng="act")
                            taps = [ux[:, 0:16], ux[:, 16:32], ux[:, 32:48]]
                            ts(y, taps[2], cfwT[:, 2, c:c + 1], cfbT[:, c:c + 1], ALU.mult, ALU.add)
                        else:
                            ux = upx[gv][pr][blk % 2]
                            cp(ux[:, 2:2 + N], bks[gv][:, 0:N], eng="act")
                            act(y, bks[gv][:, 0:N], AF.Identity, scale=cfwT[:, 2, c:c + 1], bias=cfbT[:, c:c + 1])
                            if blk == 0:
                                mset(ux[:, 0:2], 0.0)
                            if blk < 3:
                                cp(upx[gv][pr][(blk + 1) % 2][:, 0:2], ux[:, N:N + 2])
                            else:
                                cp(fcar[:, :, c], ux[:, N:N + 2])
                            taps = [ux[:, 0:N], ux[:, 1:1 + N], ux[:, 2:2 + N]]
                        stt(y, taps[1], cfwT[:, 1, c:c + 1], y, ALU.mult, ALU.add)
                        stt(y, taps[0], cfwT[:, 0, c:c + 1], y, ALU.mult, ALU.add)
                    yg = yb[0][pr][:, 0:N]
                    act(yg, yg, AF.Silu)
                    tt(actT[pr][ui % 2][:, 0:N], yg, yb[1][pr][:, 0:N], ALU.mult)
                if sample:
                    b3 = nbp(poolU)
                    for k in range(8):
                        mm(b3[0:16, :], h2T[:, k, T:T + 16], wu[s][:, k, :], start=(k == 0), stop=(k == 7))
                    cp(fsn[0:16, :], b3[0:16, :])
                    P.dma("sp", out=fs[l][:, 1, g * 256:(g + 1) * 256], in_=fsn[0:16, 0:256])
                    P.dma("sp", out=fs[l][:, 1, DFF + g * 256:DFF + (g + 1) * 256], in_=fsn[0:16, 256:512])

            def down(ui):
                g, blk = units[ui]
                s = g % 2
                sample = (blk == 4)
                ntile = 1 if sample else 4
                n = 16 if sample else 128
                for tt_ in range(ntile):
                    xin = Xs[0:16, :] if sample else X[:, blk * 4 + tt_, :]
                    td = tmpD[tt_ % 2]
                    for half in range(2):
                        bD = nbp(poolD)
                        for pr in range(2):
                            mm(bD[0:n, :], actT[pr][ui % 2][:, tt_ * n:(tt_ + 1) * n],
                               wd[s][:, pr, half * 512:(half + 1) * 512], start=(pr == 0), stop=(pr == 1))
                        hs = slice(half * 512, (half + 1) * 512)
                        if sample:
                            tt(td[0:n, hs], bD[0:n, :], gS[0:n, hs], ALU.mult)
                            tt(xin[:, hs], xin[:, hs], td[0:n, hs], ALU.add)
                        else:
                            tt(xin[:, hs], xin[:, hs], bD[0:n, :], ALU.add)
                if sample:
                    tt(wd[s], wd[s], bc(gB.unsqueeze(1), [128, 2, 1024]), ALU.mult)
                if blk == 3 and g + 2 < NG:
                    load_w(g + 2)

            poolA = {"l": [7], "i": 0}
            poolD["l"] = [4, 5, 6]
            recA = record([(lambda t=t: norm_transpose(X[:, t, :], 128, xn, A2T, 24, h2T, tmpA, t * 128))
                           for t in range(4, NT)], poolA)
            recB = record([lambda: up(0), lambda: up(1), lambda: down(0), lambda: up(2), lambda: down(1)], None)
            merge_replay(recA, recB)
            poolD["l"] = [4, 5, 6, 7]
            poolD["i"] = 0
            def run_units(a_, b_):
                for ui in range(a_, b_):
                    if ui + 1 < len(units):
                        up(ui + 1)
                    down(ui)

            run_units(2, 5)
            if l + 1 < L:
                poolU["l"] = [0, 1, 2]
                poolU["i"] = 0
                recM = record([lambda: mod_ops(l + 1, wm2, fraw, 256)], {"l": [3], "i": 0})
                recB = record([lambda: run_units(5, 25)], None)
                merge_replay(recM, recB)
                poolU["l"] = [0, 1, 2, 3]
                poolU["i"] = 0
                run_units(25, len(units))
            else:
                run_units(5, len(units))
            b = nb()
            mm(b[0:88, 0:128], fcar.rearrange("p a b -> p (a b)"), identF)
            cp(tmpA[0:88, 0:128], b[0:88, 0:128])
            P.dma("sp", out=fp[l].rearrange("j (c p) -> (j c) p", p=128), in_=tmpA[0:88, 0:128])

        stopped = False
        try:
            ckpt("setup")
            mod_phase(0)
            ckpt("mod0")
            for l in range(L):
                mixer_phase2(l)
                ckpt("mix%d" % l)
                ffn_phase(l)
                ckpt("ffn%d" % l)
        except StopBuild:
            stopped = True
            PT = {"modT": modT_t[:, :], "X0": X[:, 0, :], "X1": X[:, 1, :], "X15": X[:, 15, :], "Xs": Xs[0:16, :], "SM": SM[:, :],
                  "gS": gS[0:16, :], "A1T": A1T_t[:, :], "A2T": A2T_t[:, :], "cwT": cwT[:, :], "bsT": bsT[:, :],
                  "nT1": nT1[:, :], "bmT": bmT[:, :], "cfwT": cfwT_t[:, :], "cfbT": cfbT[:, :], "cT": cT_t[:, :]}
            for nm in dump_names:
                if nm in PT:
                    dump(nm, PT[nm])

        CV.reset()
        fnB = CV.f32(1024)
        junk = CV.f32(1024)
        yo = [CV.f32(1024) for _ in range(2)]
        P.dma("sp", out=fnB, in_=final_norm.partition_broadcast(128))
        for t in range(NT + 1):
            sample = (t == NT)
            n = 16 if sample else 128
            xin = Xs[0:16, :] if sample else X[:, t, :]
            ss = SM[0:n, (2 * (t % 2)):(2 * (t % 2)) + 1]
            rstd = SM[0:n, (2 * (t % 2)) + 1:(2 * (t % 2)) + 2]
            mset(ss, 0.0)
            act(junk[0:n, :], xin, AF.Square, accum_out=ss)
            rsqrt_mean(rstd, ss, 1.0 / D, n)
            y_ = yo[t % 2]
            stt(y_[0:n, :], xin, rstd, fnB[0:n, :], ALU.mult, ALU.mult)
            if sample:
                P.dma("sp", out=ys[:, :], in_=y_[0:16, :])
            else:
                P.dma("sp", out=yp[t * 128:(t + 1) * 128, :], in_=y_[:, :])

        names = P.sem_names()
        semh = {nm: es.enter_context(nc.semaphore(nm)) for nm in names}
        block = es.enter_context(nc.Block())

        def emit(engname, e):
            for (wl, meth, kw, dsem, where) in P.ops[engname]:
                for (k, v) in wl:
                    e.wait_ge(semh[k], v)
                try:
                    ins = getattr(e, meth)(**kw)
                except Exception:
                    print("EMIT FAILED", engname, meth, where, {k_: (v_.shape if _is_ap(v_) else v_) for k_, v_ in kw.items()})
                    raise
                if dsem is None:
                    ins.then_inc(semh[engname], 1)
                else:
                    ins.then_inc(semh[dsem], 16)

        @block.tensor
        def _(e):
            emit("pe", e)

        @block.scalar
        def _(e):
            emit("act", e)

        @block.vector
        def _(e):
            emit("dve", e)

        @block.gpsimd
        def _(e):
            emit("pool", e)

        @block.sync
        def _(e):
            emit("sp", e)
            for nm, v in P.dval.items():
                e.wait_ge(semh[nm], v)
            for en in ("pe", "act", "dve"):
                if P.cnt[en]:
                    e.wait_ge(semh[en], P.cnt[en])
    return nc, P


_CACHE = {}


def kernel(**inputs):
    f = lambda a: np.ascontiguousarray(np.asarray(a, dtype=np.float32))
    inp = {k: f(v) for k, v in inputs.items()}
    if "nc" not in _CACHE:
        _CACHE["nc"] = build_program()[0]
    nc = _CACHE["nc"]
    consts = make_consts()
    shared = {k: inp[k] for k in ("w_mod", "b_mod", "norm1", "w_in", "conv_qkv", "a_log", "dt_bias", "gdn_norm",
                                  "sgu_norm", "w_sgu", "b_sgu", "w_out", "norm2", "w_up", "conv_ffn_w",
                                  "conv_ffn_b", "w_down", "final_norm")}
    in_maps = []
    for i in range(NCORES):
        sl = slice(NS * i, NS * (i + 1))
        m = dict(shared)
        m["consts"] = consts
        m["xp"] = f(inp["x_prompt"][i])
        m["xs"] = f(inp["x_sample"][sl, 0, :])
        m["sd"] = f(inp["state_delta"][:, sl])
        m["sq"] = f(inp["state_qkv_conv"][:, sl])
        m["sf"] = f(inp["state_ffn_conv"][:, sl])
        m["call"] = f(np.concatenate([inp["c_prompt"][i:i + 1], inp["c_sample"][sl]], axis=0))
        in_maps.append(m)
    ncr = _CACHE.get("ncores_dbg", NCORES)
    res = run_bass_kernel_spmd(nc, in_maps[:ncr], core_ids=list(range(ncr)))
    R = list(res.results)
    while len(R) < NCORES:
        R.append({k: np.zeros_like(v) for k, v in R[0].items()})
    _CACHE["dbg"] = np.stack([R[i]["dbg"] for i in range(NCORES)]) if "dbg" in R[0] else None
    y_prompt = np.stack([R[i]["yp"] for i in range(NCORES)], axis=0)
    y_sample = np.concatenate([R[i]["ys"] for i in range(NCORES)], axis=0)[:, None, :]
    delta_p = np.stack([R[i]["dp"] for i in range(NCORES)], axis=1)
    delta_s = np.concatenate([R[i]["ds"] for i in range(NCORES)], axis=1)
    qkv_p = np.stack([R[i]["qp"] for i in range(NCORES)], axis=1)
    qkv_s = np.concatenate([R[i]["qs"] for i in range(NCORES)], axis=1)
    ffn_p = np.stack([R[i]["fp"] for i in range(NCORES)], axis=1)
    ffn_s = np.concatenate([R[i]["fs"] for i in range(NCORES)], axis=1)
    sgu_v = np.concatenate([R[i]["sv"] for i in range(NCORES)], axis=1).reshape(L, NS * NCORES, 1, H, 128)
    outs = (y_prompt, y_sample, delta_p, delta_s, qkv_p, qkv_s, ffn_p, ffn_s, sgu_v)
    return tuple(np.ascontiguousarray(o, dtype=np.float32) for o in outs)
```

```python
import math
import sys
import numpy as np
from contextlib import ExitStack
import concourse.bass as bass
import concourse.mybir as mybir
from concourse.bass_utils import run_bass_kernel_spmd

F32 = mybir.dt.float32
BF16 = mybir.dt.bfloat16
AF = mybir.ActivationFunctionType
ALU = mybir.AluOpType
AX = mybir.AxisListType

D = 1024
T = 2048
NT = 16
L = 4
NS = 16
H = 4
QKV = 1536
IN = 3080
DFF = 2816
FF2 = 5632
NG = 11
EPS = 1e-6
NCORES = 8

C_ID, C_ONE, C_TI, C_TS, C_ML, C_MQ, C_TR = 0, 128, 256, 384, 512, 640, 768
C_CM = 896
C_E0 = 898
C_I16 = 914
NCONST = 914 + 256

ENG = ("pe", "act", "dve", "pool", "sp")
SELF_GAP = 1 << 30


def make_consts():
    c = np.zeros((128, NCONST), np.float32)
    i = np.arange(128)
    same = (i[:, None] // 64) == (i[None, :] // 64)
    c[:, C_ID:C_ID + 128] = np.eye(128)
    c[:, C_ONE:C_ONE + 128] = 1.0
    c[:, C_TI:C_TI + 128] = (same & (i[:, None] <= i[None, :]))
    c[:, C_TS:C_TS + 128] = (same & (i[:, None] > i[None, :]))
    c[:, C_ML:C_ML + 128] = (same & (i[:, None] > i[None, :]))
    c[:, C_MQ:C_MQ + 128] = (same & (i[:, None] >= i[None, :]))
    c[:, C_TR:C_TR + 128] = (i[:, None] >= i[None, :])
    c[:, C_CM + 0] = (i < 64)
    c[:, C_CM + 1] = (i >= 64)
    c[0, C_E0:C_E0 + 16] = 1.0
    c[:, C_I16:C_I16 + 256] = np.eye(16, dtype=np.float32).reshape(1, 256)
    return c


def _is_ap(v):
    return hasattr(v, "tensor") and hasattr(v, "ap") and hasattr(v, "offset")


def _box(ap):
    t = ap.tensor
    s = 2 if ap.dtype == BF16 else 4
    a = ap.ap
    off = int(ap.offset)
    if type(t).__name__.startswith("DRam"):
        ext = sum(st * (c - 1) for st, c in a)
        return (t.name, 0, 1, off * s, (off + ext + 1) * s)
    if type(t).__name__.startswith("PSum"):
        return (t.name, 0, 128, 0, 2048)
    pstep, npart = a[0]
    if pstep == 0:
        p0, fo = 0, off
    else:
        p0 = off // pstep
        fo = off - p0 * pstep
    ext = sum(st * (c - 1) for st, c in a[1:])
    return (t.name, p0, p0 + npart, fo * s, (fo + ext + 1) * s)


def _where():
    f = sys._getframe(2)
    out = []
    while f is not None and len(out) < 4:
        out.append(f.f_lineno)
        f = f.f_back
    return out


class Prog:
    def __init__(self):
        self.ops = {e: [] for e in ENG}
        self.cnt = {e: 0 for e in ENG}
        self.known = {e: {} for e in ENG}
        self.recs = {}
        self.dpool = {"sp": ["dsp%d" % i for i in range(24)], "pool": ["dpl%d" % i for i in range(12)]}
        self.dnext = {"sp": 0, "pool": 0}
        self.dval = {}
        self.rec = None

    def replay(self, item):
        kind, eng, meth, kw = item
        if kind == 0:
            self.op(eng, meth, **kw)
        else:
            self.dma(eng, **kw)

    def _collect(self, kw):
        accs = []
        need = {}
        for k, v in kw.items():
            if _is_ap(v):
                accs.append((_box(v), k in ("out", "accum_out", "ap")))
        for bx, isw in accs:
            name, p0, p1, lo, hi = bx
            for r in self.recs.get(name, ()):
                if r[0] < p1 and p0 < r[1] and r[2] < hi and lo < r[3]:
                    if isw or r[4] == "w":
                        if need.get(r[5], 0) < r[6]:
                            need[r[5]] = r[6]
        return accs, need

    def _commit(self, accs, key, val):
        for bx, isw in accs:
            name, p0, p1, lo, hi = bx
            lst = self.recs.setdefault(name, [])
            if isw:
                lst[:] = [r for r in lst if not (p0 <= r[0] and r[1] <= p1 and lo <= r[2] and r[3] <= hi)]
                lst.append([p0, p1, lo, hi, "w", key, val])
            else:
                for r in lst:
                    if r[4] == "r" and r[5] == key and r[0] == p0 and r[1] == p1 and r[2] == lo and r[3] == hi:
                        r[6] = val
                        break
                else:
                    lst.append([p0, p1, lo, hi, "r", key, val])

    def _filter(self, eng, need):
        wl = []
        kn = self.known[eng]
        for k, v in need.items():
            if k == eng:
                if eng == "pe":
                    continue
                if v <= self.cnt[eng] - SELF_GAP and False:
                    continue
            if kn.get(k, 0) >= v:
                continue
            kn[k] = v
            wl.append((k, v))
        return wl

    def op(self, eng, meth, **kw):
        if self.rec is not None:
            self.rec.append((0, eng, meth, kw))
            return
        accs, need = self._collect(kw)
        wl = self._filter(eng, need)
        self.cnt[eng] += 1
        self.ops[eng].append((wl, meth, kw, None, _where()))
        self._commit(accs, eng, self.cnt[eng])

    def dma(self, eng, **kw):
        if self.rec is not None:
            self.rec.append((1, eng, None, kw))
            return
        accs, need = self._collect(kw)
        pool = self.dpool[eng]
        i = self.dnext[eng]
        self.dnext[eng] = (i + 1) % len(pool)
        sem = pool[i]
        prev = self.dval.get(sem, 0)
        if prev:
            need[sem] = max(need.get(sem, 0), prev)
        wl = self._filter(eng, need)
        v = prev + 16
        self.dval[sem] = v
        self.ops[eng].append((wl, "dma_start", kw, sem, _where()))
        self._commit(accs, sem, v)

    def sem_names(self):
        return [e for e in ENG if e != "sp"] + self.dpool["sp"] + self.dpool["pool"]


def v2(ap, a):
    return ap.rearrange("p (a b) -> p a b", a=a)


def v3(ap, a, b):
    return ap.rearrange("p (a b c) -> p a b c", a=a, b=b)


class Carve:
    def __init__(self, ph, n):
        self.ph = ph
        self.n = n
        self.off = 0

    def reset(self):
        self.off = 0

    def f32(self, n):
        assert self.off + n <= self.n, ("arena overflow", self.off, n, self.n)
        ap = self.ph[:, self.off:self.off + n]
        self.off += n
        return ap

    def bf16(self, n):
        nf = (n + 1) // 2
        assert self.off + nf <= self.n, ("arena overflow", self.off, nf, self.n)
        ap = self.ph[:, self.off:self.off + nf].bitcast(BF16)[:, 0:n]
        self.off += nf
        return ap


class StopBuild(Exception):
    pass


def build_program(stop_at=None, dump_names=()):
    nc = bass.Bass("TRN2", target_bir_lowering=False)
    P = Prog()

    P.marks = []

    def ckpt(name):
        P.marks.append((name, dict(P.cnt)))
        if stop_at is not None and name == stop_at:
            raise StopBuild(name)

    P.dump_info = []
    dcol = [0]

    def dump(name, ap):
        if name not in dump_names:
            return
        if len(ap.shape) > 2:
            ap = ap.rearrange("p a b -> p (a b)") if len(ap.shape) == 3 else ap.rearrange("p a b c -> p (a b c)")
        w = ap.shape[1]
        P.dma("pool" if ap.dtype == BF16 else "sp", out=dbg[0:ap.shape[0], dcol[0]:dcol[0] + w], in_=ap)
        P.dump_info.append((name, dcol[0], ap.shape[0], w))
        dcol[0] += w

    def din(name, shape):
        return nc.dram_tensor(name, list(shape), F32, kind="ExternalInput").ap()

    def dout(name, shape):
        return nc.dram_tensor(name, list(shape), F32, kind="ExternalOutput").ap()

    xp = din("xp", [T, D])
    xs = din("xs", [NS, D])
    sd = din("sd", [L, NS, H, 128, 128])
    sq = din("sq", [L, NS, 3, QKV])
    sf = din("sf", [L, NS, 2, FF2])
    call = din("call", [17, D])
    w_mod = din("w_mod", [L, D, 6 * D])
    b_mod = din("b_mod", [L, 6 * D])
    norm1 = din("norm1", [L, D])
    w_in = din("w_in", [L, D, IN])
    conv_qkv = din("conv_qkv", [L, 4, QKV])
    a_log = din("a_log", [L, H])
    dt_bias = din("dt_bias", [L, H])
    gdn_norm = din("gdn_norm", [L, 128])
    sgu_norm = din("sgu_norm", [L, H, 128])
    w_sgu = din("w_sgu", [L, H, 128, 128])
    b_sgu = din("b_sgu", [L, H, 128])
    w_out = din("w_out", [L, D, D])
    norm2 = din("norm2", [L, D])
    w_up = din("w_up", [L, D, FF2])
    conv_ffn_w = din("conv_ffn_w", [L, 3, FF2])
    conv_ffn_b = din("conv_ffn_b", [L, FF2])
    w_down = din("w_down", [L, DFF, D])
    final_norm = din("final_norm", [D])
    consts = din("consts", [128, NCONST])

    yp = dout("yp", [T, D])
    ys = dout("ys", [NS, D])
    dp = dout("dp", [L, H, 128, 128])
    ds = dout("ds", [L, NS, H, 128, 128])
    qp = dout("qp", [L, 3, QKV])
    qs = dout("qs", [L, NS, 3, QKV])
    fp = dout("fp", [L, 2, FF2])
    fs = dout("fs", [L, NS, 2, FF2])
    sv = dout("sv", [L, NS, 512])
    dbg = dout("dbg", [128, 8192]) if stop_at is not None else None

    es = ExitStack()
    with es:
        def sb(name, shape, dt=F32):
            return es.enter_context(nc.sbuf_tensor(name, list(shape), dt))

        cst = sb("cst", [128, NCONST])
        identb_t = sb("identb", [128, 128], BF16)
        onesb_t = sb("onesb", [128, 128], BF16)
        X = sb("X", [128, NT, D])
        Xs = sb("Xs", [128, D])
        cT_t = sb("cT", [128, 8 * 17], BF16)
        nT1 = sb("nT1", [128, 32])
        nT2 = sb("nT2", [128, 32])
        gdnB = sb("gdnB", [128, 128])
        sgnB = sb("sgnB", [128, 512])
        alB = sb("alB", [128, 16])
        dtB = sb("dtB", [128, 16])
        nexpA = sb("nexpA", [128, 16])
        modT_t = sb("modT", [128, 48 * 17])
        bmT = sb("bmT", [128, 48])
        A1T_t = sb("A1T", [128, 8 * 17])
        A2T_t = sb("A2T", [128, 8 * 17])
        gS = sb("gS", [128, D])
        SM = sb("SM", [128, 256])
        cwT = sb("cwT", [128, 48])
        bsT = sb("bsT", [128, 4])
        cfwT_t = sb("cfwT", [128, 3 * 44])
        cfbT = sb("cfbT", [128, 44])
        ps = [es.enter_context(nc.psum_tensor("ps%d" % i, [128, 512], F32)) for i in range(8)]
        rem = int(nc.sbuf_bytes_remaining) if not callable(nc.sbuf_bytes_remaining) else int(nc.sbuf_bytes_remaining())
        print('sbuf remaining', rem)
        if rem > 229376:
            rem = rem // 128
        NPH = (rem - 1024) // 4
        PH = sb("PH", [128, NPH])
        CV = Carve(PH, NPH)

        identF = cst[:, C_ID:C_ID + 128]
        ones = cst[:, C_ONE:C_ONE + 128]
        TRIinc = cst[:, C_TI:C_TI + 128]
        TRIsu = cst[:, C_TS:C_TS + 128]
        maskL = cst[:, C_ML:C_ML + 128]
        maskQ = cst[:, C_MQ:C_MQ + 128]
        tril = cst[:, C_TR:C_TR + 128]
        cm = cst[:, C_CM:C_CM + 2]
        e0 = cst[:, C_E0:C_E0 + 16]
        i16rep = v2(cst[:, C_I16:C_I16 + 256], 16)
        identb = identb_t[:, :]
        cT = v2(cT_t[:, :], 8)
        modT = v2(modT_t[:, :], 48)
        A1T = v2(A1T_t[:, :], 8)
        A2T = v2(A2T_t[:, :], 8)
        cfwT = v2(cfwT_t[:, :], 3)

        bank_i = [0]
        cur_pool = [None]

        def nb():
            pl = cur_pool[0]
            if pl is not None:
                b = ps[pl["l"][pl["i"]]]
                pl["i"] = (pl["i"] + 1) % len(pl["l"])
                return b[:, :]
            b = ps[bank_i[0]]
            bank_i[0] = (bank_i[0] + 1) % 8
            return b[:, :]

        def record(fns, pool):
            assert P.rec is None
            P.rec = []
            cur_pool[0] = pool
            for f in fns:
                f()
            out = P.rec
            P.rec = None
            cur_pool[0] = None
            return out

        def merge_replay(A, B):
            na, nb_ = len(A), len(B)
            ia = ib = 0
            while ia < na or ib < nb_:
                if ib >= nb_ or (ia < na and ia * nb_ <= ib * na):
                    P.replay(A[ia])
                    ia += 1
                else:
                    P.replay(B[ib])
                    ib += 1

        def mm(out, lhsT, rhs, start=True, stop=True):
            P.op("pe", "matmul", out=out, lhsT=lhsT, rhs=rhs, start=start, stop=stop)

        def tr(out, in_, identity):
            P.op("pe", "transpose", out=out, in_=in_, identity=identity)

        def act(out, in_, func, **kw):
            P.op("act", "activation", out=out, in_=in_, func=func, **kw)

        def tt(out, in0, in1, op, eng="dve"):
            P.op(eng, "tensor_tensor", out=out, in0=in0, in1=in1, op=op)

        def ts(out, in0, s1, s2, op0, op1=None, eng="dve"):
            if op1 is None:
                P.op(eng, "tensor_scalar", out=out, in0=in0, scalar1=s1, scalar2=None, op0=op0)
            else:
                P.op(eng, "tensor_scalar", out=out, in0=in0, scalar1=s1, scalar2=s2, op0=op0, op1=op1)

        def stt(out, in0, scalar, in1, op0, op1, eng="dve"):
            P.op(eng, "scalar_tensor_tensor", out=out, in0=in0, scalar=scalar, in1=in1, op0=op0, op1=op1)

        def cp(out, in_, eng="dve"):
            if eng == "act":
                P.op("act", "activation", out=out, in_=in_, func=AF.Copy)
            else:
                P.op(eng, "tensor_copy", out=out, in_=in_)

        def mset(ap, val, eng="dve"):
            P.op(eng, "memset", ap=ap, constant=val)

        def rsqrt_mean(out, ss, scale, n):
            act(out, ss, AF.Ln, scale=scale, bias=EPS)
            act(out, out, AF.Exp, scale=-0.5)

        def bc(ap, shape):
            return ap.to_broadcast(list(shape))

        P.dma("sp", out=cst[:, :], in_=consts[:, :])
        for i in range(4):
            P.dma("sp", out=X[:, 4 * i:4 * i + 4, :],
                  in_=xp.rearrange("(t p) d -> p t d", p=128)[:, 4 * i:4 * i + 4, :])
        P.dma("sp", out=Xs[0:16, :], in_=xs[:, :])
        cp(identb, identF)
        cp(onesb_t[:, :], ones)
        P.dma("sp", out=alB[:, :], in_=a_log.rearrange("l h -> (l h)").partition_broadcast(128))
        P.dma("sp", out=dtB[:, :], in_=dt_bias.rearrange("l h -> (l h)").partition_broadcast(128))
        act(nexpA[:, :], alB[:, :], AF.Exp)
        ts(nexpA[:, :], nexpA[:, :], -1.0, None, ALU.mult)
        CV.reset()
        craw = CV.f32(1024)
        n1raw = CV.f32(128)
        n2raw = CV.f32(128)
        P.dma("sp", out=craw[0:17, :], in_=call[:, :])
        P.dma("sp", out=n1raw[0:32, :], in_=norm1.rearrange("l (c p) -> (l c) p", p=128))
        P.dma("sp", out=n2raw[0:32, :], in_=norm2.rearrange("l (c p) -> (l c) p", p=128))
        act(craw[0:17, :], craw[0:17, :], AF.Silu)
        b = nb()
        for k in range(8):
            mm(b[:, k * 17:(k + 1) * 17], craw[0:17, k * 128:(k + 1) * 128], identF[0:17, 0:17])
        cp(cT_t[:, :], b[:, 0:136])
        b = nb()
        mm(b[:, 0:32], n1raw[0:32, :], identF[0:32, 0:32])
        mm(b[:, 32:64], n2raw[0:32, :], identF[0:32, 0:32])
        cp(nT1[:, :], b[:, 0:32])
        cp(nT2[:, :], b[:, 32:64])
        dump("n1raw", n1raw[0:32, :])
        dump("nT1a", nT1[:, :])
        dump("nT2a", nT2[:, :])
        dump("craw", craw[0:17, :])

        def sm(a, b_, n=128):
            return SM[0:n, a:b_]

        def mod_phase(l):
            CV.reset()
            wm = [v2(CV.bf16(8 * 512), 8) for _ in range(2)]
            bmraw = CV.f32(128)
            mod_ops(l, wm, bmraw, 512)

        def mod_ops(l, wm, bmraw, ncols):
            sub = ncols // 128
            P.dma("sp", out=bmraw[0:48, :], in_=b_mod[l].rearrange("(c p) -> c p", p=128))
            b = nb()
            mm(b[:, 0:48], bmraw[0:48, :], identF[0:48, 0:48])
            cp(bmT[:, :], b[:, 0:48])
            wsrc = w_mod[l].rearrange("(k p) n -> p k n", p=128)
            for c in range(6 * D // ncols):
                w_ = wm[c % 2]
                P.dma("pool", out=w_, in_=wsrc[:, :, c * ncols:(c + 1) * ncols])
                b = nb()
                for j in range(sub):
                    for k in range(8):
                        mm(b[:, j * 17:(j + 1) * 17], w_[:, k, j * 128:(j + 1) * 128], cT[:, k, :],
                           start=(k == 0), stop=(k == 7))
                tt(modT[:, sub * c:sub * c + sub, :], v2(b[:, 0:17 * sub], sub),
                   bc(bmT[:, sub * c:sub * c + sub].unsqueeze(2), [128, sub, 17]), ALU.add)
            for (AT, c0, nT) in ((A1T, 8, nT1), (A2T, 32, nT2)):
                ts(AT, modT[:, c0:c0 + 8, :], 1.0, None, ALU.add)
                tt(AT, AT, bc(nT[:, l * 8:l * 8 + 8].unsqueeze(2), [128, 8, 17]), ALU.mult)

        def make_g(c0, repbuf, gB):
            rep = v2(repbuf, 8)
            cp(rep, bc(modT[:, c0:c0 + 8, 0:1], [128, 8, 128]))
            for half in range(2):
                b = nb()
                for kk in range(4):
                    k = half * 4 + kk
                    mm(b[:, kk * 128:(kk + 1) * 128], rep[:, k, :], identF)
                cp(gB[:, half * 512:(half + 1) * 512], b, eng="act")
                b = nb()
                for kk in range(4):
                    k = half * 4 + kk
                    mm(b[0:16, kk * 128:(kk + 1) * 128], modT[:, c0 + k, 1:17], identF)
                cp(gS[0:16, half * 512:(half + 1) * 512], b[0:16, :], eng="act")

        def norm_transpose(xin, n, xn, AT, shc0, dstT, tmpA, col):
            ss = sm(0, 1, n)
            rstd = sm(1, 2, n)
            mset(ss, 0.0)
            act(tmpA[0:n, :], xin, AF.Square, accum_out=ss)
            rsqrt_mean(rstd, ss, 1.0 / D, n)
            ts(xn[0:n, :], xin, rstd, None, ALU.mult)
            b = nb().bitcast(BF16)
            for k in range(8):
                tr(b[:, k * n:(k + 1) * n], xn[0:n, k * 128:(k + 1) * 128], identb[0:n, 0:n])
            bv = v2(b[:, 0:8 * n], 8)
            tv = v2(tmpA[:, 0:8 * n], 8)
            if n == 128:
                Aap = bc(AT[:, :, 0:1], [128, 8, 128])
                Sap = bc(modT[:, shc0:shc0 + 8, 0:1], [128, 8, 128])
            else:
                Aap = AT[:, :, 1:17]
                Sap = modT[:, shc0:shc0 + 8, 1:17]
            tt(tv, bv, Aap, ALU.mult)
            tt(dstT[:, :, col:col + n], tv, Sap, ALU.add)

        def mixer_phase(l):
            CV.reset()
            w_in_sb = v2(CV.bf16(8 * IN), 8)
            w_out_sb = v2(CV.bf16(8 * D), 8)
            xn = CV.bf16(1024)
            hT = v2(CV.bf16(1024), 8)
            qkvx = CV.f32(524)
            qcar = v2(CV.f32(36), 12)
            qkn = v2(CV.f32(1024), 8)
            vtok = v2(CV.f32(512), 4)
            ktok = v2(CV.f32(512), 4)
            sz = CV.f32(512)
            u_ = v2(CV.f32(512), 4)
            TmAll = CV.f32(2048)
            Tm = [v2(TmAll[:, 512 * i:512 * (i + 1)], 4) for i in range(4)]
            tmpA = TmAll[:, 0:1024]
            NMall = CV.f32(2048)
            NMr = [NMall[:, 0:1024], NMall[:, 1024:2048]]
            acc = v2(NMall[:, 0:1536], 12)
            PQ = CV.f32(1536)
            stq_buf = PQ
            PPr = [v2(PQ[:, 0:512], 4), v2(PQ[:, 512:1024], 4)]
            QKmT = v2(PQ[:, 1024:1536], 4)
            print("mixer arena used", CV.off, "of", CV.n)
            o_ = v2(CV.f32(512), 4)
            S_ = v2(CV.f32(512), 4)
            WmT = v2(CV.f32(512), 4)
            mix = xn
            mixT = hT

            wsrc = w_in[l].rearrange("(k p) n -> p k n", p=128)
            for k in range(8):
                P.dma("pool", out=w_in_sb[:, k, :], in_=wsrc[:, k, :])
            wsrc2 = w_out[l].rearrange("(k p) n -> p k n", p=128)
            for k2 in range(2):
                P.dma("pool", out=w_out_sb[:, 4 * k2:4 * k2 + 4, :], in_=wsrc2[:, 4 * k2:4 * k2 + 4, :])

            make_g(16, tmpA, None)
            craw = Tm[0]
            cr = craw.rearrange("p a b -> p (a b)")
            P.dma("sp", out=cr[0:48, 0:128], in_=conv_qkv[l].rearrange("j (c p) -> (j c) p", p=128))
            P.dma("sp", out=cr[0:4, 128:256], in_=b_sgu[l])
            b = nb()
            mm(b[:, 0:48], cr[0:48, 0:128], identF[0:48, 0:48])
            mm(b[:, 64:68], cr[0:4, 128:256], identF[0:4, 0:4])
            cp(cwT[:, :], b[:, 0:48])
            cp(bsT[:, :], b[:, 64:68])
            P.dma("sp", out=sgnB[:, :], in_=sgu_norm[l].rearrange("h d -> (h d)").partition_broadcast(128))
            P.dma("sp", out=gdnB[:, :], in_=gdn_norm[l].partition_broadcast(128))
            mset(qcar, 0.0)
            Wraw = Tm[1]
            P.dma("sp", out=Wraw, in_=w_sgu[l].rearrange("h i j -> i h j"))
            w00 = sm(100, 104, 16)
            b00 = sm(104, 108, 16)
            b = nb()
            mm(b[0:16, 0:4], e0, Wraw[:, :, 0])
            mm(b[0:16, 4:8], e0, bsT[:, :])
            cp(SM[0:16, 100:108], b[0:16, 0:8])
            tt(Wraw, Wraw, bc(tril.unsqueeze(1), [128, 4, 128]), ALU.mult)
            b = nb()
            for h in range(4):
                tr(b[:, h * 128:(h + 1) * 128], Wraw[:, h, :], identF)
            cp(WmT, v2(b, 4))
            mset(S_, 0.0)

            nA = nexpA[:, l * 4:l * 4 + 4]
            dB = dtB[:, l * 4:l * 4 + 4]

            def tile(t, sample):
                n = 16 if sample else 128
                xin = Xs[0:16, :] if sample else X[:, t, :]
                tg_ = ("stile_" if sample else "tile_")

                def tck(x):
                    if l == 0 and t == 0:
                        ckpt(tg_ + x)
                norm_transpose(xin, n, xn, A1T, 0, hT, tmpA, 0)
                dump("hT", hT.rearrange("p a b -> p (a b)"))
                tck("a")
                if sample:
                    QX = v2(qkvx[:, 0:256], 4)

                    def tap(j):
                        return QX[:, :, 16 * j:16 * j + 16]
                    cur = QX[:, :, 48:64]
                    P.dma("sp", out=qs[l][:, 0:2, :], in_=sq[l][:, 1:3, :])
                else:
                    QX = v2(qkvx[:, 0:524], 4)

                    def tap(j):
                        return QX[:, :, j:j + 128]
                    cur = QX[:, :, 3:131]
                for grp in range(3):
                    c4 = slice(4 * grp, 4 * grp + 4)
                    b = nb()
                    for cc in range(4):
                        c = grp * 4 + cc
                        for k in range(8):
                            mm(b[:, cc * n:(cc + 1) * n], w_in_sb[:, k, c * 128:(c + 1) * 128], hT[:, k, 0:n],
                               start=(k == 0), stop=(k == 7))
                    cp(cur, v2(b[:, 0:4 * n], 4), eng="act")
                    if sample:
                        stq = v2(stq_buf[0:16, :], 3)
                        P.dma("sp", out=stq, in_=sq[l][:, :, grp * 512:(grp + 1) * 512])
                        b2 = nb()
                        for cc in range(4):
                            for j in range(3):
                                idx = cc * 3 + j
                                mm(b2[:, idx * 16:(idx + 1) * 16], stq[:, j, cc * 128:(cc + 1) * 128],
                                   identF[0:16, 0:16])
                        cp(QX[:, :, 0:48], v2(b2[:, 0:192], 4), eng="act")
                        b3 = nb()
                        for k in range(8):
                            mm(b3[0:16, :], hT[:, k, 0:16], w_in_sb[:, k, grp * 512:(grp + 1) * 512],
                               start=(k == 0), stop=(k == 7))
                        qsn = Tm[2].rearrange("p a b -> p (a b)")
                        cp(qsn[0:16, :], b3[0:16, :])
                        P.dma("sp", out=qs[l][:, 2, grp * 512:(grp + 1) * 512], in_=qsn[0:16, :])
                    else:
                        cp(QX[:, :, 0:3], qcar[:, c4, :])
                    ag = acc[:, c4, 0:n]
                    tg = Tm[3][:, :, 0:n]

                    def cw(j):
                        return bc(cwT[:, j * 12 + 4 * grp:j * 12 + 4 * grp + 4].unsqueeze(2), [128, 4, n])
                    tt(ag, tap(3), cw(3), ALU.mult)
                    for j in (2, 1, 0):
                        tt(tg, tap(j), cw(j), ALU.mult)
                        tt(ag, ag, tg, ALU.add)
                    if not sample:
                        cp(qcar[:, c4, :], QX[:, :, 128:131])
                if (not sample) and t == NT - 1:
                    b = nb()
                    for j in range(3):
                        mm(b[0:12, j * 128:(j + 1) * 128], qcar[:, :, j], identF)
                    qtail = Tm[2].rearrange("p a b -> p (a b)")
                    cp(qtail[0:12, 0:384], b[0:12, 0:384])
                    P.dma("sp", out=qp[l].rearrange("j (c p) -> c j p", p=128), in_=v2(qtail[0:12, 0:384], 3))
                QC = acc[:, :, 0:n]
                act(QC, QC, AF.Silu)
                dump("QC", acc.rearrange("p a b -> p (a b)"))
                tck("b")
                sqb = v2(tmpA[:, 0:8 * n], 8)
                tt(sqb, QC[:, 0:8, :], QC[:, 0:8, :], ALU.mult)
                for half in range(2):
                    b = nb()
                    mm(b[:, 0:4 * n], ones, sqb[:, 4 * half:4 * half + 4, :])
                    rsv = sqb[:, 4 * half:4 * half + 4, :]
                    act(rsv, v2(b[:, 0:4 * n], 4), AF.Ln, bias=EPS)
                    act(rsv, rsv, AF.Exp, scale=-0.5, bias=(-0.5 * math.log(128.0) if half == 0 else 0.0))
                tt(qkn[:, :, 0:n], QC[:, 0:8, :], sqb, ALU.mult)
                dump("qkn", qkn.rearrange("p a b -> p (a b)"))
                tck("c")
                b = nb()
                for h in range(4):
                    tr(b[0:n, h * 128:(h + 1) * 128], QC[:, 8 + h, :], identF)
                cp(vtok[0:n], v2(b[0:n, :], 4), eng="act")
                b = nb()
                for h in range(4):
                    tr(b[0:n, h * 128:(h + 1) * 128], qkn[:, 4 + h, 0:n], identF)
                cp(ktok[0:n], v2(b[0:n, :], 4), eng="act")
                bz, bu, bv_, bb = nb(), nb(), nb(), nb()
                for (bk, c0, nn) in ((bz, 1536, 512), (bb, 2048, 8), (bu, 2056, 512), (bv_, 2568, 512)):
                    for k in range(8):
                        mm(bk[0:n, 0:nn], hT[:, k, 0:n], w_in_sb[:, k, c0:c0 + nn], start=(k == 0), stop=(k == 7))
                act(sz[0:n, :], bz[0:n, :], AF.Silu)
                uf = u_.rearrange("p a b -> p (a b)")
                act(uf[0:n, :], bu[0:n, :], AF.Gelu_apprx_tanh)
                vg = Tm[0]
                vgf = vg.rearrange("p a b -> p (a b)")
                act(vgf[0:n, :], bv_[0:n, :], AF.Gelu_apprx_tanh)
                ba = sm(8, 16, n)
                cp(ba, bb[0:n, 0:8])
                beta = sm(16, 20, n)
                act(beta, SM[0:n, 8:12], AF.Sigmoid)
                xa, ab_, ee, mx, g_ = sm(20, 24, n), sm(24, 28, n), sm(28, 32, n), sm(32, 36, n), sm(36, 40, n)
                tt(xa, SM[0:n, 12:16], dB[0:n, :], ALU.add)
                ts(ab_, xa, -1.0, None, ALU.mult)
                tt(ab_, ab_, xa, ALU.max)
                act(ee, ab_, AF.Exp, scale=-1.0)
                act(ee, ee, AF.Ln, bias=1.0)
                ts(mx, xa, 0.0, None, ALU.max)
                tt(mx, mx, ee, ALU.add)
                tt(g_, mx, nA[0:n, :], ALU.mult)
                dump("vtok", vtok.rearrange("p a b -> p (a b)"))
                dump("ktok", ktok.rearrange("p a b -> p (a b)"))
                dump("sz", sz)
                dump("u", u_.rearrange("p a b -> p (a b)"))
                dump("SMd", SM[:, :])
                tck("d")
                sqv = Tm[1]
                tt(sqv[0:n], vg[0:n], vg[0:n], ALU.mult)
                ssv = sm(80, 84, n)
                P.op("dve", "reduce_sum", out=ssv, in_=sqv[0:n], axis=AX.X)
                tck("e1")
                rv = sm(84, 88, n)
                rsqrt_mean(rv, ssv, 1.0 / 128, n)
                vv = Tm[1]
                tt(vv[0:n], vg[0:n], bc(rv.unsqueeze(2), [n, 4, 128]), ALU.mult)
                tt(vv[0:n], vv[0:n], v2(sgnB[0:n, :], 4), ALU.mult)
                tck("e2")
                mixv = v2(mix, 8)
                if sample:
                    P.dma("sp", out=sv[l], in_=vv[0:16].rearrange("p a b -> p (a b)"))
                    zt = Tm[2]
                    tt(zt[0:16], vv[0:16], bc(w00.unsqueeze(2), [16, 4, 128]), ALU.mult)
                    tt(zt[0:16], zt[0:16], bc(b00.unsqueeze(2), [16, 4, 128]), ALU.add)
                    tt(mixv[0:16, 4:8, :], zt[0:16], u_[0:16], ALU.mult)
                else:
                    b = nb()
                    for h in range(4):
                        mm(b[:, h * 128:(h + 1) * 128], WmT[:, h, :], vv[:, h, :])
                    tck("e3")
                    for h in range(4):
                        stt(mixv[:, 4 + h, :], b[:, h * 128:(h + 1) * 128], bsT[:, h:h + 1], u_[:, h, :],
                            ALU.add, ALU.mult)
                dump("vv", Tm[1].rearrange("p a b -> p (a b)"))
                tck("e")
                if sample:
                    delta_sample(l, g_, beta)
                else:
                    delta_prompt(g_, beta)
                dump("o", o_.rearrange("p a b -> p (a b)"))
                dump("S", S_.rearrange("p a b -> p (a b)"))
                tck("f")
                sqo = Tm[0]
                tt(sqo[0:n], o_[0:n], o_[0:n], ALU.mult)
                sso = sm(88, 92, n)
                P.op("dve", "reduce_sum", out=sso, in_=sqo[0:n], axis=AX.X)
                ro = sm(92, 96, n)
                rsqrt_mean(ro, sso, 1.0 / 128, n)
                on = Tm[0]
                tt(on[0:n], o_[0:n], bc(ro.unsqueeze(2), [n, 4, 128]), ALU.mult)
                tt(on[0:n], on[0:n], bc(gdnB[0:n, :].unsqueeze(1), [n, 4, 128]), ALU.mult)
                tt(mixv[0:n, 0:4, :], on[0:n], v2(sz[0:n, :], 4), ALU.mult)
                dump("mix", mix)
                tck("g")
                b = nb().bitcast(BF16)
                for k in range(8):
                    tr(b[:, k * n:(k + 1) * n], mix[0:n, k * 128:(k + 1) * 128], identb[0:n, 0:n])
                cp(mixT[:, :, 0:n], v2(b[:, 0:8 * n], 8), eng="act")
                gsrc = gS if sample else gB
                for half in range(2):
                    b = nb()
                    for k in range(8):
                        mm(b[0:n, :], mixT[:, k, 0:n], w_out_sb[:, k, half * 512:(half + 1) * 512],
                           start=(k == 0), stop=(k == 7))
                    hs = slice(half * 512, (half + 1) * 512)
                    tt(tmpA[0:n, hs], b[0:n, :], gsrc[0:n, hs], ALU.mult)
                    tt(xin[:, hs], xin[:, hs], tmpA[0:n, hs], ALU.add)

            def delta_prompt(g_, beta):
                gc = sm(40, 56)
                ex = sm(56, 72)
                gsel = sm(72, 80)
                bg = nb()
                mm(bg[:, 0:4], TRIinc, g_)
                mm(bg[:, 4:8], TRIsu, g_)
                tt(v2(gsel, 4), bc(g_.unsqueeze(2), [128, 4, 2]), bc(cm.unsqueeze(1), [128, 4, 2]), ALU.mult)
                mm(bg[:, 8:16], ones, gsel)
                cp(gc, bg[:, 0:16])
                act(ex, gc, AF.Exp)
                gcum = SM[:, 40:44]
                eg = SM[:, 56:60]
                ekd = SM[:, 60:64]
                egl = SM[:, 64:72]
                gBk = Tm[2]
                cp(gBk, bc(g_.unsqueeze(2), [128, 4, 128]))
                bG = nb()
                for h in range(4):
                    mm(bG[:, h * 128:(h + 1) * 128], gBk[:, h, :], TRIinc)
                xd = Tm[3]
                for h in range(4):
                    ts(xd[:, h, :], bG[:, h * 128:(h + 1) * 128], gcum[:, h:h + 1], 0.0, ALU.subtract, ALU.max)
                act(xd, xd, AF.Exp, scale=-1.0)
                decL = Tm[2]
                decQ = o_
                tt(decL, xd, bc(maskL.unsqueeze(1), [128, 4, 128]), ALU.mult)
                tt(decQ, xd, bc(maskQ.unsqueeze(1), [128, 4, 128]), ALU.mult)
                bK = nb()
                bQ = nb()
                for h in range(4):
                    mm(bK[:, h * 128:(h + 1) * 128], qkn[:, 4 + h, :], qkn[:, 4 + h, :])
                for h in range(4):
                    mm(bQ[:, h * 128:(h + 1) * 128], qkn[:, h, :], qkn[:, 4 + h, :])
                NM = [v3(r, 2, 4) for r in NMr]
                N1 = NM[0][:, 0]
                M1 = NM[0][:, 1]
                for h in range(4):
                    stt(N1[:, h, :], bK[:, h * 128:(h + 1) * 128], beta[:, h:h + 1], decL[:, h, :], ALU.mult, ALU.mult)
                QKm = Tm[3]
                tt(QKm, v2(bQ, 4), decQ, ALU.mult)
                b1 = nb()
                b2 = nb()
                for h in range(4):
                    tr(b1[:, h * 128:(h + 1) * 128], N1[:, h, :], identF)
                for h in range(4):
                    tr(b2[:, h * 128:(h + 1) * 128], QKm[:, h, :], identF)
                cp(M1, v2(b1, 4), eng="act")
                cp(QKmT, v2(b2, 4), eng="act")
                Pc, Pn = PPr[0], PPr[1]
                tt(Pc, bc(identF.unsqueeze(1), [128, 4, 128]), M1, ALU.subtract)
                cur = 0
                for s in range(5):
                    Nc, Mc = NM[cur][:, 0], NM[cur][:, 1]
                    Nn, Mn = NM[1 - cur][:, 0], NM[1 - cur][:, 1]
                    bN = nb()
                    for h in range(4):
                        mm(bN[:, h * 128:(h + 1) * 128], Mc[:, h, :], Nc[:, h, :])
                    if s < 4:
                        bM = nb()
                        for h in range(4):
                            mm(bM[:, h * 128:(h + 1) * 128], Nc[:, h, :], Mc[:, h, :])
                    cp(Nn, v2(bN, 4), eng="act")
                    if s < 4:
                        cp(Mn, v2(bM, 4))
                    bP = nb()
                    for h in range(4):
                        mm(bP[:, h * 128:(h + 1) * 128], Nn[:, h, :], Pc[:, h, :])
                    tt(Pn, v2(bP, 4), Pc, ALU.add)
                    Pc, Pn = Pn, Pc
                    cur = 1 - cur
                TT = Pc
                kdec = ktok
                tt(kdec, ktok, bc(ekd.unsqueeze(2), [128, 4, 128]), ALU.mult)
                r2, rhs2, vnew, otmp = Tm[0], Tm[1], Tm[2], Tm[3]
                for c in range(2):
                    rs_ = slice(64 * c, 64 * c + 64)
                    bKS = nb()
                    bQS = nb()
                    for h in range(4):
                        mm(bKS[:, h * 128:(h + 1) * 128], qkn[:, 4 + h, :], S_[:, h, :])
                    for h in range(4):
                        mm(bQS[:, h * 128:(h + 1) * 128], qkn[:, h, :], S_[:, h, :])
                    for h in range(4):
                        stt(r2[rs_, h, :], bKS[rs_, h * 128:(h + 1) * 128], eg[rs_, h:h + 1], vtok[rs_, h, :],
                            ALU.mult, ALU.subtract)
                    for h in range(4):
                        ts(rhs2[rs_, h, :], r2[rs_, h, :], beta[rs_, h:h + 1], -1.0, ALU.mult, ALU.mult)
                    bV = nb()
                    for h in range(4):
                        mm(bV[:, h * 128:(h + 1) * 128], TT[rs_, h, :], rhs2[rs_, h, :])
                    cp(vnew[rs_], v2(bV, 4)[rs_], eng="act")
                    bO = nb()
                    for h in range(4):
                        mm(bO[:, h * 128:(h + 1) * 128], QKmT[rs_, h, :], vnew[rs_, h, :])
                    cp(otmp[rs_], v2(bO, 4)[rs_], eng="act")
                    for h in range(4):
                        stt(o_[rs_, h, :], bQS[rs_, h * 128:(h + 1) * 128], eg[rs_, h:h + 1], otmp[rs_, h, :],
                            ALU.mult, ALU.add)
                    bS = nb()
                    for h in range(4):
                        mm(bS[:, h * 128:(h + 1) * 128], kdec[rs_, h, :], vnew[rs_, h, :])
                    for h in range(4):
                        stt(S_[:, h, :], S_[:, h, :], egl[:, 2 * h + c:2 * h + c + 1], bS[:, h * 128:(h + 1) * 128],
                            ALU.mult, ALU.add)

            def delta_sample(l, g_, beta):
                egs = sm(56, 60, 16)
                act(egs, g_, AF.Exp)
                b = nb()
                for h in range(4):
                    tr(b[0:16, h * 128:(h + 1) * 128], qkn[:, h, 0:16], identF)
                qtok = Tm[0]
                cp(qtok[0:16], v2(b[0:16, :], 4), eng="act")
                tt(qtok[0:16], qtok[0:16], ktok[0:16], ALU.mult)
                qk = sm(108, 112, 16)
                P.op("dve", "reduce_sum", out=qk, in_=qtok[0:16], axis=AX.X)
                gdiag = Tm[1].rearrange("p a b -> p (a b)")[0:16, 0:64]
                tt(v2(gdiag, 16), bc(g_.unsqueeze(1), [16, 16, 4]),
                   bc(identF[0:16, 0:16].unsqueeze(2), [16, 16, 4]), ALU.mult)
                b = nb()
                mm(b[:, 0:64], ones[0:16, :], gdiag)
                EGb = sm(128, 192)
                act(EGb, b[:, 0:64], AF.Exp)
                r2, vnew, otmp = Tm[2], Tm[3], PPr[0]
                kqm = Tm[1].rearrange("p a b -> p (a b)")
                for h in range(4):
                    SA = v2(NMall, 16)
                    P.dma("sp", out=SA, in_=sd[l][:, h].rearrange("b k v -> k b v"))
                    kTm = v2(kqm[:, 0:256], 16)
                    qTm = v2(kqm[:, 256:512], 16)
                    tt(kTm, bc(qkn[:, 4 + h, 0:16].unsqueeze(1), [128, 16, 16]), i16rep, ALU.mult)
                    tt(qTm, bc(qkn[:, h, 0:16].unsqueeze(1), [128, 16, 16]), i16rep, ALU.mult)
                    bKS = nb()
                    bQS = nb()
                    for bb_ in range(16):
                        mm(bKS[0:16, 0:128], kTm[:, bb_, :], SA[:, bb_, :], start=(bb_ == 0), stop=(bb_ == 15))
                    for bb_ in range(16):
                        mm(bQS[0:16, 0:128], qTm[:, bb_, :], SA[:, bb_, :], start=(bb_ == 0), stop=(bb_ == 15))
                    stt(r2[0:16, h, :], bKS[0:16, 0:128], egs[:, h:h + 1], vtok[0:16, h, :], ALU.mult, ALU.subtract)
                    ts(vnew[0:16, h, :], r2[0:16, h, :], beta[:, h:h + 1], -1.0, ALU.mult, ALU.mult)
                    ts(otmp[0:16, h, :], vnew[0:16, h, :], qk[:, h:h + 1], None, ALU.mult)
                    stt(o_[0:16, h, :], bQS[0:16, 0:128], egs[:, h:h + 1], otmp[0:16, h, :], ALU.mult, ALU.add)
                    vmr = Tm[0].rearrange("p a b -> p (a b)")
                    for q4 in range(4):
                        bS = nb()
                        for j in range(4):
                            bb_ = q4 * 4 + j
                            vm = vmr[0:16, j * 128:(j + 1) * 128]
                            ts(vm, vnew[0:16, h, :], identF[0:16, bb_:bb_ + 1], None, ALU.mult)
                            mm(bS[:, j * 128:(j + 1) * 128], ktok[0:16, h, :], vm)
                        for j in range(4):
                            bb_ = q4 * 4 + j
                            stt(SA[:, bb_, :], SA[:, bb_, :], SM[:, 128 + bb_ * 4 + h:128 + bb_ * 4 + h + 1],
                                bS[:, j * 128:(j + 1) * 128], ALU.mult, ALU.add)
                    P.dma("sp", out=ds[l][:, h].rearrange("b k v -> k b v"), in_=SA)

            ckpt("mix%d_params" % l)
            for t in range(NT):
                tile(t, False)
                ckpt("mix%d_t%d" % (l, t))
            P.dma("sp", out=dp[l].rearrange("h k v -> k h v"), in_=S_)
            ckpt("mix%d_dp" % l)
            tile(0, True)

        def mixer_phase2(l):
            CV.reset()
            WBLK = [(0, 512), (512, 512), (1024, 512), (1536, 512), (2048, 8), (2056, 512), (2568, 512)]
            w_in_blk = [v2(CV.bf16(8 * wn), 8) for (_, wn) in WBLK]

            def wcol(k, c0, n_):
                for bi, (b0, wn) in enumerate(WBLK):
                    if b0 <= c0 and c0 + n_ <= b0 + wn:
                        return w_in_blk[bi][:, k, c0 - b0:c0 - b0 + n_]
                raise AssertionError((c0, n_))
            w_out_sb = v2(CV.bf16(8 * D), 8)
            xn = CV.bf16(1024)
            hT = v2(CV.bf16(1024), 8)
            mixT = v2(xn, 8)
            qkvx = CV.f32(524)
            qcar = v2(CV.f32(36), 12)
            FA = CV.f32(2560)
            acc = v2(FA[:, 0:1536], 12)
            FT = FA[:, 1536:2560]
            qkn = v2(CV.f32(1024), 8)
            vtok = v2(CV.f32(512), 4)
            ktok = v2(CV.f32(512), 4)
            szb = [CV.f32(512) for _ in range(2)]
            u_ = v2(CV.f32(512), 4)
            mixb = [CV.bf16(1024) for _ in range(2)]
            TmAll = CV.f32(1536)
            Tm = [v2(TmAll[:, 512 * i:512 * (i + 1)], 4) for i in range(3)]
            tmpA = TmAll[:, 0:1024]
            NPQ = CV.f32(2048)
            NP = NPQ[:, 0:1536]
            NM = v3(NP[:, 0:1024], 2, 4)
            Pm = v2(NP[:, 1024:1536], 4)
            stq_buf = NP
            QKmT = v2(NPQ[:, 1536:2048], 4)
            o_ = v2(CV.f32(512), 4)
            S_ = v2(CV.f32(512), 4)
            WmT = v2(CV.f32(512), 4)
            print("mixer2 arena used", CV.off, "of", CV.n)

            wsrc = w_in[l].rearrange("(k p) n -> p k n", p=128)
            for bi, (b0, wn) in enumerate(WBLK):
                P.dma("pool", out=w_in_blk[bi], in_=wsrc[:, :, b0:b0 + wn])
            wsrc2 = w_out[l].rearrange("(k p) n -> p k n", p=128)
            for k2 in range(2):
                P.dma("pool", out=w_out_sb[:, 4 * k2:4 * k2 + 4, :], in_=wsrc2[:, 4 * k2:4 * k2 + 4, :])

            gBt = FA[:, 0:1024]
            make_g(16, FA[:, 1024:2048], gBt)
            cr = Tm[0].rearrange("p a b -> p (a b)")
            P.dma("sp", out=cr[0:48, 0:128], in_=conv_qkv[l].rearrange("j (c p) -> (j c) p", p=128))
            P.dma("sp", out=cr[0:4, 128:256], in_=b_sgu[l])
            b = nb()
            mm(b[:, 0:48], cr[0:48, 0:128], identF[0:48, 0:48])
            mm(b[:, 64:68], cr[0:4, 128:256], identF[0:4, 0:4])
            cp(cwT[:, :], b[:, 0:48])
            cp(bsT[:, :], b[:, 64:68])
            P.dma("sp", out=sgnB[:, :], in_=sgu_norm[l].rearrange("h d -> (h d)").partition_broadcast(128))
            P.dma("sp", out=gdnB[:, :], in_=gdn_norm[l].partition_broadcast(128))
            mset(qcar, 0.0)
            Wraw = Tm[1]
            P.dma("sp", out=Wraw, in_=w_sgu[l].rearrange("h i j -> i h j"))
            w00 = sm(100, 104, 16)
            b00 = sm(104, 108, 16)
            b = nb()
            mm(b[0:16, 0:4], e0, Wraw[:, :, 0])
            mm(b[0:16, 4:8], e0, bsT[:, :])
            cp(SM[0:16, 100:108], b[0:16, 0:8])
            tt(Wraw, Wraw, bc(tril.unsqueeze(1), [128, 4, 128]), ALU.mult)
            b = nb()
            for h in range(4):
                tr(b[:, h * 128:(h + 1) * 128], Wraw[:, h, :], identF)
            cp(WmT, v2(b, 4))

            nA = nexpA[:, l * 4:l * 4 + 4]
            dB = dtB[:, l * 4:l * 4 + 4]

            def slots(par, n):
                beta = SM[0:n, 16:20] if par == 0 else SM[0:n, 112:116]
                g_ = SM[0:n, 36:40] if par == 0 else SM[0:n, 116:120]
                return beta, g_

            def front(t, sample):
                n = 16 if sample else 128
                par = t % 2
                xin = Xs[0:16, :] if sample else X[:, t, :]
                sz = szb[par]
                mix = mixb[par]
                mixv = v2(mix, 8)
                beta, g_ = slots(par, n)
                if sample:
                    QX = v2(qkvx[:, 0:256], 4)

                    def tap(j):
                        return QX[:, :, 16 * j:16 * j + 16]
                    cur = QX[:, :, 48:64]
                else:
                    QX = v2(qkvx[:, 0:524], 4)

                    def tap(j):
                        return QX[:, :, j:j + 128]
                    cur = QX[:, :, 3:131]
                QC = acc[:, :, 0:n]

                def F1():
                    norm_transpose(xin, n, xn, A1T, 0, hT, FT, 0)
                    if sample:
                        P.dma("sp", out=qs[l][:, 0:2, :], in_=sq[l][:, 1:3, :])

                def Fq(grp):
                    c4 = slice(4 * grp, 4 * grp + 4)
                    b = nb()
                    for cc in range(4):
                        c = grp * 4 + cc
                        for k in range(8):
                            mm(b[:, cc * n:(cc + 1) * n], wcol(k, c * 128, 128), hT[:, k, 0:n],
                               start=(k == 0), stop=(k == 7))
                    cp(cur, v2(b[:, 0:4 * n], 4), eng="act")
                    if sample:
                        stq = v2(stq_buf[0:16, :], 3)
                        P.dma("sp", out=stq, in_=sq[l][:, :, grp * 512:(grp + 1) * 512])
                        b2 = nb()
                        for cc in range(4):
                            for j in range(3):
                                idx = cc * 3 + j
                                mm(b2[:, idx * 16:(idx + 1) * 16], stq[:, j, cc * 128:(cc + 1) * 128],
                                   identF[0:16, 0:16])
                        cp(QX[:, :, 0:48], v2(b2[:, 0:192], 4), eng="act")
                        b3 = nb()
                        for k in range(8):
                            mm(b3[0:16, :], hT[:, k, 0:16], wcol(k, grp * 512, 512),
                               start=(k == 0), stop=(k == 7))
                        qsn = Tm[2].rearrange("p a b -> p (a b)")
                        cp(qsn[0:16, :], b3[0:16, :])
                        P.dma("sp", out=qs[l][:, 2, grp * 512:(grp + 1) * 512], in_=qsn[0:16, :])
                    else:
                        cp(QX[:, :, 0:3], qcar[:, c4, :])
                    ag = acc[:, c4, 0:n]
                    tg = v2(FT[:, 0:512], 4)[:, :, 0:n]

                    def cw(j):
                        return bc(cwT[:, j * 12 + 4 * grp:j * 12 + 4 * grp + 4].unsqueeze(2), [128, 4, n])
                    tt(ag, tap(3), cw(3), ALU.mult)
                    for j in (2, 1, 0):
                        tt(tg, tap(j), cw(j), ALU.mult)
                        tt(ag, ag, tg, ALU.add)
                    if not sample:
                        cp(qcar[:, c4, :], QX[:, :, 128:131])
                    act(ag, ag, AF.Silu)
                    if grp == 2 and (not sample) and t == NT - 1:
                        b = nb()
                        for j in range(3):
                            mm(b[0:12, j * 128:(j + 1) * 128], qcar[:, :, j], identF)
                        qtail = FT
                        cp(qtail[0:12, 0:384], b[0:12, 0:384])
                        P.dma("sp", out=qp[l].rearrange("j (c p) -> c j p", p=128), in_=v2(qtail[0:12, 0:384], 3))

                def F7():
                    bz, bu, bv_, bb = nb(), nb(), nb(), nb()
                    for (bk, c0, nn) in ((bb, 2048, 8), (bz, 1536, 512), (bu, 2056, 512), (bv_, 2568, 512)):
                        for k in range(8):
                            mm(bk[0:n, 0:nn], hT[:, k, 0:n], wcol(k, c0, nn), start=(k == 0), stop=(k == 7))
                    ba = sm(8, 16, n)
                    cp(ba, bb[0:n, 0:8])
                    act(sz[0:n, :], bz[0:n, :], AF.Silu)
                    uf = u_.rearrange("p a b -> p (a b)")
                    act(uf[0:n, :], bu[0:n, :], AF.Gelu_apprx_tanh)
                    act(FT[0:n, 0:512], bv_[0:n, :], AF.Gelu_apprx_tanh)
                    act(beta, SM[0:n, 8:12], AF.Sigmoid)
                    xa, ab_, ee, mx = sm(20, 24, n), sm(24, 28, n), sm(28, 32, n), sm(32, 36, n)
                    tt(xa, SM[0:n, 12:16], dB[0:n, :], ALU.add)
                    ts(ab_, xa, -1.0, None, ALU.mult)
                    tt(ab_, ab_, xa, ALU.max)
                    act(ee, ab_, AF.Exp, scale=-1.0)
                    act(ee, ee, AF.Ln, bias=1.0)
                    ts(mx, xa, 0.0, None, ALU.max)
                    tt(mx, mx, ee, ALU.add)
                    tt(g_, mx, nA[0:n, :], ALU.mult)

                def F8():
                    vg = v2(FT[:, 0:512], 4)
                    vv = v2(FT[:, 512:1024], 4)
                    tt(vv[0:n], vg[0:n], vg[0:n], ALU.mult)
                    ssv = sm(80, 84, n)
                    P.op("dve", "reduce_sum", out=ssv, in_=vv[0:n], axis=AX.X)
                    rv = sm(84, 88, n)
                    rsqrt_mean(rv, ssv, 1.0 / 128, n)
                    tt(vv[0:n], vg[0:n], bc(rv.unsqueeze(2), [n, 4, 128]), ALU.mult)
                    tt(vv[0:n], vv[0:n], v2(sgnB[0:n, :], 4), ALU.mult)
                    if sample:
                        P.dma("sp", out=sv[l], in_=vv[0:16].rearrange("p a b -> p (a b)"))
                        zt = Tm[2]
                        tt(zt[0:16], vv[0:16], bc(w00.unsqueeze(2), [16, 4, 128]), ALU.mult)
                        tt(zt[0:16], zt[0:16], bc(b00.unsqueeze(2), [16, 4, 128]), ALU.add)
                        tt(mixv[0:16, 4:8, :], zt[0:16], u_[0:16], ALU.mult)
                    else:
                        b = nb()
                        for h in range(4):
                            mm(b[:, h * 128:(h + 1) * 128], WmT[:, h, :], vv[:, h, :])
                        for h in range(4):
                            stt(mixv[:, 4 + h, :], b[:, h * 128:(h + 1) * 128], bsT[:, h:h + 1], u_[:, h, :],
                                ALU.add, ALU.mult)

                def F5():
                    sqbf = FT[:, 0:512].bitcast(BF16)
                    sqb = v2(sqbf[:, 0:8 * n], 8)
                    tt(sqb, QC[:, 0:8, :], QC[:, 0:8, :], ALU.mult)
                    for half in range(2):
                        b = nb()
                        mm(b[:, 0:4 * n], onesb_t[:, :], sqb[:, 4 * half:4 * half + 4, :])
                        rsv = v2(FT[:, 512:512 + 4 * n], 4)
                        act(rsv, v2(b[:, 0:4 * n], 4), AF.Ln, bias=EPS)
                        act(rsv, rsv, AF.Exp, scale=-0.5, bias=(-0.5 * math.log(128.0) if half == 0 else 0.0))
                        tt(qkn[:, 4 * half:4 * half + 4, 0:n], QC[:, 4 * half:4 * half + 4, :], rsv, ALU.mult)

                def F6():
                    b = nb()
                    for h in range(4):
                        tr(b[0:n, h * 128:(h + 1) * 128], QC[:, 8 + h, :], identF)
                    cp(vtok[0:n], v2(b[0:n, :], 4), eng="act")
                    b = nb()
                    for h in range(4):
                        tr(b[0:n, h * 128:(h + 1) * 128], qkn[:, 4 + h, 0:n], identF)
                    cp(ktok[0:n], v2(b[0:n, :], 4), eng="act")

                early = [F1, lambda: Fq(0), lambda: Fq(1), lambda: Fq(2), F7, F8]
                late = [F5, F6]
                return early, late

            def back(t):
                par = t % 2
                sz = szb[par]
                mix = mixb[par]
                mixv = v2(mix, 8)
                beta, g_ = slots(par, 128)
                xin = X[:, t, :]
                gc = sm(40, 56)
                ex = sm(56, 72)
                gsel = sm(72, 80)
                gcum = SM[:, 40:44]
                eg = SM[:, 56:60]
                ekd = SM[:, 60:64]
                egl = SM[:, 64:72]
                N_ = NM[:, 0]
                M_ = NM[:, 1]

                def B1():
                    bg = nb()
                    mm(bg[:, 0:4], TRIinc, g_)
                    mm(bg[:, 4:8], TRIsu, g_)
                    tt(v2(gsel, 4), bc(g_.unsqueeze(2), [128, 4, 2]), bc(cm.unsqueeze(1), [128, 4, 2]), ALU.mult)
                    mm(bg[:, 8:16], ones, gsel)
                    cp(gc, bg[:, 0:16])
                    act(ex, gc, AF.Exp)
                    gBk = Tm[1]
                    cp(gBk, bc(g_.unsqueeze(2), [128, 4, 128]))
                    bG = nb()
                    for h in range(4):
                        mm(bG[:, h * 128:(h + 1) * 128], gBk[:, h, :], TRIinc)
                    xd = Tm[2]
                    for h in range(4):
                        ts(xd[:, h, :], bG[:, h * 128:(h + 1) * 128], gcum[:, h:h + 1], 0.0, ALU.subtract, ALU.max)
                    act(xd, xd, AF.Exp, scale=-1.0)
                    tt(Tm[1], xd, bc(maskL.unsqueeze(1), [128, 4, 128]), ALU.mult)
                    tt(o_, xd, bc(maskQ.unsqueeze(1), [128, 4, 128]), ALU.mult)

                def B2():
                    decL, decQ = Tm[1], o_
                    bK = nb()
                    bQ = nb()
                    for h in range(4):
                        mm(bK[:, h * 128:(h + 1) * 128], qkn[:, 4 + h, :], qkn[:, 4 + h, :])
                    for h in range(4):
                        mm(bQ[:, h * 128:(h + 1) * 128], qkn[:, h, :], qkn[:, 4 + h, :])
                    for h in range(4):
                        stt(N_[:, h, :], bK[:, h * 128:(h + 1) * 128], beta[:, h:h + 1], decL[:, h, :],
                            ALU.mult, ALU.mult)
                    QKm = Tm[2]
                    tt(QKm, v2(bQ, 4), decQ, ALU.mult)
                    b1 = nb()
                    b2 = nb()
                    for h in range(4):
                        tr(b1[:, h * 128:(h + 1) * 128], N_[:, h, :], identF)
                    for h in range(4):
                        tr(b2[:, h * 128:(h + 1) * 128], QKm[:, h, :], identF)
                    cp(M_, v2(b1, 4), eng="act")
                    cp(QKmT, v2(b2, 4), eng="act")
                    tt(Pm, bc(identF.unsqueeze(1), [128, 4, 128]), M_, ALU.subtract)

                def Bs(s):
                    bN = nb()
                    for h in range(4):
                        mm(bN[:, h * 128:(h + 1) * 128], M_[:, h, :], N_[:, h, :])
                    if s < 4:
                        bM = nb()
                        for h in range(4):
                            mm(bM[:, h * 128:(h + 1) * 128], N_[:, h, :], M_[:, h, :])
                    cp(N_, v2(bN, 4), eng="act")
                    if s < 4:
                        cp(M_, v2(bM, 4))
                    bP = nb()
                    for h in range(4):
                        mm(bP[:, h * 128:(h + 1) * 128], N_[:, h, :], Pm[:, h, :])
                    tt(Pm, v2(bP, 4), Pm, ALU.add)
                    if s == 4:
                        tt(ktok, ktok, bc(ekd.unsqueeze(2), [128, 4, 128]), ALU.mult)

                def Bc(c):
                    TT = Pm
                    kdec = ktok
                    r2, vnew, otmp = Tm[0], Tm[1], Tm[2]
                    rs_ = slice(64 * c, 64 * c + 64)
                    bKS = nb()
                    bQS = nb()
                    for h in range(4):
                        mm(bKS[:, h * 128:(h + 1) * 128], qkn[:, 4 + h, :], S_[:, h, :])
                    for h in range(4):
                        mm(bQS[:, h * 128:(h + 1) * 128], qkn[:, h, :], S_[:, h, :])
                    for h in range(4):
                        stt(r2[rs_, h, :], bKS[rs_, h * 128:(h + 1) * 128], eg[rs_, h:h + 1], vtok[rs_, h, :],
                            ALU.mult, ALU.subtract)
                    for h in range(4):
                        ts(r2[rs_, h, :], r2[rs_, h, :], beta[rs_, h:h + 1], -1.0, ALU.mult, ALU.mult)
                    bV = nb()
                    for h in range(4):
                        mm(bV[:, h * 128:(h + 1) * 128], TT[rs_, h, :], r2[rs_, h, :])
                    cp(vnew[rs_], v2(bV, 4)[rs_], eng="act")
                    bO = nb()
                    for h in range(4):
                        mm(bO[:, h * 128:(h + 1) * 128], QKmT[rs_, h, :], vnew[rs_, h, :])
                    cp(otmp[rs_], v2(bO, 4)[rs_], eng="act")
                    for h in range(4):
                        stt(o_[rs_, h, :], bQS[rs_, h * 128:(h + 1) * 128], eg[rs_, h:h + 1], otmp[rs_, h, :],
                            ALU.mult, ALU.add)
                    bS = nb()
                    for h in range(4):
                        mm(bS[:, h * 128:(h + 1) * 128], kdec[rs_, h, :], vnew[rs_, h, :])
                    for h in range(4):
                        stt(S_[:, h, :], S_[:, h, :], egl[:, 2 * h + c:2 * h + c + 1], bS[:, h * 128:(h + 1) * 128],
                            ALU.mult, ALU.add)

                def B10():
                    gated_norm(128, mixv, sz)

                def B11():
                    out_proj(128, mix, xin, None)

                a = [B1, B2] + [(lambda s=s: Bs(s)) for s in range(5)] + [lambda: Bc(0), lambda: Bc(1)]
                return a, [B10, B11]

            def gated_norm(n, mixv, sz):
                sqo = Tm[0]
                tt(sqo[0:n], o_[0:n], o_[0:n], ALU.mult)
                sso = sm(88, 92, n)
                P.op("dve", "reduce_sum", out=sso, in_=sqo[0:n], axis=AX.X)
                ro = sm(92, 96, n)
                rsqrt_mean(ro, sso, 1.0 / 128, n)
                on = Tm[0]
                tt(on[0:n], o_[0:n], bc(ro.unsqueeze(2), [n, 4, 128]), ALU.mult)
                tt(on[0:n], on[0:n], bc(gdnB[0:n, :].unsqueeze(1), [n, 4, 128]), ALU.mult)
                tt(mixv[0:n, 0:4, :], on[0:n], v2(sz[0:n, :], 4), ALU.mult)

            def out_proj(n, mix, xin, gsrc):
                b = nb().bitcast(BF16)
                for k in range(8):
                    tr(b[:, k * n:(k + 1) * n], mix[0:n, k * 128:(k + 1) * 128], identb[0:n, 0:n])
                cp(mixT[:, :, 0:n], v2(b[:, 0:8 * n], 8), eng="act")
                for half in range(2):
                    b = nb()
                    for k in range(8):
                        mm(b[0:n, :], mixT[:, k, 0:n], w_out_sb[:, k, half * 512:(half + 1) * 512],
                           start=(k == 0), stop=(k == 7))
                    hs = slice(half * 512, (half + 1) * 512)
                    if gsrc is None:
                        tt(xin[:, hs], xin[:, hs], b[0:n, :], ALU.add)
                    else:
                        tt(tmpA[0:n, hs], b[0:n, :], gsrc[0:n, hs], ALU.mult)
                        tt(xin[:, hs], xin[:, hs], tmpA[0:n, hs], ALU.add)

            def delta_sample(beta, g_):
                egs = sm(56, 60, 16)
                act(egs, g_, AF.Exp)
                b = nb()
                for h in range(4):
                    tr(b[0:16, h * 128:(h + 1) * 128], qkn[:, h, 0:16], identF)
                qtok = Tm[0]
                cp(qtok[0:16], v2(b[0:16, :], 4), eng="act")
                tt(qtok[0:16], qtok[0:16], ktok[0:16], ALU.mult)
                qk = sm(108, 112, 16)
                P.op("dve", "reduce_sum", out=qk, in_=qtok[0:16], axis=AX.X)
                gdiag = Tm[1].rearrange("p a b -> p (a b)")[0:16, 0:64]
                tt(v2(gdiag, 16), bc(g_.unsqueeze(1), [16, 16, 4]),
                   bc(identF[0:16, 0:16].unsqueeze(2), [16, 16, 4]), ALU.mult)
                b = nb()
                mm(b[:, 0:64], ones[0:16, :], gdiag)
                EGb = sm(128, 192)
                act(EGb, b[:, 0:64], AF.Exp)
                r2, vnew, otmp = Tm[2], S_, u_
                kqm = Tm[1].rearrange("p a b -> p (a b)")
                for h in range(4):
                    SA = v2(FA[:, 0:2048], 16) if h % 2 == 0 else v2(NPQ, 16)
                    P.dma("sp", out=SA, in_=sd[l][:, h].rearrange("b k v -> k b v"))
                    kTm = v2(kqm[:, 0:256], 16)
                    qTm = v2(kqm[:, 256:512], 16)
                    tt(kTm, bc(qkn[:, 4 + h, 0:16].unsqueeze(1), [128, 16, 16]), i16rep, ALU.mult)
                    tt(qTm, bc(qkn[:, h, 0:16].unsqueeze(1), [128, 16, 16]), i16rep, ALU.mult)
                    bKS = nb()
                    bQS = nb()
                    for bb_ in range(16):
                        mm(bKS[0:16, 0:128], kTm[:, bb_, :], SA[:, bb_, :], start=(bb_ == 0), stop=(bb_ == 15))
                    for bb_ in range(16):
                        mm(bQS[0:16, 0:128], qTm[:, bb_, :], SA[:, bb_, :], start=(bb_ == 0), stop=(bb_ == 15))
                    stt(r2[0:16, h, :], bKS[0:16, 0:128], egs[:, h:h + 1], vtok[0:16, h, :], ALU.mult, ALU.subtract)
                    ts(vnew[0:16, h, :], r2[0:16, h, :], beta[:, h:h + 1], -1.0, ALU.mult, ALU.mult)
                    ts(otmp[0:16, h, :], vnew[0:16, h, :], qk[:, h:h + 1], None, ALU.mult)
                    stt(o_[0:16, h, :], bQS[0:16, 0:128], egs[:, h:h + 1], otmp[0:16, h, :], ALU.mult, ALU.add)
                    vmr = Tm[0].rearrange("p a b -> p (a b)")
                    for q4 in range(4):
                        bS = nb()
                        for j in range(4):
                            bb_ = q4 * 4 + j
                            vm = vmr[0:16, j * 128:(j + 1) * 128]
                            ts(vm, vnew[0:16, h, :], identF[0:16, bb_:bb_ + 1], None, ALU.mult)
                            mm(bS[:, j * 128:(j + 1) * 128], ktok[0:16, h, :], vm)
                        for j in range(4):
                            bb_ = q4 * 4 + j
                            stt(SA[:, bb_, :], SA[:, bb_, :], SM[:, 128 + bb_ * 4 + h:128 + bb_ * 4 + h + 1],
                                bS[:, j * 128:(j + 1) * 128], ALU.mult, ALU.add)
                    P.dma("sp", out=ds[l][:, h].rearrange("b k v -> k b v"), in_=SA)

            ckpt("mix%d_params" % l)
            e_, l_ = front(0, True)
            for f in e_ + l_:
                f()
            beta_s, g_s = slots(0, 16)
            delta_sample(beta_s, g_s)
            gated_norm(16, v2(mixb[0], 8), szb[0])
            out_proj(16, mixb[0], Xs[0:16, :], gS)
            ckpt("mix%d_s" % l)
            mset(S_, 0.0)
            make_g(16, FA[:, 1024:2048], gBt)
            tt(w_out_sb, w_out_sb, bc(gBt.unsqueeze(1), [128, 8, 1024]), ALU.mult)
            e_, l_ = front(0, False)
            for f in e_ + l_:
                f()
            poolF = {"l": [0, 1, 2, 3], "i": 0}
            poolB = {"l": [4, 5, 6, 7], "i": 0}
            for t in range(NT):
                ba_, bb_l = back(t)
                if t + 1 < NT:
                    fe, fl = front(t + 1, False)
                else:
                    fe, fl = [], []
                merge_replay(record(fe, poolF), record(ba_, poolB))
                merge_replay(record(fl, poolF), record(bb_l, poolB))
                ckpt("mix%d_t%d" % (l, t))
            P.dma("sp", out=dp[l].rearrange("h k v -> k h v"), in_=S_)
            ckpt("mix%d_dp" % l)

        def ffn_phase(l):
            CV.reset()
            NTK = T + NS
            h2T = v2(CV.bf16(8 * NTK), 8)
            wu = [v2(CV.bf16(8 * 512), 8) for _ in range(2)]
            wd = [v2(CV.bf16(2 * 1024), 2) for _ in range(2)]
            xn = CV.bf16(1024)
            tmpA = CV.f32(1024)
            upx = [[[CV.f32(516) for _ in range(2)] for _ in range(2)] for _ in range(2)]
            yb = [[CV.f32(512) for _ in range(2)] for _ in range(2)]
            actT = [[CV.bf16(512) for _ in range(2)] for _ in range(2)]
            tmpD = [CV.f32(1024) for _ in range(2)]
            fcar = v2(CV.f32(88), 2)
            fraw = CV.f32(128)
            sfg = CV.f32(1024)
            fsn = CV.f32(512)
            ptmp = CV.f32(512)
            upxs = [[CV.f32(48) for _ in range(2)] for _ in range(2)]

            gB = CV.f32(1024)
            wm2 = [v2(CV.bf16(8 * 256), 8) for _ in range(2)]
            make_g(40, tmpA, gB)
            b = nb()
            for j in range(3):
                P.dma("sp", out=fraw[0:44, :], in_=conv_ffn_w[l][j].rearrange("(c p) -> c p", p=128))
                mm(b[:, j * 44:(j + 1) * 44], fraw[0:44, :], identF[0:44, 0:44])
            cp(cfwT_t[:, :], b[:, 0:132])
            P.dma("sp", out=fraw[0:44, :], in_=conv_ffn_b[l].rearrange("(c p) -> c p", p=128))
            b = nb()
            mm(b[:, 0:44], fraw[0:44, :], identF[0:44, 0:44])
            cp(cfbT[:, :], b[:, 0:44])
            P.dma("sp", out=fs[l][:, 0, :], in_=sf[l][:, 1, :])

            norm_transpose(Xs[0:16, :], 16, xn, A2T, 24, h2T, tmpA, T)
            for t in range(4):
                norm_transpose(X[:, t, :], 128, xn, A2T, 24, h2T, tmpA, t * 128)
            ckpt("ffn%d_h2T" % l)

            wus = w_up[l].rearrange("(k p) n -> p k n", p=128)
            wds = w_down[l].rearrange("(c p) n -> p c n", p=128)

            def load_w(g):
                s = g % 2
                P.dma("pool", out=wu[s][:, :, 0:256], in_=wus[:, :, g * 256:(g + 1) * 256])
                P.dma("pool", out=wu[s][:, :, 256:512], in_=wus[:, :, DFF + g * 256:DFF + (g + 1) * 256])
                P.dma("pool", out=wd[s], in_=wds[:, 2 * g:2 * g + 2, :])

            load_w(0)
            load_w(1)
            poolU = {"l": [0, 1, 2, 3], "i": 0}
            poolD = {"l": [4, 5, 6, 7], "i": 0}

            def nbp(pl):
                b_ = ps[pl["l"][pl["i"]]]
                pl["i"] = (pl["i"] + 1) % len(pl["l"])
                return b_[:, :]

            units = [(g, blk) for g in range(NG) for blk in (4, 0, 1, 2, 3)]

            def up(ui):
                g, blk = units[ui]
                s = g % 2
                sample = (blk == 4)
                N = 16 if sample else 512
                col = T if sample else blk * 512
                sfv = v3(sfg[0:16, :], 2, 2)
                if sample:
                    ckpt("ffn%d_g%d" % (l, g))
                    P.dma("sp", out=sfv[:, :, 0, :], in_=sf[l][:, :, g * 256:(g + 1) * 256])
                    P.dma("sp", out=sfv[:, :, 1, :], in_=sf[l][:, :, DFF + g * 256:DFF + (g + 1) * 256])
                for pr in range(2):
                    cidx = (2 * g + pr, 22 + 2 * g + pr)
                    bks = (nbp(poolU), nbp(poolU))
                    for gv in range(2):
                        for k in range(8):
                            mm(bks[gv][:, 0:N], wu[s][:, k, gv * 256 + pr * 128:gv * 256 + (pr + 1) * 128],
                               h2T[:, k, col:col + N], start=(k == 0), stop=(k == 7))
                    for gv in range(2):
                        c = cidx[gv]
                        y = yb[gv][pr][:, 0:N]
                        if sample:
                            ux = upxs[gv][pr]
                            b2 = nbp(poolU)
                            for j in range(2):
                                mm(b2[:, j * 16:(j + 1) * 16], sfv[:, j, gv, pr * 128:(pr + 1) * 128],
                                   identF[0:16, 0:16])
                            cp(ux[:, 0:32], b2[:, 0:32], eng="act")
                            cp(ux[:, 32:48], bks[gv][:, 0:16], eng="act")
                            taps = [ux[:, 0:16], ux[:, 16:32], ux[:, 32:48]]
                            ts(y, taps[2], cfwT[:, 2, c:c + 1], cfbT[:, c:c + 1], ALU.mult, ALU.add)
                        else:
                            ux = upx[gv][pr][blk % 2]
                            cp(ux[:, 2:2 + N], bks[gv][:, 0:N], eng="act")
                            act(y, bks[gv][:, 0:N], AF.Identity, scale=cfwT[:, 2, c:c + 1], bias=cfbT[:, c:c + 1])
                            if blk == 0:
                                mset(ux[:, 0:2], 0.0)
                            if blk < 3:
                                cp(upx[gv][pr][(blk + 1) % 2][:, 0:2], ux[:, N:N + 2])
                            else:
                                cp(fcar[:, :, c], ux[:, N:N + 2])
                            taps = [ux[:, 0:N], ux[:, 1:1 + N], ux[:, 2:2 + N]]
                        stt(y, taps[1], cfwT[:, 1, c:c + 1], y, ALU.mult, ALU.add)
                        stt(y, taps[0], cfwT[:, 0, c:c + 1], y, ALU.mult, ALU.add)
                    yg = yb[0][pr][:, 0:N]
                    act(yg, yg, AF.Silu)
                    tt(actT[pr][ui % 2][:, 0:N], yg, yb[1][pr][:, 0:N], ALU.mult)
                if sample:
                    b3 = nbp(poolU)
                    for k in range(8):
                        mm(b3[0:16, :], h2T[:, k, T:T + 16], wu[s][:, k, :], start=(k == 0), stop=(k == 7))
                    cp(fsn[0:16, :], b3[0:16, :])
                    P.dma("sp", out=fs[l][:, 1, g * 256:(g + 1) * 256], in_=fsn[0:16, 0:256])
                    P.dma("sp", out=fs[l][:, 1, DFF + g * 256:DFF + (g + 1) * 256], in_=fsn[0:16, 256:512])

            def down(ui):
                g, blk = units[ui]
                s = g % 2
                sample = (blk == 4)
                ntile = 1 if sample else 4
                n = 16 if sample else 128
                for tt_ in range(ntile):
                    xin = Xs[0:16, :] if sample else X[:, blk * 4 + tt_, :]
                    td = tmpD[tt_ % 2]
                    for half in range(2):
                        bD = nbp(poolD)
                        for pr in range(2):
                            mm(bD[0:n, :], actT[pr][ui % 2][:, tt_ * n:(tt_ + 1) * n],
                               wd[s][:, pr, half * 512:(half + 1) * 512], start=(pr == 0), stop=(pr == 1))
                        hs = slice(half * 512, (half + 1) * 512)
                        if sample:
                            tt(td[0:n, hs], bD[0:n, :], gS[0:n, hs], ALU.mult)
                            tt(xin[:, hs], xin[:, hs], td[0:n, hs], ALU.add)
                        else:
                            tt(xin[:, hs], xin[:, hs], bD[0:n, :], ALU.add)
                if sample:
                    tt(wd[s], wd[s], bc(gB.unsqueeze(1), [128, 2, 1024]), ALU.mult)
                if blk == 3 and g + 2 < NG:
                    load_w(g + 2)

            poolA = {"l": [7], "i": 0}
            poolD["l"] = [4, 5, 6]
            recA = record([(lambda t=t: norm_transpose(X[:, t, :], 128, xn, A2T, 24, h2T, tmpA, t * 128))
                           for t in range(4, NT)], poolA)
            recB = record([lambda: up(0), lambda: up(1), lambda: down(0), lambda: up(2), lambda: down(1)], None)
            merge_replay(recA, recB)
            poolD["l"] = [4, 5, 6, 7]
            poolD["i"] = 0
            def run_units(a_, b_):
                for ui in range(a_, b_):
                    if ui + 1 < len(units):
                        up(ui + 1)
                    down(ui)

            run_units(2, 5)
            if l + 1 < L:
                poolU["l"] = [0, 1, 2]
                poolU["i"] = 0
                recM = record([lambda: mod_ops(l + 1, wm2, fraw, 256)], {"l": [3], "i": 0})
                recB = record([lambda: run_units(5, 25)], None)
                merge_replay(recM, recB)
                poolU["l"] = [0, 1, 2, 3]
                poolU["i"] = 0
                run_units(25, len(units))
            else:
                run_units(5, len(units))
            b = nb()
            mm(b[0:88, 0:128], fcar.rearrange("p a b -> p (a b)"), identF)
            cp(tmpA[0:88, 0:128], b[0:88, 0:128])
            P.dma("sp", out=fp[l].rearrange("j (c p) -> (j c) p", p=128), in_=tmpA[0:88, 0:128])

        stopped = False
        try:
            ckpt("setup")
            mod_phase(0)
            ckpt("mod0")
            for l in range(L):
                mixer_phase2(l)
                ckpt("mix%d" % l)
                ffn_phase(l)
                ckpt("ffn%d" % l)
        except StopBuild:
            stopped = True
            PT = {"modT": modT_t[:, :], "X0": X[:, 0, :], "X1": X[:, 1, :], "X15": X[:, 15, :], "Xs": Xs[0:16, :], "SM": SM[:, :],
                  "gS": gS[0:16, :], "A1T": A1T_t[:, :], "A2T": A2T_t[:, :], "cwT": cwT[:, :], "bsT": bsT[:, :],
                  "nT1": nT1[:, :], "bmT": bmT[:, :], "cfwT": cfwT_t[:, :], "cfbT": cfbT[:, :], "cT": cT_t[:, :]}
            for nm in dump_names:
                if nm in PT:
                    dump(nm, PT[nm])

        CV.reset()
        fnB = CV.f32(1024)
        junk = CV.f32(1024)
        yo = [CV.f32(1024) for _ in range(2)]
        P.dma("sp", out=fnB, in_=final_norm.partition_broadcast(128))
        for t in range(NT + 1):
            sample = (t == NT)
            n = 16 if sample else 128
            xin = Xs[0:16, :] if sample else X[:, t, :]
            ss = SM[0:n, (2 * (t % 2)):(2 * (t % 2)) + 1]
            rstd = SM[0:n, (2 * (t % 2)) + 1:(2 * (t % 2)) + 2]
            mset(ss, 0.0)
            act(junk[0:n, :], xin, AF.Square, accum_out=ss)
            rsqrt_mean(rstd, ss, 1.0 / D, n)
            y_ = yo[t % 2]
            stt(y_[0:n, :], xin, rstd, fnB[0:n, :], ALU.mult, ALU.mult)
            if sample:
                P.dma("sp", out=ys[:, :], in_=y_[0:16, :])
            else:
                P.dma("sp", out=yp[t * 128:(t + 1) * 128, :], in_=y_[:, :])

        names = P.sem_names()
        semh = {nm: es.enter_context(nc.semaphore(nm)) for nm in names}
        block = es.enter_context(nc.Block())

        def emit(engname, e):
            for (wl, meth, kw, dsem, where) in P.ops[engname]:
                for (k, v) in wl:
                    e.wait_ge(semh[k], v)
                try:
                    ins = getattr(e, meth)(**kw)
                except Exception:
                    print("EMIT FAILED", engname, meth, where, {k_: (v_.shape if _is_ap(v_) else v_) for k_, v_ in kw.items()})
                    raise
                if dsem is None:
                    ins.then_inc(semh[engname], 1)
                else:
                    ins.then_inc(semh[dsem], 16)

        @block.tensor
        def _(e):
            emit("pe", e)

        @block.scalar
        def _(e):
            emit("act", e)

        @block.vector
        def _(e):
            emit("dve", e)

        @block.gpsimd
        def _(e):
            emit("pool", e)

        @block.sync
        def _(e):
            emit("sp", e)
            for nm, v in P.dval.items():
                e.wait_ge(semh[nm], v)
            for en in ("pe", "act", "dve"):
                if P.cnt[en]:
                    e.wait_ge(semh[en], P.cnt[en])
    return nc, P


_CACHE = {}


def kernel(**inputs):
    f = lambda a: np.ascontiguousarray(np.asarray(a, dtype=np.float32))
    inp = {k: f(v) for k, v in inputs.items()}
    if "nc" not in _CACHE:
        _CACHE["nc"] = build_program()[0]
    nc = _CACHE["nc"]
    consts = make_consts()
    shared = {k: inp[k] for k in ("w_mod", "b_mod", "norm1", "w_in", "conv_qkv", "a_log", "dt_bias", "gdn_norm",
                                  "sgu_norm", "w_sgu", "b_sgu", "w_out", "norm2", "w_up", "conv_ffn_w",
                                  "conv_ffn_b", "w_down", "final_norm")}
    in_maps = []
    for i in range(NCORES):
        sl = slice(NS * i, NS * (i + 1))
        m = dict(shared)
        m["consts"] = consts
        m["xp"] = f(inp["x_prompt"][i])
        m["xs"] = f(inp["x_sample"][sl, 0, :])
        m["sd"] = f(inp["state_delta"][:, sl])
        m["sq"] = f(inp["state_qkv_conv"][:, sl])
        m["sf"] = f(inp["state_ffn_conv"][:, sl])
        m["call"] = f(np.concatenate([inp["c_prompt"][i:i + 1], inp["c_sample"][sl]], axis=0))
        in_maps.append(m)
    ncr = _CACHE.get("ncores_dbg", NCORES)
    res = run_bass_kernel_spmd(nc, in_maps[:ncr], core_ids=list(range(ncr)))
    R = list(res.results)
    while len(R) < NCORES:
        R.append({k: np.zeros_like(v) for k, v in R[0].items()})
    _CACHE["dbg"] = np.stack([R[i]["dbg"] for i in range(NCORES)]) if "dbg" in R[0] else None
    y_prompt = np.stack([R[i]["yp"] for i in range(NCORES)], axis=0)
    y_sample = np.concatenate([R[i]["ys"] for i in range(NCORES)], axis=0)[:, None, :]
    delta_p = np.stack([R[i]["dp"] for i in range(NCORES)], axis=1)
    delta_s = np.concatenate([R[i]["ds"] for i in range(NCORES)], axis=1)
    qkv_p = np.stack([R[i]["qp"] for i in range(NCORES)], axis=1)
    qkv_s = np.concatenate([R[i]["qs"] for i in range(NCORES)], axis=1)
    ffn_p = np.stack([R[i]["fp"] for i in range(NCORES)], axis=1)
    ffn_s = np.concatenate([R[i]["fs"] for i in range(NCORES)], axis=1)
    sgu_v = np.concatenate([R[i]["sv"] for i in range(NCORES)], axis=1).reshape(L, NS * NCORES, 1, H, 128)
    outs = (y_prompt, y_sample, delta_p, delta_s, qkv_p, qkv_s, ffn_p, ffn_s, sgu_v)
    return tuple(np.ascontiguousarray(o, dtype=np.float32) for o in outs)
```
